# Optimizing a Trainium2 kernel written in Bass

```python
import jax, jax.numpy as jnp
from jax import lax
import numpy as np

D_MODEL = 1024
BATCH = 4
SEQ = 8192
DEPTH = 2

CTX_LEN = 256
GRID_W = 64

CONV_DIM = 512
CONV_WIDTH = 31
LRU_DIM = 512
LRU_BLOCKS = 8
LRU_CONV = 4
LRU_C = 8.0
GDN_HEADS = 4
GDN_HEAD_DIM = 128
GDN_DIM = GDN_HEADS * GDN_HEAD_DIM
GDN_CONV = 4
GDN_CHUNK = 64
ATT_HEADS = 8
ATT_KV_HEADS = 2
ATT_HEAD_DIM = 64
ATT_DIM = ATT_HEADS * ATT_HEAD_DIM
ATT_KV_DIM = ATT_KV_HEADS * ATT_HEAD_DIM
ATT_WINDOW = 128
ATT_BLOCK = 128
ROPE_BASE = 10000.0

N_BRANCH = 4
D_FF = 2816
LN_EPS = 1e-5
NORM_EPS = 1e-6
DEEPNORM_ALPHA = (2 * DEPTH) ** 0.25
DEEPNORM_BETA = (8 * DEPTH) ** -0.25

IN_WIDTHS = (CONV_DIM, CONV_DIM, LRU_DIM, LRU_DIM, 3 * GDN_DIM, GDN_DIM, 2 * GDN_HEADS, 2 * GDN_HEADS, ATT_DIM, ATT_KV_DIM, ATT_KV_DIM, N_BRANCH * D_MODEL)
IN_TOTAL = sum(IN_WIDTHS)

kernel_name = 'hybrid_conv_lru_deltanet_swa_flow_block'


def layer_norm(x):
    xf = x.astype(jnp.float32)
    mu = jnp.mean(xf, axis=-1, keepdims=True)
    var = jnp.mean(jnp.square(xf - mu), axis=-1, keepdims=True)
    return ((xf - mu) * lax.rsqrt(var + LN_EPS)).astype(x.dtype)


def modulate(x, shift, scale):
    return layer_norm(x) * (1.0 + scale) + shift


def swiglu(h, w_gate_up, w_down):
    g, u = jnp.split(h @ w_gate_up, 2, axis=-1)
    return (jax.nn.silu(g) * u) @ w_down


def depthwise_conv(x, w, pad_left, pad_right):
    n_ch = w.shape[1]
    return lax.conv_general_dilated(x, w[:, None, :], window_strides=(1,), padding=[(pad_left, pad_right)], dimension_numbers=('NWC', 'WIO', 'NWC'), feature_group_count=n_ch)


def _flip(t):
    return t[:, ::-1]


def _ident(t):
    return t


def split_in(z):
    idx = np.cumsum(IN_WIDTHS)[:-1].tolist()
    return jnp.split(z, idx, axis=-1)


def axial_rope(n_rows):
    row = jnp.repeat(jnp.arange(n_rows, dtype=jnp.float32), GRID_W)
    col = jnp.tile(jnp.arange(GRID_W, dtype=jnp.float32), n_rows)
    n_freq = ATT_HEAD_DIM // 4
    inv = ROPE_BASE ** (-jnp.arange(n_freq, dtype=jnp.float32) / n_freq)
    ang = jnp.concatenate([row[:, None] * inv, col[:, None] * inv], axis=-1)
    return jnp.cos(ang), jnp.sin(ang)


def apply_rope(t, cos, sin):
    half = t.shape[-1] // 2
    t1, t2 = t[..., :half], t[..., half:]
    c = cos[None, :, None, :].astype(t.dtype)
    s = sin[None, :, None, :].astype(t.dtype)
    return jnp.concatenate([t1 * c - t2 * s, t1 * s + t2 * c], axis=-1)


def conformer_conv(val, gate, p):
    u = val * jax.nn.sigmoid(gate)
    half = (CONV_WIDTH - 1) // 2
    u = depthwise_conv(u, p['conv_w'], half, half) + p['conv_b']
    u = layer_norm(u) * p['conv_norm_g'] + p['conv_norm_b']
    return jax.nn.silu(u)


def _linear_combine(left, right):
    a1, b1 = left
    a2, b2 = right
    return a1 * a2, a2 * b1 + b2


def rglru_direction(xs, h0, conv_w, conv_b, w_r, b_r, w_i, b_i, lam):
    xc = depthwise_conv(xs, conv_w, LRU_CONV - 1, 0) + conv_b
    xf = xc.astype(jnp.float32)
    xb = xf.reshape(xf.shape[:-1] + (LRU_BLOCKS, LRU_DIM // LRU_BLOCKS))
    r = jax.nn.sigmoid(jnp.einsum('btnd,nde->btne', xb, w_r).reshape(xf.shape) + b_r)
    i = jax.nn.sigmoid(jnp.einsum('btnd,nde->btne', xb, w_i).reshape(xf.shape) + b_i)
    log_a = -LRU_C * r * jax.nn.softplus(-lam)
    a = jnp.exp(log_a)
    b = jnp.sqrt(-jnp.expm1(2.0 * log_a)) * (i * xf)
    b = b.at[:, 0].add(a[:, 0] * h0)
    _, h = lax.associative_scan(_linear_combine, (a, b), axis=1)
    return h, h[:, -1]


def rglru_branch(x_lat, x_ctx, g_lat, g_ctx, p, ctx_out):
    h0 = jnp.zeros((x_lat.shape[0], LRU_DIM), jnp.float32)
    h_lat, h_ctx = [], []
    for d in range(2):
        fl = _flip if d else _ident
        prm = (p['lru_conv_w'][d], p['lru_conv_b'][d], p['lru_w_r'][d], p['lru_b_r'][d], p['lru_w_i'][d], p['lru_b_i'][d], p['lru_lam'][d])
        hc, s_ctx = rglru_direction(fl(x_ctx), h0, *prm)
        hl, _ = rglru_direction(fl(x_lat), s_ctx, *prm)
        h_lat.append(fl(hl))
        h_ctx.append(fl(hc))
    y = (h_lat[0] + h_lat[1]).astype(x_lat.dtype) * jax.nn.gelu(g_lat)
    y_c = (h_ctx[0] + h_ctx[1]).astype(x_ctx.dtype) * jax.nn.gelu(g_ctx) if ctx_out else None
    return y, y_c


def gdn_chunked(q, k, v, beta, g, s0):
    B_, T, H, dk = q.shape
    C = GDN_CHUNK
    N = T // C

    def chunks(t):
        return jnp.moveaxis(t.reshape((B_, N, C) + t.shape[2:]), 3, 1)

    q, k, v, beta, g = (chunks(t) for t in (q, k, v, beta, g))
    q = q * (dk ** -0.5)
    gc = jnp.cumsum(g, axis=-1)
    tril = jnp.tril(jnp.ones((C, C), dtype=bool))
    eye = jnp.eye(C, dtype=bool)
    decay = jnp.exp(jnp.where(tril, gc[..., :, None] - gc[..., None, :], -jnp.inf))
    kb = k * beta[..., None]
    m = jnp.where(tril & ~eye, jnp.einsum('bhncd,bhnsd->bhncs', kb, k) * decay, 0.0)
    a_mat = m + eye.astype(m.dtype)
    rhs = jnp.concatenate([v * beta[..., None], kb * jnp.exp(gc)[..., None]], axis=-1)
    sol = lax.linalg.triangular_solve(a_mat, rhs, left_side=True, lower=True, unit_diagonal=True)
    dv = v.shape[-1]
    u, w = sol[..., :dv], sol[..., dv:]
    attn = jnp.einsum('bhncd,bhnsd->bhncs', q, k) * decay
    q_dec = q * jnp.exp(gc)[..., None]
    k_state = k * jnp.exp(gc[..., -1:] - gc)[..., None]
    chunk_decay = jnp.exp(gc[..., -1])

    def step(s, inp):
        u_n, w_n, attn_n, qd_n, ks_n, cd_n = inp
        v_new = u_n - jnp.einsum('bhcd,bhde->bhce', w_n, s)
        o = jnp.einsum('bhcd,bhde->bhce', qd_n, s) + jnp.einsum('bhcs,bhse->bhce', attn_n, v_new)
        s = s * cd_n[..., None, None] + jnp.einsum('bhcd,bhce->bhde', ks_n, v_new)
        return s, o

    xs = tuple(jnp.moveaxis(t, 2, 0) for t in (u, w, attn, q_dec, k_state, chunk_decay))
    s_final, o = lax.scan(step, s0, xs)
    o = jnp.transpose(o, (1, 0, 3, 2, 4)).reshape(B_, T, H, dv)
    return o, s_final


def _l2norm(t):
    return t * lax.rsqrt(jnp.sum(jnp.square(t), axis=-1, keepdims=True) + NORM_EPS)


def gdn_direction(qkv, a, b, s0, conv_w, a_log, dt_bias):
    B_, T, _ = qkv.shape
    u = jax.nn.silu(depthwise_conv(qkv, conv_w, GDN_CONV - 1, 0)).astype(jnp.float32)
    q, k, v = (t.reshape(B_, T, GDN_HEADS, GDN_HEAD_DIM) for t in jnp.split(u, 3, axis=-1))
    beta = jax.nn.sigmoid(b.astype(jnp.float32))
    g = -jnp.exp(a_log.astype(jnp.float32)) * jax.nn.softplus(a.astype(jnp.float32) + dt_bias)
    return gdn_chunked(_l2norm(q), _l2norm(k), v, beta, g, s0)


def _gdn_out(o, z, g_norm):
    zz = z.reshape(o.shape).astype(jnp.float32)
    y = o * lax.rsqrt(jnp.mean(jnp.square(o), axis=-1, keepdims=True) + NORM_EPS) * g_norm
    y = y * jax.nn.silu(zz)
    return y.reshape(o.shape[0], o.shape[1], GDN_DIM).astype(z.dtype)


def gdn_branch(qkv, qkv_c, z, z_c, a, a_c, b, b_c, p, ctx_out):
    s0 = jnp.zeros((qkv.shape[0], GDN_HEADS, GDN_HEAD_DIM, GDN_HEAD_DIM), jnp.float32)
    o_lat, o_ctx = [], []
    for d in range(2):
        fl = _flip if d else _ident
        sl = slice(d * GDN_HEADS, (d + 1) * GDN_HEADS)
        prm = (p['gdn_conv_w'][d], p['gdn_a_log'][d], p['gdn_dt_bias'][d])
        oc, s_ctx = gdn_direction(fl(qkv_c), fl(a_c[..., sl]), fl(b_c[..., sl]), s0, *prm)
        ol, _ = gdn_direction(fl(qkv), fl(a[..., sl]), fl(b[..., sl]), s_ctx, *prm)
        o_lat.append(fl(ol))
        o_ctx.append(fl(oc))
    y = _gdn_out(o_lat[0] + o_lat[1], z, p['gdn_norm_g'])
    y_c = _gdn_out(o_ctx[0] + o_ctx[1], z_c, p['gdn_norm_g']) if ctx_out else None
    return y, y_c


def _attend(qg, keys, values, masks, sink):
    n_kv, n_grp = qg.shape[2], qg.shape[3]
    scale = qg.shape[-1] ** -0.5
    logits = []
    for kk, mm in zip(keys, masks):
        s = jnp.einsum('bqhgd,bkhd->bhgqk', qg, kk).astype(jnp.float32) * scale
        if mm is not None:
            s = jnp.where(mm, s, -jnp.inf)
        logits.append(s)
    sink_col = jnp.broadcast_to(sink.astype(jnp.float32).reshape(n_kv, n_grp, 1, 1), logits[0].shape[:-1] + (1,))
    probs = jax.nn.softmax(jnp.concatenate(logits + [sink_col], axis=-1), axis=-1)
    out = None
    off = 0
    for vv in values:
        n = vv.shape[1]
        o = jnp.einsum('bhgqk,bkhd->bqhgd', probs[..., off:off + n].astype(vv.dtype), vv)
        out = o if out is None else out + o
        off += n
    return out


def window_attention(q, k, v, kc, vc, sink):
    B_, T, Hq, d = q.shape
    n_kv = k.shape[2]
    nb = T // ATT_BLOCK
    span = 3 * ATT_BLOCK
    qb = jnp.moveaxis(q.reshape(B_, nb, ATT_BLOCK, n_kv, Hq // n_kv, d), 1, 0)
    pad = ((0, 0), (ATT_BLOCK, ATT_BLOCK), (0, 0), (0, 0))
    kp = jnp.pad(k, pad)
    vp = jnp.pad(v, pad)
    rel = (jnp.arange(span) - ATT_BLOCK)[None, :] - jnp.arange(ATT_BLOCK)[:, None]
    band = jnp.abs(rel) <= ATT_WINDOW

    def block(args):
        qi, i = args
        start = i * ATT_BLOCK
        ki = lax.dynamic_slice_in_dim(kp, start, span, axis=1)
        vi = lax.dynamic_slice_in_dim(vp, start, span, axis=1)
        kpos = start - ATT_BLOCK + jnp.arange(span)
        mask = band & ((kpos >= 0) & (kpos < T))[None, :]
        return _attend(qi, (ki, kc), (vi, vc), (mask, None), sink)

    o = lax.map(block, (qb, jnp.arange(nb)))
    return jnp.moveaxis(o, 0, 1).reshape(B_, T, Hq * d)


def context_attention(qc, kc, vc, sink):
    B_, Lc, Hq, d = qc.shape
    n_kv = kc.shape[2]
    qg = qc.reshape(B_, Lc, n_kv, Hq // n_kv, d)
    return _attend(qg, (kc,), (vc,), (None,), sink).reshape(B_, Lc, Hq * d)


def token_mixer(h, hc, rope, p, ctx_out):
    B_, T, D = h.shape
    Lc = hc.shape[1]
    (ca, cg, lx, lg, gqkv, gz, ga, gb, aq, ak, av, gates) = split_in(h @ p['w_in'])
    (ca_c, cg_c, lx_c, lg_c, gqkv_c, gz_c, ga_c, gb_c, aq_c, ak_c, av_c, gates_c) = split_in(hc @ p['w_in'])
    cos, sin = rope

    y_conv = conformer_conv(ca, cg, p)
    y_lru, y_lru_c = rglru_branch(lx, lx_c, lg, lg_c, p, ctx_out)
    y_gdn, y_gdn_c = gdn_branch(gqkv, gqkv_c, gz, gz_c, ga, ga_c, gb, gb_c, p, ctx_out)
    q = apply_rope(aq.reshape(B_, T, ATT_HEADS, ATT_HEAD_DIM), cos, sin)
    k = apply_rope(ak.reshape(B_, T, ATT_KV_HEADS, ATT_HEAD_DIM), cos, sin)
    v = av.reshape(B_, T, ATT_KV_HEADS, ATT_HEAD_DIM)
    kc = ak_c.reshape(B_, Lc, ATT_KV_HEADS, ATT_HEAD_DIM)
    vc = av_c.reshape(B_, Lc, ATT_KV_HEADS, ATT_HEAD_DIM)
    y_att = window_attention(q, k, v, kc, vc, p['att_sink'])

    def merge(y_a, y_b, y_c, y_d, gate_cols):
        gt = jax.nn.sigmoid(gate_cols).reshape(gate_cols.shape[:-1] + (N_BRANCH, D))
        mrg = gt[..., 0, :] * (y_a @ p['proj_conv'])
        mrg = mrg + gt[..., 1, :] * (y_b @ p['proj_lru'])
        mrg = mrg + gt[..., 2, :] * (y_c @ p['proj_gdn'])
        mrg = mrg + gt[..., 3, :] * (y_d @ p['proj_att'])
        return mrg @ p['w_out']

    y = merge(y_conv, y_lru, y_gdn, y_att, gates)
    if not ctx_out:
        return y, None
    y_conv_c = conformer_conv(ca_c, cg_c, p)
    y_att_c = context_attention(aq_c.reshape(B_, Lc, ATT_HEADS, ATT_HEAD_DIM), kc, vc, p['att_sink'])
    yc = merge(y_conv_c, y_lru_c, y_gdn_c, y_att_c, gates_c)
    return y, yc


def trunk_layer(x, xc, mod, mod_c, rope, p, last):
    def post_norm(s, out, j):
        return layer_norm(DEEPNORM_ALPHA * s + out) * p['norm_g'][j] + p['norm_b'][j]

    def ffn_step(s, m, j, f):
        hm = modulate(s, m[:, :, j, 0], m[:, :, j, 1])
        y = swiglu(hm, p['ffn_w_in'][f], p['ffn_w_out'][f])
        return post_norm(s, 0.5 * m[:, :, j, 2] * y, j)

    x = ffn_step(x, mod, 0, 0)
    xc = ffn_step(xc, mod_c, 0, 0)
    h = modulate(x, mod[:, :, 1, 0], mod[:, :, 1, 1])
    hc = modulate(xc, mod_c[:, :, 1, 0], mod_c[:, :, 1, 1])
    y, yc = token_mixer(h, hc, rope, p, not last)
    x = post_norm(x, mod[:, :, 1, 2] * y, 1)
    x = ffn_step(x, mod, 2, 1)
    if not last:
        xc = post_norm(xc, mod_c[:, :, 1, 2] * yc, 1)
        xc = ffn_step(xc, mod_c, 2, 1)
    return x, xc


def setup_inputs(seed: int = 0) -> dict:
    key = jax.random.key(seed)
    ks = iter(jax.random.split(key, 48))
    L = DEPTH
    D = D_MODEL
    bd = LRU_DIM // LRU_BLOCKS

    def nrm(shape, scale):
        return jax.random.normal(next(ks), shape, jnp.float32) * scale

    def unif(shape, lo, hi):
        return jax.random.uniform(next(ks), shape, jnp.float32, minval=lo, maxval=hi)

    a_pow = unif((L, 2, LRU_DIM), 0.9, 0.999) ** (1.0 / LRU_C)
    dt = jnp.exp(unif((L, 2, GDN_HEADS), float(np.log(1e-3)), float(np.log(1e-1))))
    return {
        'x': nrm((BATCH, SEQ, D), 1.0),
        'c': nrm((BATCH, D), 1.0),
        'ctx': nrm((BATCH, CTX_LEN, D), 1.0),
        'c_ctx': nrm((D,), 1.0),
        'ada_w': nrm((L, D, 9 * D), D ** -0.5),
        'ada_b': nrm((L, 9 * D), 0.02),
        'norm_g': 1.0 + nrm((L, 3, D), 0.02),
        'norm_b': nrm((L, 3, D), 0.02),
        'ffn_w_in': nrm((L, 2, D, 2 * D_FF), D ** -0.5),
        'ffn_w_out': nrm((L, 2, D_FF, D), DEEPNORM_BETA * D_FF ** -0.5),
        'w_in': nrm((L, D, IN_TOTAL), D ** -0.5),
        'conv_w': nrm((L, CONV_WIDTH, CONV_DIM), CONV_WIDTH ** -0.5),
        'conv_b': nrm((L, CONV_DIM), 0.02),
        'conv_norm_g': 1.0 + nrm((L, CONV_DIM), 0.02),
        'conv_norm_b': nrm((L, CONV_DIM), 0.02),
        'lru_conv_w': nrm((L, 2, LRU_CONV, LRU_DIM), LRU_CONV ** -0.5),
        'lru_conv_b': nrm((L, 2, LRU_DIM), 0.02),
        'lru_w_r': nrm((L, 2, LRU_BLOCKS, bd, bd), bd ** -0.5),
        'lru_b_r': nrm((L, 2, LRU_DIM), 0.02),
        'lru_w_i': nrm((L, 2, LRU_BLOCKS, bd, bd), bd ** -0.5),
        'lru_b_i': nrm((L, 2, LRU_DIM), 0.02),
        'lru_lam': jnp.log(a_pow) - jnp.log1p(-a_pow),
        'gdn_conv_w': nrm((L, 2, GDN_CONV, 3 * GDN_DIM), GDN_CONV ** -0.5),
        'gdn_a_log': jnp.log(unif((L, 2, GDN_HEADS), 1.0, 16.0)),
        'gdn_dt_bias': dt + jnp.log(-jnp.expm1(-dt)),
        'gdn_norm_g': 1.0 + nrm((L, GDN_HEAD_DIM), 0.02),
        'att_sink': nrm((L, ATT_HEADS), 0.5),
        'proj_conv': nrm((L, CONV_DIM, D), DEEPNORM_BETA * CONV_DIM ** -0.5),
        'proj_lru': nrm((L, LRU_DIM, D), DEEPNORM_BETA * LRU_DIM ** -0.5),
        'proj_gdn': nrm((L, GDN_DIM, D), DEEPNORM_BETA * GDN_DIM ** -0.5),
        'proj_att': nrm((L, ATT_DIM, D), DEEPNORM_BETA * ATT_DIM ** -0.5),
        'w_out': nrm((L, D, D), DEEPNORM_BETA * D ** -0.5),
    }


def reference(x, c, ctx, c_ctx, ada_w, ada_b, norm_g, norm_b, ffn_w_in, ffn_w_out, w_in, conv_w, conv_b, conv_norm_g, conv_norm_b, lru_conv_w, lru_conv_b, lru_w_r, lru_b_r, lru_w_i, lru_b_i, lru_lam, gdn_conv_w, gdn_a_log, gdn_dt_bias, gdn_norm_g, att_sink, proj_conv, proj_lru, proj_gdn, proj_att, w_out):
    B_, T, D = x.shape
    n_rows = T // GRID_W
    rope = axial_rope(n_rows)
    xc = ctx
    for l in range(DEPTH):
        p = dict(norm_g=norm_g[l], norm_b=norm_b[l], ffn_w_in=ffn_w_in[l], ffn_w_out=ffn_w_out[l], w_in=w_in[l], conv_w=conv_w[l], conv_b=conv_b[l], conv_norm_g=conv_norm_g[l], conv_norm_b=conv_norm_b[l], lru_conv_w=lru_conv_w[l], lru_conv_b=lru_conv_b[l], lru_w_r=lru_w_r[l], lru_b_r=lru_b_r[l], lru_w_i=lru_w_i[l], lru_b_i=lru_b_i[l], lru_lam=lru_lam[l], gdn_conv_w=gdn_conv_w[l], gdn_a_log=gdn_a_log[l], gdn_dt_bias=gdn_dt_bias[l], gdn_norm_g=gdn_norm_g[l], att_sink=att_sink[l], proj_conv=proj_conv[l], proj_lru=proj_lru[l], proj_gdn=proj_gdn[l], proj_att=proj_att[l], w_out=w_out[l])
        mod = (jax.nn.silu(c) @ ada_w[l] + ada_b[l]).reshape(B_, 1, 3, 3, D)
        mod_c = (jax.nn.silu(c_ctx) @ ada_w[l] + ada_b[l]).reshape(1, 1, 3, 3, D)
        x, xc = trunk_layer(x, xc, mod, mod_c, rope, p, l == DEPTH - 1)
    return x
```

```python
import contextlib
import numpy as np
import concourse.bass as bass
import concourse.mybir as mybir
from concourse.ap import AP
from concourse.bass_utils import run_bass_kernel_spmd

F32 = mybir.dt.float32
BF16 = mybir.dt.bfloat16
AF = mybir.ActivationFunctionType
ALU = mybir.AluOpType

ENGS = ("pe", "act", "dve", "pool", "sp")
N_DMA_SEMS = 48

D = 1024
DFF = 2816
NCTX = 256
INW = 8976
INW_X = INW + 640
O_CA, O_CG, O_LX, O_LG, O_GQKV, O_GZ, O_GA, O_GB, O_AQ, O_AK, O_AV, O_GATES = (
    0, 512, 1024, 1536, 2048, 3584, 4096, 4104, 4112, 4624, 4752, 4880)
O_AQS, O_AKS = INW, INW + 512
ALPHA = 4.0 ** 0.25
LN_EPS = 1e-5
NORM_EPS = 1e-6
LRU_C = 8.0


class Buf:
    __slots__ = ("name", "lw", "rd")

    def __init__(self, name=""):
        self.name = name
        self.lw = None
        self.rd = []


class Op:
    __slots__ = ("eng", "fn", "waits", "inc", "idx", "dma", "sem", "val", "cnt")

    def __init__(self, eng, fn, dma):
        self.eng = eng
        self.fn = fn
        self.dma = dma
        self.waits = []
        self.inc = False
        self.idx = -1
        self.sem = -1
        self.val = 0
        self.cnt = 0


class Prog:
    def __init__(self, nc):
        self.nc = nc
        self.q = {e: [] for e in ENGS}
        self.waited = {e: {s: -1 for s in ENGS} for e in ENGS}
        self.waited_dma = {e: set() for e in ENGS}
        self.dma_slot_last = [None] * N_DMA_SEMS
        self.dma_slot_cnt = [0] * N_DMA_SEMS
        self.dma_k = 0
        self.pending = {e: [] for e in ENGS}

    def _need(self, op, d):
        if d is None or d is op:
            return
        e = op.eng
        if d.dma:
            if id(d) in self.waited_dma[e]:
                return
            self.waited_dma[e].add(id(d))
            op.waits.append(d)
            return
        if d.eng == e and not op.dma and e == "pe":
            return
        if self.waited[e][d.eng] >= d.idx:
            return
        self.waited[e][d.eng] = d.idx
        op.waits.append(d)
        d.inc = True

    def add(self, eng, fn, reads=(), writes=(), dma=False):
        op = Op(eng, fn, dma)
        op.idx = len(self.q[eng])
        if self.pending[eng]:
            for d in self.pending[eng]:
                self._need(op, d)
            self.pending[eng] = []
        for b in reads:
            self._need(op, b.lw)
        for b in writes:
            self._need(op, b.lw)
            for r in b.rd:
                self._need(op, r)
        if dma:
            s = self.dma_k % N_DMA_SEMS
            self.dma_k += 1
            self._need(op, self.dma_slot_last[s])
            self.dma_slot_last[s] = op
            self.dma_slot_cnt[s] += 1
            op.sem = s
            op.val = 16 * self.dma_slot_cnt[s]
        for b in reads:
            b.rd.append(op)
        for b in writes:
            b.lw = op
            b.rd = []
        self.q[eng].append(op)
        return op

    def dma(self, eng, out, in_, reads=(), writes=(), **kw):
        return self.add(eng, lambda e: e.dma_start(out=out, in_=in_, **kw), reads, writes, dma=True)

    def barrier(self):
        deps = []
        for e in ENGS:
            for op in reversed(self.q[e]):
                if not op.dma:
                    deps.append(op)
                    break
        deps += [o for o in self.dma_slot_last if o is not None]
        for e in ENGS:
            self.pending[e] = list(deps)

    def emit(self):
        nc = self.nc
        tot = {}
        for e in ENGS:
            c = 0
            for op in self.q[e]:
                if not op.dma and op.inc:
                    c += 1
                    op.cnt = c
            tot[e] = c
        with contextlib.ExitStack() as st:
            esem = {e: st.enter_context(nc.semaphore("s_" + e)) for e in ENGS}
            dsem = [st.enter_context(nc.semaphore("d_%d" % i)) for i in range(N_DMA_SEMS)]
            block = st.enter_context(nc.Block())

            def run(name, eng):
                for op in self.q[name]:
                    for d in op.waits:
                        if d.dma:
                            eng.wait_ge(dsem[d.sem], d.val)
                        else:
                            eng.wait_ge(esem[d.eng], d.cnt)
                    ins = op.fn(eng)
                    if op.dma:
                        ins.then_inc(dsem[op.sem], 16)
                    elif op.inc:
                        ins.then_inc(esem[op.eng], 1)
                if name == "sp":
                    for s in range(N_DMA_SEMS):
                        if self.dma_slot_cnt[s]:
                            eng.wait_ge(dsem[s], 16 * self.dma_slot_cnt[s])
                    for e in ENGS:
                        if tot[e]:
                            eng.wait_ge(esem[e], tot[e])

            @block.tensor
            def _(eng):
                run("pe", eng)

            @block.scalar
            def _(eng):
                run("act", eng)

            @block.vector
            def _(eng):
                run("dve", eng)

            @block.gpsimd
            def _(eng):
                run("pool", eng)

            @block.sync
            def _(eng):
                run("sp", eng)


class T:
    __slots__ = ("t", "b", "psum")

    def __init__(self, t, name, psum=False):
        self.t = t
        self.b = Buf(name)
        self.psum = psum


WEIGHT_SPECS = [
    ("ada_w", (2, D, 9 * D)), ("ada_b", (2, 9 * D)), ("norm_g", (2, 3, D)), ("norm_b", (2, 3, D)),
    ("ffn_w_in", (2, 2, D, 2 * DFF)), ("ffn_w_out", (2, 2, DFF, D)), ("w_in", (2, D, INW)),
    ("conv_w", (2, 31, 512)), ("conv_b", (2, 512)), ("conv_norm_g", (2, 512)), ("conv_norm_b", (2, 512)),
    ("lru_conv_w", (2, 2, 4, 512)), ("lru_conv_b", (2, 2, 512)), ("lru_w_r", (2, 2, 8, 64, 64)),
    ("lru_b_r", (2, 2, 512)), ("lru_w_i", (2, 2, 8, 64, 64)), ("lru_b_i", (2, 2, 512)), ("lru_lam", (2, 2, 512)),
    ("gdn_conv_w", (2, 2, 4, 1536)), ("gdn_a_log", (2, 2, 4)), ("gdn_dt_bias", (2, 2, 4)), ("gdn_norm_g", (2, 128)),
    ("att_sink", (2, 8)), ("proj_conv", (2, 512, D)), ("proj_lru", (2, 512, D)), ("proj_gdn", (2, 512, D)),
    ("proj_att", (2, 512, D)), ("w_out", (2, D, D)),
]


class MK:
    def __init__(self, TL, n_layers=2, dbg=None):
        self.TL = TL
        self.TT = TL + NCTX
        self.n_layers = n_layers
        self.dbg = dbg or {}
        self.nc = bass.Bass("TRN2", target_bir_lowering=False)
        self.P = Prog(self.nc)
        self.st = contextlib.ExitStack()
        self.arena = None
        self.store_q = "sp"
        self._tile_off = {}
        self.groups = [(0, NCTX, True)] + [(NCTX + 512 * i, 512, False) for i in range(TL // 512)]

    def dram_in(self, name, shape, dt=F32):
        return self.nc.dram_tensor(name, list(shape), dt, kind="ExternalInput").ap()

    def dram_out(self, name, shape, dt=F32):
        return self.nc.dram_tensor(name, list(shape), dt, kind="ExternalOutput").ap()

    def dram(self, name, shape, dt=F32):
        return self.nc.dram_tensor(name, list(shape), dt).ap()

    def sb(self, name, shape, dt=F32):
        if self.arena is None:
            self.arena_bytes = 206 * 1024
            self.arena = self.st.enter_context(self.nc.sbuf_tensor("arena", [128, self.arena_bytes // 4], F32))
            self.aoff = 0
        esz = 2 if dt == BF16 else 4
        n = 1
        for d in shape[1:]:
            n *= d
        nbytes = (n * esz + 3) // 4 * 4
        assert self.aoff + nbytes <= self.arena_bytes, "SBUF arena overflow at %s (%d + %d)" % (name, self.aoff, nbytes)
        v = self.arena[0:shape[0], self.aoff // 4:(self.aoff + nbytes) // 4]
        tile_off = self.aoff
        self.aoff += nbytes
        if dt == BF16:
            v = v.bitcast(BF16)
        if len(shape) == 3:
            v = v.rearrange("p (a b) -> p a b", a=shape[1])
        elif len(shape) == 4:
            v = v.rearrange("p (a b c) -> p a b c", a=shape[1], b=shape[2])
        t = T(v, name)
        self._tile_off[id(t)] = tile_off
        return t

    def mark(self):
        return self.aoff

    def release(self, m):
        self.aoff = m
        self.store_q = "sp"
        self.P.barrier()

    def ps(self, name, shape, dt=F32):
        return T(self.st.enter_context(self.nc.psum_tensor(name, list(shape), dt)), name, psum=True)

    def op(self, eng, fn, reads=(), writes=()):
        rd = [r.b if isinstance(r, T) else r for r in reads if not (isinstance(r, T) and r.psum)]
        wr = [w.b if isinstance(w, T) else w for w in writes] + [r.b for r in reads if isinstance(r, T) and r.psum]
        return self.P.add(eng, fn, rd, wr)

    def dma(self, eng, out, in_, reads=(), writes=(), cast=False, **kw):
        if eng == "pool" and not cast:
            eng = self.store_q
        return self.P.dma(eng, out, in_, [r.b if isinstance(r, T) else r for r in reads],
                          [w.b if isinstance(w, T) else w for w in writes], **kw)

    def mm(self, out, lhsT, rhs, start, stop, reads, writes):
        return self.op("pe", lambda e: e.matmul(out, lhsT=lhsT, rhs=rhs, start=start, stop=stop), reads, writes)

    def declare_io(self):
        self.x_in = self.dram_in("x", (self.TL, D))
        self.ctx_in = self.dram_in("ctx", (NCTX, D))
        self.c_in = self.dram_in("c", (1, D))
        self.cctx_in = self.dram_in("c_ctx", (1, D))
        self.w = {n: self.dram_in(n, (self.n_layers,) + tuple(s[1:])) for n, s in WEIGHT_SPECS}
        self.ropec_in = self.dram_in("rope_c", (128, self.TL))
        self.ropes_in = self.dram_in("rope_s", (128, self.TL))
        self.attmask_in = self.dram_in("att_mask", (128, 384))
        self.gdnmask_in = self.dram_in("gdn_mask", (128, 1024))
        self.out = self.dram_out("out", (self.TL, D))
        TT = self.TT
        self.xs = self.dram("xs", (TT, D))
        self.wb = {}
        for l in range(self.n_layers):
            self.wb[l] = dict(
                w1=self.dram("w1b%d" % l, (2, D, 2 * DFF), BF16),
                w2=self.dram("w2b%d" % l, (2, DFF, D), BF16),
                win=self.dram("winb%d" % l, (D, INW_X), BF16),
                proj=self.dram("projb%d" % l, (4, 512, D), BF16),
                wout=self.dram("woutb%d" % l, (D, D), BF16),
            )
        mk_d = self.dram_out if self.dbg.get("dump") else self.dram
        self.s_ucv = mk_d("s_ucv", (512, TT))
        self.s_lx = mk_d("s_lx", (512, TT))
        self.s_lg = mk_d("s_lg", (512, TT))
        self.s_gqkv = mk_d("s_gqkv", (1536, TT))
        self.s_gz = mk_d("s_gz", (TT, 512))
        self.s_ab = mk_d("s_ab", (TT, 16))
        self.s_q = self.dram("s_q", (512, TT), BF16)
        self.s_k = self.dram("s_k", (128, TT), BF16)
        self.s_v = self.dram("s_v", (TT, 128), BF16)
        mk_y = self.dram_out if self.dbg.get("dump") else self.dram
        self.s_y = [mk_y("s_y%d" % i, (512, TT), BF16) for i in range(4)]
        self.s_hf = mk_d("s_hf", (512, TT))
        self.s_of = mk_d("s_of", (TT, 512))
        self.s_ob = self.dram("s_ob", (TT, 512))

    def alloc_common(self):
        nc = self.nc
        self.ident = self.sb("ident", (128, 128), F32)
        self.identb = self.sb("identb", (128, 128), BF16)
        self.ones = self.sb("ones", (128, 128), F32)
        self.eps_t = self.sb("eps_t", (128, 1), F32)
        self.op("pool", lambda e: e.memset(self.ident.t[:], 1.0), [], [self.ident])
        self.op("pool", lambda e: e.affine_select(out=self.ident.t[:], in_=self.ident.t[:], pattern=[[-1, 128]],
                                                  compare_op=ALU.is_equal, fill=0.0, base=0, channel_multiplier=1),
                [self.ident], [self.ident])
        self.op("pool", lambda e: e.tensor_copy(out=self.identb.t[:], in_=self.ident.t[:]), [self.ident], [self.identb])
        self.op("pool", lambda e: e.memset(self.ones.t[:], 1.0), [], [self.ones])
        self.op("pool", lambda e: e.memset(self.eps_t.t[:], LN_EPS), [], [self.eps_t])
        psall = self.st.enter_context(self.nc.psum_tensor("psall", [128, 4096], F32))
        self.pbank = [T(psall[:, i * 512:(i + 1) * 512], "pb%d" % i, psum=True) for i in range(8)]
        self.pquad = [T(psall[:, i * 1024:(i + 1) * 1024], "pq%d" % i, psum=True) for i in range(4)]

    def cast_weights(self, l):
        wb = self.wb[l]
        w = self.w

        def cast(dst, src, rows, rstep=128):
            ncol = src.shape[-1]
            b = max(d for d in range(1, 1025) if ncol % d == 0)
            for r0 in range(0, rows, rstep):
                r1 = min(rows, r0 + rstep)
                self.dma("pool", dst[r0:r1].rearrange("r (a b) -> r a b", b=b), src[r0:r1].rearrange("r (a b) -> r a b", b=b), cast=True)

        for f in range(2):
            cast(wb["w1"][f], w["ffn_w_in"][l, f], D)
            cast(wb["w2"][f], w["ffn_w_out"][l, f], DFF)
        cast(wb["win"][:, 0:INW], w["w_in"][l], D)
        for h in range(10):
            src0 = O_AQ + 64 * h
            dst0 = O_AQS + 64 * h
            for r0 in range(0, D, 256):
                self.dma("pool", wb["win"][r0:r0 + 256, dst0:dst0 + 32], w["w_in"][l, r0:r0 + 256, src0 + 32:src0 + 64], cast=True)
                self.dma("pool", wb["win"][r0:r0 + 256, dst0 + 32:dst0 + 64], w["w_in"][l, r0:r0 + 256, src0:src0 + 32], cast=True)
        for i, n in enumerate(("proj_conv", "proj_lru", "proj_gdn", "proj_att")):
            cast(wb["proj"][i], w[n][l], 512)
        cast(wb["wout"], w["w_out"][l], D)

    def modulation(self, l):
        nc = self.nc
        w = self.w
        if l == 0:
            self.cT = self.sb("cT", (128, 8, 2), F32)
            self.colmod = self.sb("colmod", (128, 48, 2), F32)
            self.gate_bc = [[self.sb("gbc%d_%d" % (r, j), (128, D), F32) for j in range(3)] for r in range(2)]
            self.ng_bc = [self.sb("ngbc%d" % j, (128, D), F32) for j in range(3)]
            self.nb_bc = [self.sb("nbbc%d" % j, (128, D), F32) for j in range(3)]
            self.sel2 = self.sb("sel2", (2, 2, 128), F32)
            craw = self.sb("craw", (128, 8, 2), F32)
            self.dma("sp", craw.t[:, :, 0], self.c_in.rearrange("o (k p) -> p (o k)", p=128), writes=[craw], allow_slow_non_contiguous=True)
            self.dma("sp", craw.t[:, :, 1], self.cctx_in.rearrange("o (k p) -> p (o k)", p=128), writes=[craw], allow_slow_non_contiguous=True)
            self.op("act", lambda e: e.activation(out=self.cT.t[:], in_=craw.t[:], func=AF.Silu), [craw], [self.cT])
            self.op("pool", lambda e: e.memset(self.sel2.t[:], 0.0), [], [self.sel2])
            self.op("pool", lambda e: e.memset(self.sel2.t[0:1, 0, :], 1.0), [self.sel2], [self.sel2])
            self.dma("sp", self.sel2.t[1:2, 1, :], self.ones.t[0:1, :], reads=[self.ones, self.sel2], writes=[self.sel2])
        mk = self.mark()
        adaw = [self.sb("adaw%d" % i, (128, 8, 512), F32) for i in range(2)]
        adab = [self.sb("adab%d" % i, (2, 512), F32) for i in range(2)]
        modrow = [self.sb("modrow%d" % i, (2, 512), F32) for i in range(2)]
        nrow = [self.sb("nrow%d" % i, (1, 512), F32) for i in range(2)]
        cT = self.cT
        pb = self.pbank
        colbank = pb[2]
        for n in range(18):
            wt = adaw[n % 2]
            ab = adab[n % 2]
            mr = modrow[n % 2]
            self.dma("sp", wt.t[:], w["ada_w"][l, :, n * 512:(n + 1) * 512].rearrange("(k p) c -> p k c", p=128), writes=[wt])
            self.dma("sp", ab.t[0:1, :], w["ada_b"][l:l + 1, n * 512:(n + 1) * 512], writes=[ab])
            self.dma("sp", ab.t[1:2, :], w["ada_b"][l:l + 1, n * 512:(n + 1) * 512], writes=[ab])
            bank = pb[n % 2]
            for k in range(8):
                self.mm(bank.t[0:2, :], cT.t[:, k, :], wt.t[:, k, :], k == 0, k == 7, [cT, wt], [bank])
            self.op("dve", lambda e, mr=mr, bank=bank, ab=ab: e.tensor_tensor(out=mr.t[:], in0=bank.t[0:2, :], in1=ab.t[:], op=ALU.add),
                    [bank, ab], [mr])
            j, kind, h = n // 6, (n // 2) % 3, n % 2
            if kind < 2:
                for q in range(4):
                    kc = h * 4 + q
                    idx = (j * 2 + kind) * 8 + kc
                    self.mm(colbank.t[:, idx * 2:idx * 2 + 2], mr.t[0:2, q * 128:(q + 1) * 128], self.ident.t[0:2, 0:2], True, True,
                            [mr, self.ident], [colbank])
            else:
                for r in range(2):
                    gb = pb[3 + r]
                    self.mm(gb.t[:], self.sel2.t[0:2, r, :], mr.t[0:2, :], True, True, [self.sel2, mr], [gb])
                    g = self.gate_bc[r][j]
                    self.op("act", lambda e, g=g, gb=gb, h=h, j=j: e.mul(out=g.t[:, h * 512:(h + 1) * 512], in_=gb.t[:],
                                                                         mul=(1.0 if j == 1 else 0.5)), [gb], [g])
        self.op("dve", lambda e: e.tensor_copy(out=self.colmod.t[:].rearrange("p a b -> p (a b)"), in_=colbank.t[:, 0:96]),
                [colbank], [self.colmod])
        for j in range(3):
            i0 = (j * 2 + 1) * 8
            self.op("dve", lambda e, i0=i0: e.tensor_scalar_add(out=self.colmod.t[:, i0:i0 + 8, :], in0=self.colmod.t[:, i0:i0 + 8, :], scalar1=1.0),
                    [self.colmod], [self.colmod])
        k = 0
        for which, (name, dst) in enumerate((("norm_g", self.ng_bc), ("norm_b", self.nb_bc))):
            for j in range(3):
                for h in range(2):
                    nr = nrow[k % 2]
                    bank = pb[5 + k % 2]
                    k += 1
                    self.dma("sp", nr.t[0:1, :], w[name][l, j:j + 1, h * 512:(h + 1) * 512], writes=[nr])
                    self.mm(bank.t[:], self.ones.t[0:1, :], nr.t[0:1, :], True, True, [self.ones, nr], [bank])
                    self.op("act", lambda e, d=dst[j], bank=bank, h=h: e.copy(out=d.t[:, h * 512:(h + 1) * 512], in_=bank.t[:]), [bank], [dst[j]])
        self.release(mk)

    def colvec(self, j, kind, kc, r):
        i = (j * 2 + kind) * 8 + kc
        return self.colmod.t[:, i, r:r + 1]

    def alloc_ln(self, nw=4):
        self.xt = [self.sb("xt%d" % i, (128, D), F32) for i in range(4)]
        self.xn = [self.sb("xn%d" % i, (128, D), BF16) for i in range(2)]
        self.st6 = [self.sb("st6_%d" % i, (128, 2, 6), F32) for i in range(2)]
        self.mv = [self.sb("mv%d" % i, (128, 2), F32) for i in range(2)]
        self.rstd = [self.sb("rstd%d" % i, (128, 1), F32) for i in range(2)]
        self.hT = self.sb("hT", (128, 8, 512), BF16)
        self.wblk = [self.sb("wblk%d" % i, (128, 8, 512), BF16) for i in range(nw)]
        self.tmp = [self.sb("tmp%d" % i, (128, D), F32) for i in range(2)]
        self.wk = 0
        self.lnk = 0

    def alloc_ffn(self):
        self.actT = self.sb("actT", (128, 22, 512), BF16)
        self.w2 = self.sb("w2", (128, 22, D), BF16)
        self.sg = [self.sb("sg%d" % i, (128, 512), F32) for i in range(2)]

    def alloc_dense(self):
        self.alloc_ln()
        self.alloc_ffn()

    def next_wblk(self):
        t = self.wblk[self.wk % len(self.wblk)]
        self.wk += 1
        return t

    def ln_to_hT(self, xt, j, r, t):
        i = self.lnk % 2
        self.lnk += 1
        st6, mv, rstd, xn = self.st6[i], self.mv[i], self.rstd[i], self.xn[i]
        for h in range(2):
            self.op("dve", lambda e, h=h: e.bn_stats(out=st6.t[:, h, :], in_=xt.t[:, h * 512:(h + 1) * 512]), [xt], [st6])
        self.op("dve", lambda e: e.bn_aggr(out=mv.t[:], in_=st6.t[:].rearrange("p a b -> p (a b)")), [st6], [mv])
        self.op("act", lambda e: e.activation(out=rstd.t[:], in_=mv.t[:, 1:2], func=AF.Sqrt, bias=self.eps_t.t[:], scale=1.0), [mv, self.eps_t], [rstd])
        self.op("dve", lambda e: e.reciprocal(out=rstd.t[:], in_=rstd.t[:]), [rstd], [rstd])
        self.op("dve", lambda e: e.tensor_scalar(out=xn.t[:], in0=xt.t[:], scalar1=mv.t[:, 0:1], scalar2=rstd.t[:], op0=ALU.subtract, op1=ALU.mult),
                [xt, mv, rstd], [xn])
        bank = self.pbank[i]
        pbf = bank.t[:].bitcast(BF16)
        for kc in range(8):
            self.op("pe", lambda e, kc=kc: e.transpose(pbf[:, kc * 128:(kc + 1) * 128], xn.t[:, kc * 128:(kc + 1) * 128], self.identb.t[:]),
                    [xn, self.identb], [bank])
        for kc in range(8):
            eng = "act" if kc % 2 == 0 else "dve"
            dst = self.hT.t[:, kc, t * 128:(t + 1) * 128]
            src = pbf[:, kc * 128:(kc + 1) * 128]
            sc = self.colvec(j, 1, kc, r)
            sh = self.colvec(j, 0, kc, r)
            if eng == "act":
                self.op("act", lambda e, dst=dst, src=src, sc=sc, sh=sh: e.activation(out=dst, in_=src, func=AF.Identity, scale=sc, bias=sh),
                        [bank, self.colmod], [self.hT])
            else:
                self.op("dve", lambda e, dst=dst, src=src, sc=sc, sh=sh: e.tensor_scalar(out=dst, in0=src, scalar1=sc, scalar2=sh, op0=ALU.mult, op1=ALU.add),
                        [bank, self.colmod], [self.hT])

    def post_norm(self, xt, ybanks, gate, j, out_t):
        i = self.lnk % 2
        self.lnk += 1
        st6, mv, rstd, tmp = self.st6[i], self.mv[i], self.rstd[i], self.tmp[i]
        for h in range(2):
            sl = slice(h * 512, (h + 1) * 512)
            self.op("dve", lambda e, h=h, sl=sl: e.tensor_tensor(out=tmp.t[:, sl], in0=ybanks[h].t[:], in1=gate.t[:, sl], op=ALU.mult),
                    [ybanks[h], gate], [tmp])
            self.op("dve", lambda e, sl=sl: e.scalar_tensor_tensor(out=tmp.t[:, sl], in0=xt.t[:, sl], scalar=ALPHA, in1=tmp.t[:, sl], op0=ALU.mult, op1=ALU.add),
                    [xt, tmp], [tmp])
            self.op("dve", lambda e, h=h, sl=sl: e.bn_stats(out=st6.t[:, h, :], in_=tmp.t[:, sl]), [tmp], [st6])
        self.op("dve", lambda e: e.bn_aggr(out=mv.t[:], in_=st6.t[:].rearrange("p a b -> p (a b)")), [st6], [mv])
        self.op("act", lambda e: e.activation(out=rstd.t[:], in_=mv.t[:, 1:2], func=AF.Sqrt, bias=self.eps_t.t[:], scale=1.0), [mv, self.eps_t], [rstd])
        self.op("dve", lambda e: e.reciprocal(out=rstd.t[:], in_=rstd.t[:]), [rstd], [rstd])
        self.op("dve", lambda e: e.tensor_scalar(out=tmp.t[:], in0=tmp.t[:], scalar1=mv.t[:, 0:1], scalar2=rstd.t[:], op0=ALU.subtract, op1=ALU.mult),
                [tmp, mv, rstd], [tmp])
        self.op("pool", lambda e: e.tensor_tensor(out=tmp.t[:], in0=tmp.t[:], in1=self.ng_bc[j].t[:], op=ALU.mult), [tmp, self.ng_bc[j]], [tmp])
        self.op("pool", lambda e: e.tensor_tensor(out=out_t.t[:], in0=tmp.t[:], in1=self.nb_bc[j].t[:], op=ALU.add), [tmp, self.nb_bc[j]], [out_t])

    def ffn(self, l, f, j, r, xts, gw):
        nt = gw // 128
        wb = self.wb[l]
        for t in range(nt):
            self.ln_to_hT(xts[t], j, r, t)
        self.dma("sp", self.w2.t[:], wb["w2"][f].rearrange("(jc p) c -> p jc c", p=128), writes=[self.w2])
        w1 = wb["w1"][f]
        blocks = []

        def load(bi):
            wt = self.next_wblk()
            for half in range(2):
                c0 = half * DFF + bi * 256
                self.dma("sp", wt.t[:, :, half * 256:(half + 1) * 256], w1[:, c0:c0 + 256].rearrange("(k p) c -> p k c", p=128), writes=[wt])
            return wt

        blocks.append(load(0))
        blocks.append(load(1))
        pk = 0
        for bi in range(11):
            if bi + 2 < 11:
                blocks.append(load(bi + 2))
            wt = blocks[bi]
            for sub in range(2):
                jc = bi * 2 + sub
                pg = self.pbank[2 + (pk % 2) * 2]
                pu = self.pbank[3 + (pk % 2) * 2]
                sg = self.sg[pk % 2]
                pk += 1
                for k in range(8):
                    self.mm(pg.t[:, 0:gw], wt.t[:, k, sub * 128:(sub + 1) * 128], self.hT.t[:, k, 0:gw], k == 0, k == 7, [wt, self.hT], [pg])
                for k in range(8):
                    self.mm(pu.t[:, 0:gw], wt.t[:, k, 256 + sub * 128:256 + (sub + 1) * 128], self.hT.t[:, k, 0:gw], k == 0, k == 7, [wt, self.hT], [pu])
                self.op("act", lambda e, pg=pg, sg=sg: e.activation(out=sg.t[:, 0:gw], in_=pg.t[:, 0:gw], func=AF.Silu), [pg], [sg])
                self.op("dve", lambda e, pu=pu, sg=sg, jc=jc: e.tensor_tensor(out=self.actT.t[:, jc, 0:gw], in0=sg.t[:, 0:gw], in1=pu.t[:, 0:gw], op=ALU.mult),
                        [pu, sg], [self.actT])
        for t in range(nt):
            yb = [self.pbank[6], self.pbank[7]]
            for h in range(2):
                for jc in range(22):
                    self.mm(yb[h].t[:], self.actT.t[:, jc, t * 128:(t + 1) * 128], self.w2.t[:, jc, h * 512:(h + 1) * 512], jc == 0, jc == 21,
                            [self.actT, self.w2], [yb[h]])
            self.post_norm(xts[t], yb, self.gate_bc[r][j], j, xts[t])


    def alloc_a2(self):
        self.stg = [self.sb("stg%d" % i, (128, 512), F32) for i in range(4)]
        self.stgb = [self.sb("stgb%d" % i, (128, 512), BF16) for i in range(2)]
        self.rc = self.sb("rope_c_t", (128, 512), F32)
        self.rs = self.sb("rope_s_t", (128, 512), F32)
        self.sk = 0
        self.pk2 = 0

    def _stg(self):
        t = self.stg[self.sk % 4]
        self.sk += 1
        return t

    def _acc(self):
        b = self.pbank[2 + self.pk2 % 4]
        self.pk2 += 1
        return b

    def load_wcols(self, l, c0, n):
        wt = self.next_wblk()
        self.dma("sp", wt.t[:, :, 0:n], self.wb[l]["win"][:, c0:c0 + n].rearrange("(k p) c -> p k c", p=128), writes=[wt])
        return wt

    def cm_mm(self, wt, off, gw):
        acc = self._acc()
        for k in range(8):
            self.mm(acc.t[:, 0:gw], wt.t[:, k, off:off + 128], self.hT.t[:, k, 0:gw], k == 0, k == 7, [wt, self.hT], [acc])
        return acc

    def gelu_tanh(self, x, out, n):
        t = self._stg()
        self.op("dve", lambda e: e.tensor_tensor(out=t.t[:, 0:n], in0=x.t[:, 0:n], in1=x.t[:, 0:n], op=ALU.mult), [x], [t])
        self.op("dve", lambda e: e.tensor_scalar(out=t.t[:, 0:n], in0=t.t[:, 0:n], scalar1=0.044715, scalar2=1.0, op0=ALU.mult, op1=ALU.add), [t], [t])
        self.op("dve", lambda e: e.tensor_tensor(out=t.t[:, 0:n], in0=t.t[:, 0:n], in1=x.t[:, 0:n], op=ALU.mult), [t, x], [t])
        self.op("act", lambda e: e.activation(out=t.t[:, 0:n], in_=t.t[:, 0:n], func=AF.Sigmoid, scale=1.5957691216057308), [t], [t])
        self.op("dve", lambda e: e.tensor_tensor(out=out.t[:, 0:n], in0=t.t[:, 0:n], in1=x.t[:, 0:n], op=ALU.mult), [t, x], [out])

    def mixer_inputs(self, l, r, tok0, gw):
        nt = gw // 128
        ts = slice(tok0, tok0 + gw)
        wa = self.load_wcols(l, O_CA, 512)
        wg = self.load_wcols(l, O_CG, 512)
        for i in range(4):
            pa = self.cm_mm(wa, i * 128, gw)
            pg = self.cm_mm(wg, i * 128, gw)
            sg = self._stg()
            u = self._stg()
            self.op("act", lambda e, pg=pg, sg=sg: e.activation(out=sg.t[:, 0:gw], in_=pg.t[:, 0:gw], func=AF.Sigmoid), [pg], [sg])
            self.op("dve", lambda e, pa=pa, sg=sg, u=u: e.tensor_tensor(out=u.t[:, 0:gw], in0=pa.t[:, 0:gw], in1=sg.t[:, 0:gw], op=ALU.mult), [pa, sg], [u])
            self.dma("pool", self.s_ucv[i * 128:(i + 1) * 128, ts], u.t[:, 0:gw], reads=[u])
        wx = self.load_wcols(l, O_LX, 512)
        for i in range(4):
            pa = self.cm_mm(wx, i * 128, gw)
            u = self._stg()
            self.op("act", lambda e, pa=pa, u=u: e.copy(out=u.t[:, 0:gw], in_=pa.t[:, 0:gw]), [pa], [u])
            self.dma("pool", self.s_lx[i * 128:(i + 1) * 128, ts], u.t[:, 0:gw], reads=[u])
        wx = self.load_wcols(l, O_LG, 512)
        for i in range(4):
            pa = self.cm_mm(wx, i * 128, gw)
            x = self._stg()
            u = self._stg()
            self.op("act", lambda e, pa=pa, x=x: e.copy(out=x.t[:, 0:gw], in_=pa.t[:, 0:gw]), [pa], [x])
            self.gelu_tanh(x, u, gw)
            self.dma("pool", self.s_lg[i * 128:(i + 1) * 128, ts], u.t[:, 0:gw], reads=[u])
        for blk in range(3):
            wx = self.load_wcols(l, O_GQKV + blk * 512, 512)
            for i in range(4):
                pa = self.cm_mm(wx, i * 128, gw)
                u = self._stg()
                self.op("act", lambda e, pa=pa, u=u: e.copy(out=u.t[:, 0:gw], in_=pa.t[:, 0:gw]), [pa], [u])
                c = blk * 4 + i
                self.dma("pool", self.s_gqkv[c * 128:(c + 1) * 128, ts], u.t[:, 0:gw], reads=[u])
        wz = self.load_wcols(l, O_GZ, 512)
        wab = self.load_wcols(l, O_GA, 16)
        wv = self.load_wcols(l, O_AV, 128)
        for t in range(nt):
            rows = slice(tok0 + t * 128, tok0 + (t + 1) * 128)
            acc = self._acc()
            for k in range(8):
                self.mm(acc.t[:, 0:512], self.hT.t[:, k, t * 128:(t + 1) * 128], wz.t[:, k, 0:512], k == 0, k == 7, [wz, self.hT], [acc])
            u = self._stg()
            self.op("act", lambda e, acc=acc, u=u: e.activation(out=u.t[:], in_=acc.t[:], func=AF.Silu), [acc], [u])
            self.dma("pool", self.s_gz[rows, :], u.t[:], reads=[u])
            acc = self._acc()
            for k in range(8):
                self.mm(acc.t[:, 0:16], self.hT.t[:, k, t * 128:(t + 1) * 128], wab.t[:, k, 0:16], k == 0, k == 7, [wab, self.hT], [acc])
            for k in range(8):
                self.mm(acc.t[:, 128:256], self.hT.t[:, k, t * 128:(t + 1) * 128], wv.t[:, k, 0:128], k == 0, k == 7, [wv, self.hT], [acc])
            u = self._stg()
            ub = self.stgb[t % 2]
            self.op("act", lambda e, acc=acc, u=u: e.copy(out=u.t[:, 0:16], in_=acc.t[:, 0:16]), [acc], [u])
            self.op("dve", lambda e, acc=acc, ub=ub: e.tensor_copy(out=ub.t[:, 0:128], in_=acc.t[:, 128:256]), [acc], [ub])
            self.dma("pool", self.s_ab[rows, :], u.t[:, 0:16], reads=[u])
            self.dma("pool", self.s_v[rows, :], ub.t[:, 0:128], reads=[ub])
        if r == 0:
            self.dma("sp", self.rc.t[:, 0:gw], self.ropec_in[:, tok0 - NCTX:tok0 - NCTX + gw], writes=[self.rc])
            self.dma("sp", self.rs.t[:, 0:gw], self.ropes_in[:, tok0 - NCTX:tok0 - NCTX + gw], writes=[self.rs])
        wq = self.load_wcols(l, O_AQ, 512)
        wqs = self.load_wcols(l, O_AQS, 512) if r == 0 else None
        wk = self.load_wcols(l, O_AK, 128)
        wks = self.load_wcols(l, O_AKS, 128) if r == 0 else None
        for i in range(5):
            w1_, w2_, off = (wq, wqs, i * 128) if i < 4 else (wk, wks, 0)
            scale = 0.125 if i < 4 else 1.0
            pa = self.cm_mm(w1_, off, gw)
            ub = self.stgb[i % 2]
            if r == 0:
                pb_ = self.cm_mm(w2_, off, gw)
                a = self._stg()
                b = self._stg()
                self.op("dve", lambda e, pa=pa, a=a: e.tensor_tensor(out=a.t[:, 0:gw], in0=pa.t[:, 0:gw], in1=self.rc.t[:, 0:gw], op=ALU.mult), [pa, self.rc], [a])
                self.op("dve", lambda e, pb_=pb_, b=b: e.tensor_tensor(out=b.t[:, 0:gw], in0=pb_.t[:, 0:gw], in1=self.rs.t[:, 0:gw], op=ALU.mult), [pb_, self.rs], [b])
                self.op("dve", lambda e, a=a, b=b: e.tensor_tensor(out=a.t[:, 0:gw], in0=a.t[:, 0:gw], in1=b.t[:, 0:gw], op=ALU.add), [a, b], [a])
                self.op("act", lambda e, a=a, ub=ub, scale=scale: e.mul(out=ub.t[:, 0:gw], in_=a.t[:, 0:gw], mul=scale), [a], [ub])
            else:
                self.op("act", lambda e, pa=pa, ub=ub, scale=scale: e.mul(out=ub.t[:, 0:gw], in_=pa.t[:, 0:gw], mul=scale), [pa], [ub])
            if i < 4:
                self.dma("pool", self.s_q[i * 128:(i + 1) * 128, ts], ub.t[:, 0:gw], reads=[ub])
            else:
                self.dma("pool", self.s_k[:, ts], ub.t[:, 0:gw], reads=[ub])

    def col_load(self, dst, src_row_ap, n):
        self.dma("sp", dst, src_row_ap.rearrange("o (k p) -> p (o k)", p=128), writes=[], allow_slow_non_contiguous=True)

    def seq_segments(self):
        segs = [(0, NCTX, 0, NCTX)]
        for i in range(self.TL // 512):
            segs.append((NCTX + 512 * i, 512, NCTX, self.TT))
        return segs

    def gen_conv(self, l):
        w = self.w
        cw = self.sb("cv_w", (128, 4, 31), F32)
        cb = self.sb("cv_b", (128, 4), F32)
        cg = self.sb("cv_g", (128, 4), F32)
        cbt = self.sb("cv_bt", (128, 4), F32)
        wB = Buf("cvw")
        for c in range(4):
            self.P.dma("sp", cw.t[:, c, :], w["conv_w"][l, :, c * 128:(c + 1) * 128].rearrange("k p -> p k"), [], [cw.b], allow_slow_non_contiguous=True)
        for dst, name in ((cb, "conv_b"), (cg, "conv_norm_g"), (cbt, "conv_norm_b")):
            self.P.dma("sp", dst.t[:], w[name][l:l + 1, :].rearrange("o (k p) -> p (o k)", p=128), [], [dst.b], allow_slow_non_contiguous=True)
        uin = [self.sb("cv_u%d" % i, (128, 542), F32) for i in range(2)]
        Dg = self.sb("cv_Dg", (128, 4, 31, 128), F32)
        for c in range(4):
            idb = AP(self.ident.t[:].tensor, self.ident.t[:].offset, [list(self.ident.t[:].ap[0]), [0, 31], list(self.ident.t[:].ap[1])])
            wv = cw.t[:, c, :]
            wbc = AP(wv.tensor, wv.offset, [list(wv.ap[0]), list(wv.ap[1]), [0, 128]])
            self.op("pool", lambda e, c=c, idb=idb, wbc=wbc: e.tensor_tensor(out=Dg.t[:, c], in0=idb, in1=wbc, op=ALU.mult), [self.ident, cw], [Dg])
        acc = [self.sb("cv_acc%d" % i, (128, 512), F32) for i in range(4)]
        sq = self.sb("cv_sq", (128, 512), F32)
        mean = self.sb("cv_mean", (128, 512), F32)
        rstd = self.sb("cv_rstd", (128, 512), F32)
        yb = [self.sb("cv_y%d" % i, (128, 512), BF16) for i in range(2)]
        kbox = [0]

        def chunk(t0, n, s0, s1):
            p1, p2 = self.pbank[0], self.pbank[1]
            for c in range(4):
                u = uin[kbox[0] % 2]
                kbox[0] += 1
                lo, hi = max(s0, t0 - 15), min(s1, t0 + n + 15)
                if lo > t0 - 15 or hi < t0 + n + 15:
                    self.op("pool", lambda e, u=u: e.memset(u.t[:], 0.0), [], [u])
                self.dma("sp", u.t[:, lo - (t0 - 15):hi - (t0 - 15)], self.s_ucv[c * 128:(c + 1) * 128, lo:hi], writes=[u])
                a = acc[c]
                p3 = self.pbank[7]
                for kk in range(31):
                    self.mm(p3.t[:, 0:n], Dg.t[:, c, kk, :], u.t[:, kk:kk + n], kk == 0, kk == 30, [Dg, u], [p3])
                self.op("act", lambda e, a=a, p3=p3, c=c: e.activation(out=a.t[:, 0:n], in_=p3.t[:, 0:n], func=AF.Identity, bias=cb.t[:, c:c + 1], scale=1.0), [p3, cb], [a])
                self.op("act", lambda e, a=a: e.activation(out=sq.t[:, 0:n], in_=a.t[:, 0:n], func=AF.Square), [a], [sq])
                self.mm(p1.t[:, 0:n], self.ones.t[:], a.t[:, 0:n], c == 0, c == 3, [self.ones, a], [p1])
                self.mm(p2.t[:, 0:n], self.ones.t[:], sq.t[:, 0:n], c == 0, c == 3, [self.ones, sq], [p2])
            self.op("act", lambda e: e.mul(out=mean.t[:, 0:n], in_=p1.t[:, 0:n], mul=1.0 / 512), [p1], [mean])
            self.op("dve", lambda e: e.tensor_tensor(out=sq.t[:, 0:n], in0=mean.t[:, 0:n], in1=mean.t[:, 0:n], op=ALU.mult), [mean], [sq])
            self.op("dve", lambda e: e.scalar_tensor_tensor(out=rstd.t[:, 0:n], in0=p2.t[:, 0:n], scalar=1.0 / 512, in1=sq.t[:, 0:n], op0=ALU.mult, op1=ALU.subtract),
                    [p2, sq], [rstd])
            self.op("act", lambda e: e.activation(out=rstd.t[:, 0:n], in_=rstd.t[:, 0:n], func=AF.Sqrt, bias=self.eps_t.t[:], scale=1.0), [rstd, self.eps_t], [rstd])
            self.op("dve", lambda e: e.reciprocal(out=rstd.t[:, 0:n], in_=rstd.t[:, 0:n]), [rstd], [rstd])
            for c in range(4):
                a = acc[c]
                y = yb[c % 2]
                self.op("dve", lambda e, a=a: e.tensor_tensor(out=a.t[:, 0:n], in0=a.t[:, 0:n], in1=mean.t[:, 0:n], op=ALU.subtract), [a, mean], [a])
                self.op("dve", lambda e, a=a: e.tensor_tensor(out=a.t[:, 0:n], in0=a.t[:, 0:n], in1=rstd.t[:, 0:n], op=ALU.mult), [a, rstd], [a])
                self.op("act", lambda e, a=a, y=y, c=c: e.activation(out=y.t[:, 0:n], in_=a.t[:, 0:n], func=AF.Silu, scale=cg.t[:, c:c + 1], bias=cbt.t[:, c:c + 1]),
                        [a, cg, cbt], [y])
                self.dma("pool", self.s_y[0][c * 128:(c + 1) * 128, t0:t0 + n], y.t[:, 0:n], reads=[y])
        for seg in self.seq_segments():
            chunk(*seg)
            yield

    def gen_lru(self, l):
        w = self.w
        cw = self.sb("lr_cw", (128, 2, 4, 4), F32)
        prm = self.sb("lr_prm", (128, 5, 2, 4), F32)
        wr = self.sb("lr_wr", (128, 2, 4, 128), F32)
        wi = self.sb("lr_wi", (128, 2, 4, 128), F32)
        self.op("pool", lambda e: e.memset(wr.t[:], 0.0), [], [wr])
        self.op("pool", lambda e: e.memset(wi.t[:], 0.0), [], [wi])
        for d in range(2):
            for c in range(4):
                self.P.dma("sp", cw.t[:, d, c, :], w["lru_conv_w"][l, d, :, c * 128:(c + 1) * 128].rearrange("k p -> p k"), [], [cw.b], allow_slow_non_contiguous=True)
                for nb in range(2):
                    blk = slice(nb * 64, nb * 64 + 64)
                    self.dma("sp", wr.t[blk, d, c, nb * 64:nb * 64 + 64], w["lru_w_r"][l, d, 2 * c + nb], writes=[wr])
                    self.dma("sp", wi.t[blk, d, c, nb * 64:nb * 64 + 64], w["lru_w_i"][l, d, 2 * c + nb], writes=[wi])
            for i, name in enumerate(("lru_conv_b", "lru_b_r", "lru_b_i", "lru_lam")):
                self.P.dma("sp", prm.t[:, i, d, :], w[name][l, d:d + 1, :].rearrange("o (k p) -> p (o k)", p=128), [], [prm.b], allow_slow_non_contiguous=True)
        self.op("act", lambda e: e.activation(out=prm.t[:, 4], in_=prm.t[:, 3], func=AF.Exp, scale=-1.0), [prm], [prm])
        self.op("act", lambda e: e.activation(out=prm.t[:, 4], in_=prm.t[:, 4], func=AF.Ln, bias=1.0, scale=1.0), [prm], [prm])
        self.op("act", lambda e: e.mul(out=prm.t[:, 4], in_=prm.t[:, 4], mul=-LRU_C), [prm], [prm])
        xin = [self.sb("lr_x%d" % i, (128, 515), F32) for i in range(2)]
        xc = [self.sb("lr_xc%d" % i, (128, 512), F32) for i in range(2)]
        rg = [self.sb("lr_r%d" % i, (128, 512), F32) for i in range(2)]
        ig = [self.sb("lr_i%d" % i, (128, 512), F32) for i in range(2)]
        av = [self.sb("lr_a%d" % i, (128, 512), F32) for i in range(2)]
        hv = [self.sb("lr_h%d" % i, (128, 512), F32) for i in range(2)]
        hf = [self.sb("lr_hf%d" % i, (128, 512), F32) for i in range(2)]
        gl = [self.sb("lr_gl%d" % i, (128, 512), F32) for i in range(2)]
        yb = [self.sb("lr_y%d" % i, (128, 512), BF16) for i in range(2)]
        state = self.sb("lr_state", (128, 2, 4), F32)
        self.op("pool", lambda e: e.memset(state.t[:], 0.0), [], [state])
        hfB = {}
        segs = self.seq_segments()
        k = 0

        def rev(ap, n):
            full = ap[:, 0:n]
            return AP(full.tensor, full.offset + (n - 1), [[full.ap[0][0], 128], [-1, n]])

        def step(d, t0, n, s0, s1, c, i):
            x, xcv, r_, i_, a_, h_ = xin[i], xc[i], rg[i], ig[i], av[i], hv[i]
            lo, hi = (max(s0, t0 - 3), t0 + n) if d == 0 else (t0, min(s1, t0 + n + 3))
            base = t0 - 3 if d == 0 else t0
            if hi - lo < n + 3:
                self.op("pool", lambda e, x=x: e.memset(x.t[:], 0.0), [], [x])
            self.dma("sp", x.t[:, lo - base:hi - base], self.s_lx[c * 128:(c + 1) * 128, lo:hi], writes=[x])
            for kk in range(4):
                off = kk if d == 0 else 3 - kk
                if kk == 0:
                    self.op("dve", lambda e, x=x, xcv=xcv, off=off, c=c, d=d: e.tensor_scalar(out=xcv.t[:, 0:n], in0=x.t[:, off:off + n], scalar1=cw.t[:, d, c, 0:1],
                                                                                  scalar2=prm.t[:, 0, d, c:c + 1], op0=ALU.mult, op1=ALU.add), [x, cw, prm], [xcv])
                else:
                    self.op("dve", lambda e, x=x, xcv=xcv, off=off, c=c, d=d, kk=kk: e.scalar_tensor_tensor(out=xcv.t[:, 0:n], in0=x.t[:, off:off + n],
                            scalar=cw.t[:, d, c, kk:kk + 1], in1=xcv.t[:, 0:n], op0=ALU.mult, op1=ALU.add), [x, cw, xcv], [xcv])
            pr, pi = self.pbank[2], self.pbank[3]
            self.mm(pr.t[:, 0:n], wr.t[:, d, c, :], xcv.t[:, 0:n], True, True, [wr, xcv], [pr])
            self.mm(pi.t[:, 0:n], wi.t[:, d, c, :], xcv.t[:, 0:n], True, True, [wi, xcv], [pi])
            self.op("act", lambda e, pr=pr, r_=r_, c=c, d=d: e.activation(out=r_.t[:, 0:n], in_=pr.t[:, 0:n], func=AF.Sigmoid, bias=prm.t[:, 1, d, c:c + 1], scale=1.0), [pr, prm], [r_])
            self.op("act", lambda e, pi=pi, i_=i_, c=c, d=d: e.activation(out=i_.t[:, 0:n], in_=pi.t[:, 0:n], func=AF.Sigmoid, bias=prm.t[:, 2, d, c:c + 1], scale=1.0), [pi, prm], [i_])
            self.op("act", lambda e, r_=r_, a_=a_, c=c, d=d: e.activation(out=a_.t[:, 0:n], in_=r_.t[:, 0:n], func=AF.Exp, scale=prm.t[:, 4, d, c:c + 1]), [r_, prm], [a_])
            self.op("dve", lambda e, a_=a_, r_=r_: e.tensor_tensor(out=r_.t[:, 0:n], in0=a_.t[:, 0:n], in1=a_.t[:, 0:n], op=ALU.mult), [a_], [r_])
            self.op("act", lambda e, r_=r_: e.activation(out=r_.t[:, 0:n], in_=r_.t[:, 0:n], func=AF.Sqrt, bias=1.0, scale=-1.0), [r_], [r_])
            self.op("dve", lambda e, i_=i_, xcv=xcv: e.tensor_tensor(out=i_.t[:, 0:n], in0=i_.t[:, 0:n], in1=xcv.t[:, 0:n], op=ALU.mult), [i_, xcv], [i_])
            self.op("dve", lambda e, i_=i_, r_=r_: e.tensor_tensor(out=i_.t[:, 0:n], in0=i_.t[:, 0:n], in1=r_.t[:, 0:n], op=ALU.mult), [i_, r_], [i_])
            st = state.t[:, d, c:c + 1]
            if d == 0:
                self.op("dve", lambda e, h_=h_, a_=a_, i_=i_, st=st: e.tensor_tensor_scan(out=h_.t[:, 0:n], data0=a_.t[:, 0:n], data1=i_.t[:, 0:n], initial=st,
                                                                                  op0=ALU.mult, op1=ALU.add), [a_, i_, state], [h_])
                self.op("act", lambda e, h_=h_, st=st: e.copy(out=st, in_=h_.t[:, n - 1:n]), [h_], [state])
                B = Buf("hf")
                hfB[(c, t0)] = B
                self.P.dma(self.store_q, self.s_hf[c * 128:(c + 1) * 128, t0:t0 + n], h_.t[:, 0:n], [h_.b], [B])
            else:
                self.op("dve", lambda e, h_=h_, a_=a_, i_=i_, st=st: e.tensor_tensor_scan(out=rev(h_.t, n), data0=rev(a_.t, n), data1=rev(i_.t, n), initial=st,
                                                                                  op0=ALU.mult, op1=ALU.add), [a_, i_, state], [h_])
                self.op("act", lambda e, h_=h_, st=st: e.copy(out=st, in_=h_.t[:, 0:1]), [h_], [state])
                f_, g_, y = hf[i], gl[i], yb[i]
                self.P.dma("sp", f_.t[:, 0:n], self.s_hf[c * 128:(c + 1) * 128, t0:t0 + n], [hfB[(c, t0)]], [f_.b])
                self.dma("sp", g_.t[:, 0:n], self.s_lg[c * 128:(c + 1) * 128, t0:t0 + n], writes=[g_])
                self.op("dve", lambda e, f_=f_, h_=h_: e.tensor_tensor(out=f_.t[:, 0:n], in0=f_.t[:, 0:n], in1=h_.t[:, 0:n], op=ALU.add), [f_, h_], [f_])
                self.op("dve", lambda e, f_=f_, g_=g_, y=y: e.tensor_tensor(out=y.t[:, 0:n], in0=f_.t[:, 0:n], in1=g_.t[:, 0:n], op=ALU.mult), [f_, g_], [y])
                self.dma("pool", self.s_y[1][c * 128:(c + 1) * 128, t0:t0 + n], y.t[:, 0:n], reads=[y])
        for d in range(2):
            order = segs if d == 0 else [segs[0]] + segs[:0:-1]
            for (t0, n, s0, s1) in order:
                for c in range(4):
                    step(d, t0, n, s0, s1, c, k % 2)
                    k += 1
                    yield


    def gen_att(self, l, ctx_out):
        w = self.w
        bm = self.sb("at_bm", (128, 384), F32)
        self.dma("sp", bm.t[:], self.attmask_in, writes=[bm])
        srow = self.sb("at_srow", (1, 8), F32)
        sinkB = self.sb("at_sink", (128, 8), F32)
        self.dma("sp", srow.t[:], w["att_sink"][l:l + 1, :], writes=[srow])
        pb = self.pbank
        self.mm(pb[4].t[:, 0:8], self.ones.t[0:1, :], srow.t[0:1, :], True, True, [self.ones, srow], [pb[4]])
        self.op("act", lambda e: e.copy(out=sinkB.t[:], in_=pb[4].t[:, 0:8]), [pb[4]], [sinkB])
        qT = [self.sb("at_q%d" % i, (64, 128), BF16) for i in range(4)]
        kT = [self.sb("at_k%d" % i, (64, 640), BF16) for i in range(2)]
        vv = [self.sb("at_v%d" % i, (128, 5, 64), BF16) for i in range(2)]
        sc = [self.sb("at_sc%d" % i, (128, 640), F32) for i in range(2)]
        pp = [self.sb("at_p%d" % i, (128, 640), BF16) for i in range(2)]
        pT = [self.sb("at_pT%d" % i, (128, 5, 128), BF16) for i in range(2)]
        st = [self.sb("at_st%d" % i, (128, 8), F32) for i in range(2)]
        otok = [self.sb("at_o%d" % i, (128, 512), BF16) for i in range(2)]
        oT = [self.sb("at_oT%d" % i, (128, 512), BF16) for i in range(2)]
        nblk = self.TL // 128
        blocks = ([("c", 0), ("c", 1)] if ctx_out else []) + [("l", i) for i in range(nblk)]
        kq = 0
        kk = 0
        for bi, (kind, i) in enumerate(blocks):
            q0 = i * 128 if kind == "c" else NCTX + i * 128
            if kind == "c":
                lat = []
            else:
                lat = [b for b in (i - 1, i, i + 1) if 0 <= b < nblk]
            nl = len(lat) * 128
            nk = 256 + nl
            nch = nk // 128
            ot = otok[bi % 2]
            for g in range(2):
                kt = kT[kk % 2]
                vt = vv[kk % 2]
                kk += 1
                self.dma("sp", kt.t[:, 0:256], self.s_k[g * 64:(g + 1) * 64, 0:256], writes=[kt])
                self.dma("sp", vt.t[:, 0:2, :], self.s_v[0:256, g * 64:(g + 1) * 64].rearrange("(c p) d -> p c d", p=128), writes=[vt])
                if lat:
                    l0 = NCTX + lat[0] * 128
                    self.dma("sp", kt.t[:, 256:256 + nl], self.s_k[g * 64:(g + 1) * 64, l0:l0 + nl], writes=[kt])
                    self.dma("sp", vt.t[:, 2:2 + len(lat), :], self.s_v[l0:l0 + nl, g * 64:(g + 1) * 64].rearrange("(c p) d -> p c d", p=128), writes=[vt])
                for h in range(4):
                    hq = g * 4 + h
                    j = kq % 2
                    q = qT[kq % 4]
                    kq += 1
                    s_, p_, pt_, st_ = sc[j], pp[j], pT[j], st[j]
                    self.dma("sp", q.t[:], self.s_q[hq * 64:(hq + 1) * 64, q0:q0 + 128], writes=[q])
                    pc, pl, ptr, po = pb[4], pb[5], pb[6], pb[4]
                    self.mm(pc.t[:, 0:256], q.t[:], kt.t[:, 0:256], True, True, [q, kt], [pc])
                    self.op("act", lambda e, s_=s_, pc=pc: e.copy(out=s_.t[:, 0:256], in_=pc.t[:, 0:256]), [pc], [s_])
                    if lat:
                        self.mm(pl.t[:, 0:nl], q.t[:], kt.t[:, 256:256 + nl], True, True, [q, kt], [pl])
                        m0 = (lat[0] - (i - 1)) * 128
                        self.op("dve", lambda e, s_=s_, pl=pl, m0=m0, nl=nl: e.tensor_tensor(out=s_.t[:, 256:256 + nl], in0=pl.t[:, 0:nl], in1=bm.t[:, m0:m0 + nl], op=ALU.add),
                                [pl, bm], [s_])
                    self.op("dve", lambda e, s_=s_, st_=st_, nk=nk: e.reduce_max(out=st_.t[:, 0:1], in_=s_.t[:, 0:nk], axis=mybir.AxisListType.X), [s_], [st_])
                    self.op("dve", lambda e, st_=st_, hq=hq: e.tensor_tensor(out=st_.t[:, 0:1], in0=st_.t[:, 0:1], in1=sinkB.t[:, hq:hq + 1], op=ALU.max), [st_, sinkB], [st_])
                    self.op("dve", lambda e, st_=st_: e.tensor_scalar_mul(out=st_.t[:, 1:2], in0=st_.t[:, 0:1], scalar1=-1.0), [st_], [st_])
                    self.op("act", lambda e, s_=s_, p_=p_, st_=st_, nk=nk: e.activation(out=p_.t[:, 0:nk], in_=s_.t[:, 0:nk], func=AF.Exp, bias=st_.t[:, 1:2], scale=1.0,
                                                                                accum_out=st_.t[:, 2:3]), [s_, st_], [p_, st_])
                    self.op("act", lambda e, st_=st_, hq=hq: e.activation(out=st_.t[:, 3:4], in_=sinkB.t[:, hq:hq + 1], func=AF.Exp, bias=st_.t[:, 1:2], scale=1.0), [st_, sinkB], [st_])
                    self.op("dve", lambda e, st_=st_: e.tensor_tensor(out=st_.t[:, 4:5], in0=st_.t[:, 2:3], in1=st_.t[:, 3:4], op=ALU.add), [st_], [st_])
                    self.op("dve", lambda e, st_=st_: e.reciprocal(out=st_.t[:, 5:6], in_=st_.t[:, 4:5]), [st_], [st_])
                    ptb = ptr.t[:].bitcast(BF16)
                    for c in range(nch):
                        self.op("pe", lambda e, c=c, p_=p_, ptb=ptb: e.transpose(ptb[:, c * 128:(c + 1) * 128], p_.t[:, c * 128:(c + 1) * 128], self.identb.t[:]),
                                [p_, self.identb], [ptr])
                    self.op("act", lambda e, pt_=pt_, ptb=ptb, nk=nk: e.copy(out=pt_.t[:].rearrange("p a b -> p (a b)")[:, 0:nk], in_=ptb[:, 0:nk]), [ptr], [pt_])
                    for c in range(nch):
                        self.mm(po.t[:, 256:320], pt_.t[:, c, :], vt.t[:, c, :], c == 0, c == nch - 1, [pt_, vt], [po])
                    self.op("dve", lambda e, ot=ot, po=po, st_=st_, hq=hq: e.tensor_scalar_mul(out=ot.t[:, hq * 64:(hq + 1) * 64], in0=po.t[:, 256:320], scalar1=st_.t[:, 5:6]),
                            [po, st_], [ot])
                    yield
            ptr = pb[6]
            ptb = ptr.t[:].bitcast(BF16)
            ot_T = oT[bi % 2]
            for c in range(4):
                self.op("pe", lambda e, c=c, ot=ot, ptb=ptb: e.transpose(ptb[:, c * 128:(c + 1) * 128], ot.t[:, c * 128:(c + 1) * 128], self.identb.t[:]), [ot, self.identb], [ptr])
            self.op("act", lambda e, ot_T=ot_T, ptb=ptb: e.copy(out=ot_T.t[:], in_=ptb[:, 0:512]), [ptr], [ot_T])
            self.dma("pool", self.s_y[3][:, q0:q0 + 128].rearrange("(c p) t -> p c t", p=128), ot_T.t[:].rearrange("p (c t) -> p c t", c=4), reads=[ot_T])

    def mixers_cla(self, l, ctx_out):
        mk = self.mark()
        self.store_q = "pool"
        gens = [self.gen_conv(l), self.gen_lru(l), self.gen_att(l, ctx_out)]
        weights = [1, 8, 31]
        alive = [True, True, True]
        while any(alive):
            for gi in range(3):
                for _ in range(weights[gi]):
                    if not alive[gi]:
                        break
                    try:
                        next(gens[gi])
                    except StopIteration:
                        alive[gi] = False
        self.release(mk)

    def alias(self, parent, name, shape, dt=F32, off_bytes=0):
        v = parent.t
        flat = v
        base = self._tile_off[id(parent)] + off_bytes
        n = 1
        for d in shape[1:]:
            n *= d
        nbytes = n * (2 if dt == BF16 else 4)
        a = self.arena[0:shape[0], base // 4:(base + nbytes) // 4]
        if dt == BF16:
            a = a.bitcast(BF16)
        if len(shape) == 3:
            a = a.rearrange("p (a b) -> p a b", a=shape[1])
        t = T(a, name)
        t.b = parent.b
        self._tile_off[id(t)] = base
        return t

    @staticmethod
    def bc_inner(ap2, n):
        return AP(ap2.tensor, ap2.offset, [list(ap2.ap[0]), list(ap2.ap[1]), [0, n]])

    @staticmethod
    def bc_inner3(ap3, n):
        return AP(ap3.tensor, ap3.offset, [list(ap3.ap[0]), list(ap3.ap[1]), list(ap3.ap[2]), [0, n]])

    @staticmethod
    def bc_mid(ap2, h):
        return AP(ap2.tensor, ap2.offset, [list(ap2.ap[0]), [0, h], list(ap2.ap[1])])

    def mixer_gdn(self, l):
        w = self.w
        mk = self.mark()
        PQ = self.pquad
        gm = self.sb("gd_mask", (128, 8, 128), F32)
        self.dma("sp", gm.t[:], self.gdnmask_in.rearrange("p (a b) -> p a b", a=8), writes=[gm])
        cw = self.sb("gd_cw", (128, 2, 4, 12), F32)
        for d in range(2):
            for c in range(12):
                self.P.dma("sp", cw.t[:, d, :, c], w["gdn_conv_w"][l, d, :, c * 128:(c + 1) * 128].rearrange("k p -> p k"), [], [cw.b], allow_slow_non_contiguous=True)
        prow = self.sb("gd_prow", (1, 16 + 128), F32)
        self.dma("sp", prow.t[:, 0:8], w["gdn_a_log"][l:l + 1].rearrange("o d h -> o (d h)"), writes=[prow])
        self.dma("sp", prow.t[:, 8:16], w["gdn_dt_bias"][l:l + 1].rearrange("o d h -> o (d h)"), writes=[prow])
        self.dma("sp", prow.t[:, 16:144], w["gdn_norm_g"][l:l + 1, :], writes=[prow])
        pbc = self.sb("gd_pbc", (128, 144), F32)
        self.mm(PQ[0].t[:, 0:144], self.ones.t[0:1, :], prow.t[0:1, :], True, True, [self.ones, prow], [PQ[0]])
        self.op("act", lambda e: e.copy(out=pbc.t[:], in_=PQ[0].t[:, 0:144]), [PQ[0]], [pbc])
        self.op("act", lambda e: e.activation(out=pbc.t[:, 0:8], in_=pbc.t[:, 0:8], func=AF.Exp), [pbc], [pbc])
        self.op("act", lambda e: e.mul(out=pbc.t[:, 0:8], in_=pbc.t[:, 0:8], mul=-1.0), [pbc], [pbc])
        eps6 = self.sb("gd_eps", (128, 1), F32)
        self.op("pool", lambda e: e.memset(eps6.t[:], NORM_EPS), [], [eps6])
        S = self.sb("gd_S", (128, 2, 4, 128), F32)
        self.op("pool", lambda e: e.memset(S.t[:], 0.0), [], [S])
        ntile = self.TT // 128
        tiles = list(range(ntile))
        ofB = {}
        bc_inner, bc_mid = self.bc_inner, self.bc_mid

        def chain(d, hp):
            n_ = "gd%d%d_" % (d, hp)
            Xin = self.sb(n_ + "Xin", (128, 6, 131), F32)
            Ya = self.alias(Xin, n_ + "Ya", (128, 2, 256))
            U = self.sb(n_ + "U", (128, 6, 128), F32)
            Yb = self.alias(U, n_ + "Yb", (128, 2, 256))
            T1 = self.sb(n_ + "T1", (128, 2, 128), F32)
            R1 = self.sb(n_ + "R1", (128, 1024), F32)
            tmpU = self.alias(R1, n_ + "tmpU", (128, 6, 128))
            EE = self.alias(R1, n_ + "EE", (128, 2, 256))
            GE = self.alias(R1, n_ + "GE", (128, 2, 256), off_bytes=2048)
            abs_ = [self.sb(n_ + "ab%d" % i, (128, 16), F32) for i in range(2)]
            scs = [self.sb(n_ + "sc%d" % i, (128, 8, 2), F32) for i in range(2)]
            KQ = self.sb(n_ + "KQ", (128, 2, 256), F32)
            so = self.sb(n_ + "so", (128, 2, 256), F32)
            sol = self.sb(n_ + "sol", (128, 2, 256), F32)
            Ks = self.sb(n_ + "Ks", (128, 2, 128), F32)
            gL = self.sb(n_ + "gL", (128, 2, 128), F32)
            XX = self.sb(n_ + "XX", (128, 2, 256), F32)
            aT = self.sb(n_ + "aT", (128, 2, 128), F32)
            PP = self.sb(n_ + "PP", (128, 2, 256), F32)
            Xo = self.sb(n_ + "Xo", (128, 3, 2, 128), F32)
            wT = self.sb(n_ + "wT", (128, 2, 128), F32)
            vn = self.sb(n_ + "vn", (128, 2, 128), F32)
            o2 = self.sb(n_ + "o2", (128, 2, 128), F32)
            otok = self.sb(n_ + "ot", (128, 2, 128), F32)
            QA, QB = self.pbank[4 * d + 2 * hp], self.pbank[4 * d + 2 * hp + 1]
            qa4 = QA.t[:].rearrange("p (h c) -> p h c", h=2)
            qb4 = QB.t[:].rearrange("p (h c) -> p h c", h=2)
            qa8 = QA.t[:].rearrange("p (h c) -> p h c", h=4)
            SL, IU = gm.t[:, 2 * d, :], gm.t[:, 2 * d + 1, :]
            SL_b, IU_b = bc_mid(SL, 2), bc_mid(IU, 2)
            D16_b = bc_mid(gm.t[:, 4, :], 4)
            I_b = bc_mid(self.ident.t[:], 4)
            cw5 = cw.t[:].rearrange("p d k (s h) -> p d k s h", s=3)
            cwv = lambda kk: cw5[:, d, kk, :, 2 * hp:2 * hp + 2]
            order = tiles if d == 0 else [1, 0] + tiles[:1:-1]
            def s0(tl, sc, ab):
                t0 = tl * 128
                s0_, s1_ = (0, NCTX) if t0 < NCTX else (NCTX, self.TT)
                lo, hi = (max(s0_, t0 - 3), t0 + 128) if d == 0 else (t0, min(s1_, t0 + 131))
                base = t0 - 3 if d == 0 else t0

                def p0():
                    if hi - lo < 131:
                        self.op("pool", lambda e: e.memset(Xin.t[:], 0.0), [], [Xin])
                    for s3 in range(3):
                        r0 = (s3 * 4 + 2 * hp) * 128
                        self.dma("sp", Xin.t[:, 2 * s3:2 * s3 + 2, lo - base:hi - base], self.s_gqkv[r0:r0 + 256, lo:hi].rearrange("(c p) t -> p c t", p=128), writes=[Xin])
                    self.dma("sp", ab.t[:], self.s_ab[t0:t0 + 128, :], writes=[ab])
                    for kk in range(4):
                        off = kk if d == 0 else 3 - kk
                        wb_ = self.bc_inner3(cwv(kk), 128)
                        src = Xin.t[:, :, off:off + 128].rearrange("p (s h) t -> p s h t", s=3)
                        if kk == 0:
                            self.op("dve", lambda e, src=src, wb_=wb_: e.tensor_tensor(out=U.t[:].rearrange("p (s h) t -> p s h t", s=3), in0=src, in1=wb_, op=ALU.mult), [Xin, cw], [U])
                        else:
                            self.op("pool", lambda e, src=src, wb_=wb_: e.tensor_tensor(out=tmpU.t[:].rearrange("p (s h) t -> p s h t", s=3), in0=src, in1=wb_, op=ALU.mult), [Xin, cw], [tmpU])
                            self.op("dve", lambda e: e.tensor_tensor(out=U.t[:], in0=U.t[:], in1=tmpU.t[:], op=ALU.add), [U, tmpU], [U])

                def p1():
                    self.op("act", lambda e: e.activation(out=U.t[:], in_=U.t[:], func=AF.Silu), [U], [U])
                    self.op("act", lambda e: e.activation(out=tmpU.t[:, 0:4, :], in_=U.t[:, 0:4, :], func=AF.Square), [U], [tmpU])
                    for c in range(4):
                        self.mm(qa8[:, c, :], self.ones.t[:], tmpU.t[:, c, :], True, True, [self.ones, tmpU], [QA])
                    self.op("act", lambda e: e.activation(out=tmpU.t[:, 0:4, :], in_=qa8, func=AF.Sqrt, bias=eps6.t[:], scale=1.0), [QA, eps6], [tmpU])

                def p2():
                    self.op("dve", lambda e: e.reciprocal(out=tmpU.t[:, 0:4, :], in_=tmpU.t[:, 0:4, :]), [tmpU], [tmpU])
                    self.op("dve", lambda e: e.tensor_tensor(out=U.t[:, 0:4, :], in0=U.t[:, 0:4, :], in1=tmpU.t[:, 0:4, :], op=ALU.mult), [U, tmpU], [U])
                    self.op("act", lambda e: e.mul(out=U.t[:, 0:2, :], in_=U.t[:, 0:2, :], mul=128.0 ** -0.5), [U], [U])

                def p3():
                    a4 = ab.t[:, 4 * d + 2 * hp:4 * d + 2 * hp + 2]
                    b4 = ab.t[:, 8 + 4 * d + 2 * hp:10 + 4 * d + 2 * hp]
                    self.op("act", lambda e, b4=b4: e.activation(out=sc.t[:, 0, :], in_=b4, func=AF.Sigmoid), [ab], [sc])
                    self.op("dve", lambda e, a4=a4: e.tensor_tensor(out=sc.t[:, 1, :], in0=a4, in1=pbc.t[:, 8 + 4 * d + 2 * hp:10 + 4 * d + 2 * hp], op=ALU.add), [ab, pbc], [sc])
                    self.op("act", lambda e: e.activation(out=sc.t[:, 1, :], in_=sc.t[:, 1, :], func=AF.Exp), [sc], [sc])
                    self.op("act", lambda e: e.activation(out=sc.t[:, 1, :], in_=sc.t[:, 1, :], func=AF.Ln, bias=1.0, scale=1.0), [sc], [sc])
                    self.op("dve", lambda e: e.tensor_tensor(out=sc.t[:, 1, :], in0=sc.t[:, 1, :], in1=pbc.t[:, 4 * d + 2 * hp:4 * d + 2 * hp + 2], op=ALU.mult), [sc, pbc], [sc])
                    self.mm(QB.t[:, 0:2], IU, sc.t[:, 1, :], True, True, [gm, sc], [QB])
                    self.mm(QB.t[:, 4:6], self.ones.t[:], sc.t[:, 1, :], True, True, [self.ones, sc], [QB])
                    self.op("act", lambda e: e.copy(out=sc.t[:, 2, :], in_=QB.t[:, 0:2]), [QB], [sc])
                    self.op("act", lambda e: e.activation(out=sc.t[:, 3, :], in_=QB.t[:, 0:2], func=AF.Exp), [QB], [sc])
                    self.op("act", lambda e: e.activation(out=sc.t[:, 7, :], in_=QB.t[:, 4:6], func=AF.Exp), [QB], [sc])
                    self.op("dve", lambda e: e.tensor_tensor(out=sc.t[:, 5, :], in0=QB.t[:, 4:6], in1=sc.t[:, 2, :], op=ALU.subtract), [QB, sc], [sc])
                    self.op("dve", lambda e: e.tensor_tensor(out=sc.t[:, 4, :], in0=sc.t[:, 0, :], in1=sc.t[:, 3, :], op=ALU.mult), [sc], [sc])
                    self.op("act", lambda e: e.activation(out=sc.t[:, 5, :], in_=sc.t[:, 5, :], func=AF.Exp), [sc], [sc])
                    self.op("dve", lambda e: e.tensor_scalar_mul(out=sc.t[:, 6, :], in0=sc.t[:, 0, :], scalar1=-1.0), [sc], [sc])
                return [p0, p1, p2, p3]

            nxt_pieces = s0(order[0], scs[0], abs_[0])
            for p_ in nxt_pieces:
                p_()
                yield
            for ti, tl in enumerate(order):
                t0 = tl * 128
                sc = scs[ti % 2]
                colb = lambda q, sc=sc: bc_inner(sc.t[:, q, :], 128)
                nxt_pieces = s0(order[ti + 1], scs[(ti + 1) % 2], abs_[(ti + 1) % 2]) if ti + 1 < len(order) else []
                self.op("act", lambda e: e.copy(out=KQ.t[:, :, 0:128], in_=U.t[:, 2:4, :]), [U], [KQ])
                self.op("pool", lambda e: e.tensor_copy(out=KQ.t[:, :, 128:256], in_=U.t[:, 0:2, :]), [U], [KQ])
                for h in range(2):
                    self.op("pe", lambda e, h=h: e.transpose(qa4[:, h, 0:128], U.t[:, 2 + h, :], self.ident.t[:]), [U, self.ident], [QA])
                    self.op("pe", lambda e, h=h: e.transpose(qa4[:, h, 128:256], U.t[:, 4 + h, :], self.ident.t[:]), [U, self.ident], [QA])
                self.op("dve", lambda e, colb=colb: e.tensor_tensor(out=so.t[:, :, 0:128], in0=qa4[:, :, 128:256], in1=colb(0), op=ALU.mult), [QA, sc], [so])
                self.op("dve", lambda e, colb=colb: e.tensor_tensor(out=so.t[:, :, 128:256], in0=qa4[:, :, 0:128], in1=colb(4), op=ALU.mult), [QA, sc], [so])
                self.op("dve", lambda e, colb=colb: e.tensor_tensor(out=Ks.t[:], in0=qa4[:, :, 0:128], in1=colb(5), op=ALU.mult), [QA, sc], [Ks])
                yield
                for h in range(2):
                    self.mm(qb4[:, h, :], KQ.t[:, h, 0:128], KQ.t[:, h, :], True, True, [KQ], [QB])
                self.op("dve", lambda e, colb=colb: e.tensor_tensor(out=gL.t[:], in0=SL_b, in1=colb(1), op=ALU.mult), [gm, sc], [gL])
                for h in range(2):
                    self.mm(qa4[:, h, 0:128], IU, gL.t[:, h, :], True, True, [gm, gL], [QA])
                    self.mm(qa4[:, h, 128:256], gL.t[:, h, :], IU, True, True, [gm, gL], [QA])
                self.op("act", lambda e: e.activation(out=EE.t[:], in_=qa4, func=AF.Exp), [QA], [EE])
                self.op("dve", lambda e: e.tensor_tensor(out=GE.t[:], in0=qb4, in1=EE.t[:], op=ALU.mult), [QB, EE], [GE])
                yield
                self.op("dve", lambda e, colb=colb: e.tensor_tensor(out=XX.t[:, :, 0:128], in0=GE.t[:, :, 0:128], in1=colb(6), op=ALU.mult), [GE, sc], [XX])
                self.op("dve", lambda e: e.tensor_tensor(out=XX.t[:, :, 0:128], in0=XX.t[:, :, 0:128], in1=SL_b, op=ALU.mult), [XX, gm], [XX])
                self.op("pool", lambda e: e.tensor_tensor(out=aT.t[:], in0=GE.t[:, :, 128:256], in1=IU_b, op=ALU.mult), [GE, gm], [aT])
                for h in range(2):
                    self.op("pe", lambda e, h=h: e.transpose(qb4[:, h, 0:128], XX.t[:, h, 0:128], self.ident.t[:]), [XX, self.ident], [QB])
                self.op("act", lambda e: e.copy(out=XX.t[:, :, 128:256], in_=qb4[:, :, 0:128]), [QB], [XX])
                xx8 = XX.t[:].rearrange("p h (a b) -> p (h a) b", a=2)
                ya8 = Ya.t[:].rearrange("p h (a b) -> p (h a) b", a=2)
                pp8 = PP.t[:].rearrange("p h (a b) -> p (h a) b", a=2)
                self.op("dve", lambda e, xx8=xx8, ya8=ya8: e.tensor_tensor(out=ya8, in0=xx8, in1=D16_b, op=ALU.mult), [XX, gm], [Ya])
                for q in range(3):
                    self.op("pool", lambda e, q=q: e.tensor_tensor(out=Xo.t[:, q], in0=XX.t[:, :, 128:256], in1=bc_mid(gm.t[:, 5 + q, :], 2), op=ALU.mult), [XX, gm], [Xo])
                self.op("dve", lambda e, ya8=ya8, pp8=pp8: e.tensor_tensor(out=pp8, in0=ya8, in1=I_b, op=ALU.add), [Ya, self.ident], [PP])
                yield
                cur, nxt = Ya, Yb
                for lev in range(3):
                    for h in range(2):
                        self.mm(qa4[:, h, 0:128], cur.t[:, h, 128:256], cur.t[:, h, 0:128], True, True, [cur], [QA])
                        self.mm(qa4[:, h, 128:256], cur.t[:, h, 0:128], cur.t[:, h, 128:256], True, True, [cur], [QA])
                    self.op("act", lambda e, nxt=nxt: e.copy(out=nxt.t[:], in_=qa4), [QA], [nxt])
                    cur, nxt = nxt, cur
                    for h in range(2):
                        self.mm(qb4[:, h, 0:128], cur.t[:, h, 128:256], PP.t[:, h, 0:128], True, True, [cur, PP], [QB])
                        self.mm(qb4[:, h, 128:256], PP.t[:, h, 0:128], cur.t[:, h, 128:256], True, True, [cur, PP], [QB])
                    self.op("dve", lambda e: e.tensor_tensor(out=PP.t[:], in0=PP.t[:], in1=qb4, op=ALU.add), [PP, QB], [PP])
                    yield
                if nxt_pieces:
                    nxt_pieces[0]()
                for q in range(3):
                    for h in range(2):
                        self.mm(qa4[:, h, 0:128], Xo.t[:, q, h, :], PP.t[:, h, 0:128], True, True, [Xo, PP], [QA])
                    self.op("act", lambda e: e.copy(out=T1.t[:], in_=qa4[:, :, 0:128]), [QA], [T1])
                    for h in range(2):
                        self.mm(qb4[:, h, 0:128], PP.t[:, h, 128:256], T1.t[:, h, :], True, True, [PP, T1], [QB])
                    self.op("dve", lambda e: e.tensor_tensor(out=PP.t[:, :, 0:128], in0=PP.t[:, :, 0:128], in1=qb4[:, :, 0:128], op=ALU.add), [PP, QB], [PP])
                    for h in range(2):
                        self.op("pe", lambda e, h=h: e.transpose(qa4[:, h, 128:256], PP.t[:, h, 0:128], self.ident.t[:]), [PP, self.ident], [QA])
                    self.op("act", lambda e: e.copy(out=PP.t[:, :, 128:256], in_=qa4[:, :, 128:256]), [QA], [PP])
                    if nxt_pieces:
                        nxt_pieces[1 + q]()
                    yield
                for h in range(2):
                    self.mm(qb4[:, h, :], PP.t[:, h, 128:256], so.t[:, h, :], True, True, [PP, so], [QB])
                self.op("act", lambda e: e.copy(out=sol.t[:], in_=qb4), [QB], [sol])
                for h in range(2):
                    self.op("pe", lambda e, h=h: e.transpose(qa4[:, h, 0:128], sol.t[:, h, 128:256], self.ident.t[:]), [sol, self.ident], [QA])
                self.op("act", lambda e: e.copy(out=wT.t[:], in_=qa4[:, :, 0:128]), [QA], [wT])
                yield
                Sd = S.t[:, d, 2 * hp:2 * hp + 2]
                for h in range(2):
                    self.mm(qb4[:, h, 0:128], wT.t[:, h, :], Sd[:, h, :], True, True, [wT, S], [QB])
                self.op("dve", lambda e: e.tensor_tensor(out=vn.t[:], in0=sol.t[:, :, 0:128], in1=qb4[:, :, 0:128], op=ALU.subtract), [sol, QB], [vn])
                for h in range(2):
                    self.mm(qa4[:, h, 0:128], KQ.t[:, h, 128:256], Sd[:, h, :], True, True, [KQ, S], [QA])
                    self.mm(qa4[:, h, 128:256], aT.t[:, h, :], vn.t[:, h, :], True, True, [aT, vn], [QA])
                self.op("act", lambda e: e.copy(out=o2.t[:], in_=qa4[:, :, 128:256]), [QA], [o2])
                self.op("dve", lambda e, colb=colb: e.tensor_tensor(out=otok.t[:], in0=qa4[:, :, 0:128], in1=colb(3), op=ALU.mult), [QA, sc], [otok])
                self.op("dve", lambda e: e.tensor_tensor(out=otok.t[:], in0=otok.t[:], in1=o2.t[:], op=ALU.add), [otok, o2], [otok])
                for h in range(2):
                    self.mm(qb4[:, h, 128:256], Ks.t[:, h, :], vn.t[:, h, :], True, True, [Ks, vn], [QB])
                self.op("pool", lambda e, colb=colb, Sd=Sd: e.tensor_tensor(out=Sd, in0=Sd, in1=colb(7), op=ALU.mult), [S, sc], [S])
                self.op("dve", lambda e, Sd=Sd: e.tensor_tensor(out=Sd, in0=Sd, in1=qb4[:, :, 128:256], op=ALU.add), [S, QB], [S])
                rows = slice(t0, t0 + 128)
                cs = slice(hp * 256, (hp + 1) * 256)
                otf = otok.t[:].rearrange("p h c -> p (h c)")
                B = Buf("o")
                ofB[(d, tl, hp)] = B
                self.P.dma("sp", (self.s_of if d == 0 else self.s_ob)[rows, cs], otf, [otok.b], [B])
                yield "T"

        e_of = self.sb("gde_of", (128, 512), F32)
        e_ob = self.sb("gde_ob", (128, 512), F32)
        e_gz = self.sb("gde_gz", (128, 512), F32)
        e_sq = self.sb("gde_sq", (128, 512), F32)
        e_ms = self.sb("gde_ms", (128, 8), F32)
        e_y = self.sb("gde_y", (128, 512), BF16)
        e_yT = self.sb("gde_yT", (128, 512), BF16)
        ebank = self.pbank[7]

        def epilogue(tl):
            t0 = tl * 128
            rows = slice(t0, t0 + 128)
            self.P.dma("sp", e_of.t[:], self.s_of[rows, :], [ofB[(0, tl, 0)], ofB[(0, tl, 1)]], [e_of.b])
            self.P.dma("sp", e_ob.t[:], self.s_ob[rows, :], [ofB[(1, tl, 0)], ofB[(1, tl, 1)]], [e_ob.b])
            self.dma("sp", e_gz.t[:], self.s_gz[rows, :], writes=[e_gz])
            self.op("dve", lambda e: e.tensor_tensor(out=e_of.t[:], in0=e_of.t[:], in1=e_ob.t[:], op=ALU.add), [e_of, e_ob], [e_of])
            for h in range(4):
                hs = slice(h * 128, (h + 1) * 128)
                self.op("act", lambda e, hs=hs, h=h: e.activation(out=e_sq.t[:, hs], in_=e_of.t[:, hs], func=AF.Square, accum_out=e_ms.t[:, h:h + 1]), [e_of], [e_sq, e_ms])
            self.op("act", lambda e: e.activation(out=e_ms.t[:, 4:8], in_=e_ms.t[:, 0:4], func=AF.Sqrt, bias=eps6.t[:], scale=1.0 / 128), [e_ms, eps6], [e_ms])
            self.op("dve", lambda e: e.reciprocal(out=e_ms.t[:, 4:8], in_=e_ms.t[:, 4:8]), [e_ms], [e_ms])
            of4 = e_of.t[:].rearrange("p (h c) -> p h c", h=4)
            self.op("dve", lambda e: e.tensor_tensor(out=of4, in0=of4, in1=bc_inner(e_ms.t[:, 4:8], 128), op=ALU.mult), [e_of, e_ms], [e_of])
            self.op("pool", lambda e: e.tensor_tensor(out=of4, in0=of4, in1=bc_mid(pbc.t[:, 16:144], 4), op=ALU.mult), [e_of, pbc], [e_of])
            self.op("dve", lambda e: e.tensor_tensor(out=e_y.t[:], in0=e_of.t[:], in1=e_gz.t[:], op=ALU.mult), [e_of, e_gz], [e_y])
            ptb = ebank.t[:, 0:256].bitcast(BF16)
            for c in range(4):
                self.op("pe", lambda e, c=c: e.transpose(ptb[:, c * 128:(c + 1) * 128], e_y.t[:, c * 128:(c + 1) * 128], self.identb.t[:]), [e_y, self.identb], [ebank])
            self.op("act", lambda e: e.copy(out=e_yT.t[:], in_=ptb[:, 0:512]), [ebank], [e_yT])
            self.dma("sp", self.s_y[2][:, t0:t0 + 128].rearrange("(c p) t -> p c t", p=128), e_yT.t[:].rearrange("p (c t) -> p c t", c=4), reads=[e_yT])

        gens = [chain(0, 0), chain(0, 1), chain(1, 0), chain(1, 1)]
        offs = [0, 6, 3, 9]
        alive = [True] * 4
        done_tiles = [0] * 4
        order_b = [1, 0] + tiles[:1:-1]
        pos_f = {t: i for i, t in enumerate(tiles)}
        pos_b = {t: i for i, t in enumerate(order_b)}
        ready_at = sorted(tiles, key=lambda t: max(pos_f[t], pos_b[t]))
        ep_i = 0
        rnd = 0
        while any(alive) or ep_i < ntile:
            for gi in range(4):
                if not alive[gi] or rnd < offs[gi]:
                    continue
                try:
                    if next(gens[gi]) == "T":
                        done_tiles[gi] += 1
                except StopIteration:
                    alive[gi] = False
            if ep_i < ntile:
                t = ready_at[ep_i]
                if min(done_tiles[0], done_tiles[1]) > pos_f[t] and min(done_tiles[2], done_tiles[3]) > pos_b[t]:
                    epilogue(t)
                    ep_i += 1
                elif not any(alive):
                    raise RuntimeError("gdn epilogue gating never satisfied")
            rnd += 1
        self.release(mk)

    @staticmethod
    def _gdn_fwd_needed(bsteps, nst, ntile):
        order = [1, 0] + list(range(ntile - 1, 1, -1))
        tl = order[min(bsteps // nst, ntile - 1)]
        return (tl + 1) * nst


    def mixer_gdn_old(self, l):
        w = self.w
        mk = self.mark()
        pb = self.pbank
        gm = self.sb("gd_mask", (128, 8, 128), F32)
        self.dma("sp", gm.t[:], self.gdnmask_in.rearrange("p (a b) -> p a b", a=8), writes=[gm])
        cw = self.sb("gd_cw", (128, 2, 12, 4), F32)
        for d in range(2):
            for c in range(12):
                self.P.dma("sp", cw.t[:, d, c, :], w["gdn_conv_w"][l, d, :, c * 128:(c + 1) * 128].rearrange("k p -> p k"), [], [cw.b], allow_slow_non_contiguous=True)
        prow = self.sb("gd_prow", (1, 16 + 128), F32)
        self.dma("sp", prow.t[:, 0:8], w["gdn_a_log"][l:l + 1].rearrange("o d h -> o (d h)"), writes=[prow])
        self.dma("sp", prow.t[:, 8:16], w["gdn_dt_bias"][l:l + 1].rearrange("o d h -> o (d h)"), writes=[prow])
        self.dma("sp", prow.t[:, 16:144], w["gdn_norm_g"][l:l + 1, :], writes=[prow])
        pbc = self.sb("gd_pbc", (128, 144), F32)
        self.mm(pb[0].t[:, 0:144], self.ones.t[0:1, :], prow.t[0:1, :], True, True, [self.ones, prow], [pb[0]])
        self.op("act", lambda e: e.copy(out=pbc.t[:], in_=pb[0].t[:, 0:144]), [pb[0]], [pbc])
        self.op("act", lambda e: e.activation(out=pbc.t[:, 0:8], in_=pbc.t[:, 0:8], func=AF.Exp), [pbc], [pbc])
        self.op("act", lambda e: e.mul(out=pbc.t[:, 0:8], in_=pbc.t[:, 0:8], mul=-1.0), [pbc], [pbc])
        eps6 = self.sb("gd_eps", (128, 1), F32)
        self.op("pool", lambda e: e.memset(eps6.t[:], NORM_EPS), [], [eps6])
        S = self.sb("gd_S", (128, 2, 4, 128), F32)
        self.op("pool", lambda e: e.memset(S.t[:], 0.0), [], [S])
        xin = [self.sb("gd_x%d" % i, (128, 131), F32) for i in range(3)]
        U = self.sb("gd_U", (128, 12, 128), F32)
        sq = [self.sb("gd_sq%d" % i, (128, 128), F32) for i in range(2)]
        ab = self.sb("gd_ab", (128, 16), F32)
        sc = self.sb("gd_sc", (128, 8, 4), F32)
        KQ = [self.sb("gd_KQ%d" % i, (128, 256), F32) for i in range(2)]
        Ktok = [self.sb("gd_Ks%d" % i, (128, 128), F32) for i in range(2)]
        sol = [self.sb("gd_sol%d" % i, (128, 256), F32) for i in range(4)]
        XX = [self.sb("gd_XX%d" % i, (128, 256), F32) for i in range(4)]
        Yb = [self.sb("gd_Yb%d" % i, (128, 256), F32) for i in range(2)]
        PPt = [self.sb("gd_PP%d" % i, (128, 256), F32) for i in range(2)]
        Xo = [self.sb("gd_Xo%d" % i, (128, 3, 128), F32) for i in range(2)]
        T1t = [self.sb("gd_T1%d" % i, (128, 128), F32) for i in range(2)]
        EN = [self.sb("gd_EN%d" % i, (128, 128), F32) for i in range(2)]
        ET = [self.sb("gd_ET%d" % i, (128, 128), F32) for i in range(2)]
        gL = [self.sb("gd_gL%d" % i, (128, 128), F32) for i in range(2)]
        aT = [self.sb("gd_aT%d" % i, (128, 128), F32) for i in range(2)]
        wT = [self.sb("gd_wT%d" % i, (128, 128), F32) for i in range(2)]
        vn = [self.sb("gd_vn%d" % i, (128, 128), F32) for i in range(2)]
        o2 = [self.sb("gd_o2%d" % i, (128, 128), F32) for i in range(2)]
        otok = [self.sb("gd_o%d" % i, (128, 512), F32) for i in range(2)]
        ofl = [self.sb("gd_of%d" % i, (128, 512), F32) for i in range(2)]
        gz = [self.sb("gd_gz%d" % i, (128, 512), F32) for i in range(2)]
        ms = self.sb("gd_ms", (128, 8), F32)
        ybt = [self.sb("gd_y%d" % i, (128, 512), BF16) for i in range(2)]
        yT = [self.sb("gd_yT%d" % i, (128, 512), BF16) for i in range(2)]
        ntile = self.TT // 128
        tiles = list(range(ntile))
        ofB = {}
        kx = 0
        hk = 0
        for d in range(2):
            SL, IU = gm.t[:, 2 * d, :], gm.t[:, 2 * d + 1, :]
            order = tiles if d == 0 else [1, 0] + tiles[:1:-1]
            for ti, tl in enumerate(order):
                t0 = tl * 128
                s0, s1 = (0, NCTX) if t0 < NCTX else (NCTX, self.TT)
                lo, hi = (max(s0, t0 - 3), t0 + 128) if d == 0 else (t0, min(s1, t0 + 131))
                base = t0 - 3 if d == 0 else t0
                for c in range(12):
                    x = xin[kx % 3]
                    kx += 1
                    if hi - lo < 131:
                        self.op("pool", lambda e, x=x: e.memset(x.t[:], 0.0), [], [x])
                    self.dma("sp", x.t[:, lo - base:hi - base], self.s_gqkv[c * 128:(c + 1) * 128, lo:hi], writes=[x])
                    for kk in range(4):
                        off = kk if d == 0 else 3 - kk
                        if kk == 0:
                            self.op("dve", lambda e, x=x, off=off, c=c, d=d: e.tensor_scalar_mul(out=U.t[:, c, :], in0=x.t[:, off:off + 128], scalar1=cw.t[:, d, c, 0:1]), [x, cw], [U])
                        else:
                            self.op("dve", lambda e, x=x, off=off, c=c, d=d, kk=kk: e.scalar_tensor_tensor(out=U.t[:, c, :], in0=x.t[:, off:off + 128], scalar=cw.t[:, d, c, kk:kk + 1],
                                                                                            in1=U.t[:, c, :], op0=ALU.mult, op1=ALU.add), [x, cw, U], [U])
                    self.op("act", lambda e, c=c: e.activation(out=U.t[:, c, :], in_=U.t[:, c, :], func=AF.Silu), [U], [U])
                for c in range(8):
                    q_ = sq[c % 2]
                    pn = pb[c % 2]
                    self.op("act", lambda e, c=c, q_=q_: e.activation(out=q_.t[:], in_=U.t[:, c, :], func=AF.Square), [U], [q_])
                    self.mm(pn.t[:, 0:128], self.ones.t[:], q_.t[:], True, True, [self.ones, q_], [pn])
                    self.op("act", lambda e, q_=q_, pn=pn: e.activation(out=q_.t[:], in_=pn.t[:, 0:128], func=AF.Sqrt, bias=eps6.t[:], scale=1.0), [pn, eps6], [q_])
                    self.op("dve", lambda e, q_=q_: e.reciprocal(out=q_.t[:], in_=q_.t[:]), [q_], [q_])
                    cst = 128.0 ** -0.5 if c < 4 else 1.0
                    self.op("dve", lambda e, c=c, q_=q_, cst=cst: e.scalar_tensor_tensor(out=U.t[:, c, :], in0=U.t[:, c, :], scalar=cst, in1=q_.t[:], op0=ALU.mult, op1=ALU.mult), [U, q_], [U])
                self.dma("sp", ab.t[:], self.s_ab[t0:t0 + 128, :], writes=[ab])
                a4 = ab.t[:, 4 * d:4 * d + 4]
                b4 = ab.t[:, 8 + 4 * d:12 + 4 * d]
                self.op("act", lambda e, b4=b4: e.activation(out=sc.t[:, 0, :], in_=b4, func=AF.Sigmoid), [ab], [sc])
                self.op("dve", lambda e, a4=a4, d=d: e.tensor_tensor(out=sc.t[:, 1, :], in0=a4, in1=pbc.t[:, 8 + 4 * d:12 + 4 * d], op=ALU.add), [ab, pbc], [sc])
                self.op("act", lambda e: e.activation(out=sc.t[:, 1, :], in_=sc.t[:, 1, :], func=AF.Exp), [sc], [sc])
                self.op("act", lambda e: e.activation(out=sc.t[:, 1, :], in_=sc.t[:, 1, :], func=AF.Ln, bias=1.0, scale=1.0), [sc], [sc])
                self.op("dve", lambda e, d=d: e.tensor_tensor(out=sc.t[:, 1, :], in0=sc.t[:, 1, :], in1=pbc.t[:, 4 * d:4 * d + 4], op=ALU.mult), [sc, pbc], [sc])
                pg = pb[2]
                self.mm(pg.t[:, 0:4], IU, sc.t[:, 1, :], True, True, [gm, sc], [pg])
                self.mm(pg.t[:, 4:8], self.ones.t[:], sc.t[:, 1, :], True, True, [self.ones, sc], [pg])
                self.op("act", lambda e, pg=pg: e.copy(out=sc.t[:, 2, :], in_=pg.t[:, 0:4]), [pg], [sc])
                self.op("act", lambda e, pg=pg: e.activation(out=sc.t[:, 3, :], in_=pg.t[:, 0:4], func=AF.Exp), [pg], [sc])
                self.op("act", lambda e, pg=pg: e.activation(out=sc.t[:, 7, :], in_=pg.t[:, 4:8], func=AF.Exp), [pg], [sc])
                self.op("dve", lambda e: e.tensor_tensor(out=sc.t[:, 4, :], in0=sc.t[:, 0, :], in1=sc.t[:, 3, :], op=ALU.mult), [sc], [sc])
                self.op("dve", lambda e, pg=pg: e.tensor_tensor(out=sc.t[:, 5, :], in0=pg.t[:, 4:8], in1=sc.t[:, 2, :], op=ALU.subtract), [pg, sc], [sc])
                self.op("act", lambda e: e.activation(out=sc.t[:, 5, :], in_=sc.t[:, 5, :], func=AF.Exp), [sc], [sc])
                self.op("dve", lambda e: e.tensor_scalar_mul(out=sc.t[:, 6, :], in0=sc.t[:, 0, :], scalar1=-1.0), [sc], [sc])
                ot = otok[ti % 2]
                for h in range(4):
                    j = hk % 2
                    hk += 1
                    QTc, KTc, VTc = U.t[:, h, :], U.t[:, 4 + h, :], U.t[:, 8 + h, :]
                    kq, ks, so, en, et, gl_, at_, wt_, vn_, o2_ = KQ[j], Ktok[j], sol[2 * j], EN[j], ET[j], gL[j], aT[j], wT[j], vn[j], o2[j]
                    so2 = sol[2 * j + 1]
                    xa, xb = XX[2 * j], XX[2 * j + 1]
                    col = lambda q, h=h: sc.t[:, q, h:h + 1]
                    self.op("pool", lambda e, kq=kq, KTc=KTc: e.tensor_copy(out=kq.t[:, 0:128], in_=KTc), [U], [kq])
                    self.op("pool", lambda e, kq=kq, QTc=QTc: e.tensor_copy(out=kq.t[:, 128:256], in_=QTc), [U], [kq])
                    p0, p1, p2, p3 = pb[4 * j], pb[4 * j + 1], pb[4 * j + 2], pb[4 * j + 3]
                    self.op("pe", lambda e, p0=p0, KTc=KTc: e.transpose(p0.t[:, 0:128], KTc, self.ident.t[:]), [U, self.ident], [p0])
                    self.op("pe", lambda e, p0=p0, VTc=VTc: e.transpose(p0.t[:, 128:256], VTc, self.ident.t[:]), [U, self.ident], [p0])
                    self.op("dve", lambda e, so=so, p0=p0, col=col: e.tensor_scalar_mul(out=so.t[:, 0:128], in0=p0.t[:, 128:256], scalar1=col(0)), [p0, sc], [so])
                    self.op("dve", lambda e, so=so, p0=p0, col=col: e.tensor_scalar_mul(out=so.t[:, 128:256], in0=p0.t[:, 0:128], scalar1=col(4)), [p0, sc], [so])
                    self.op("act", lambda e, ks=ks, p0=p0, col=col: e.mul(out=ks.t[:], in_=p0.t[:, 0:128], mul=col(5)), [p0, sc], [ks])
                    self.mm(p1.t[:, 0:256], KTc, kq.t[:], True, True, [U, kq], [p1])
                    self.op("dve", lambda e, gl_=gl_, col=col, SL=SL: e.tensor_scalar_mul(out=gl_.t[:], in0=SL, scalar1=col(1)), [gm, sc], [gl_])
                    self.mm(p2.t[:, 0:128], IU, gl_.t[:], True, True, [gm, gl_], [p2])
                    self.mm(p2.t[:, 128:256], gl_.t[:], IU, True, True, [gm, gl_], [p2])
                    self.op("act", lambda e, en=en, p2=p2: e.activation(out=en.t[:], in_=p2.t[:, 0:128], func=AF.Exp), [p2], [en])
                    self.op("act", lambda e, et=et, p2=p2: e.activation(out=et.t[:], in_=p2.t[:, 128:256], func=AF.Exp), [p2], [et])
                    self.op("dve", lambda e, en=en, p1=p1: e.tensor_tensor(out=en.t[:], in0=p1.t[:, 0:128], in1=en.t[:], op=ALU.mult), [p1, en], [en])
                    self.op("dve", lambda e, en=en, xa=xa, col=col, SL=SL: e.scalar_tensor_tensor(out=xa.t[:, 0:128], in0=en.t[:], scalar=col(6), in1=SL, op0=ALU.mult, op1=ALU.mult),
                            [en, sc, gm], [xa])
                    self.op("dve", lambda e, et=et, p1=p1: e.tensor_tensor(out=et.t[:], in0=p1.t[:, 128:256], in1=et.t[:], op=ALU.mult), [p1, et], [et])
                    self.op("pool", lambda e, et=et, at_=at_, IU=IU: e.tensor_tensor(out=at_.t[:], in0=et.t[:], in1=IU, op=ALU.mult), [et, gm], [at_])
                    self.op("pe", lambda e, p3=p3, xa=xa: e.transpose(p3.t[:, 0:128], xa.t[:, 0:128], self.ident.t[:]), [xa, self.ident], [p3])
                    self.op("act", lambda e, xa=xa, p3=p3: e.copy(out=xa.t[:, 128:256], in_=p3.t[:, 0:128]), [p3], [xa])
                    D16 = gm.t[:, 4, :]
                    ya, yb_, pp_, xo_, t1_ = xb, Yb[j], PPt[j], Xo[j], T1t[j]
                    self.op("pool", lambda e, xa=xa, ya=ya, D16=D16: e.tensor_tensor(out=ya.t[:, 0:128], in0=xa.t[:, 0:128], in1=D16, op=ALU.mult), [xa, gm], [ya])
                    self.op("pool", lambda e, xa=xa, ya=ya, D16=D16: e.tensor_tensor(out=ya.t[:, 128:256], in0=xa.t[:, 128:256], in1=D16, op=ALU.mult), [xa, gm], [ya])
                    for q in range(3):
                        self.op("pool", lambda e, xa=xa, xo_=xo_, q=q: e.tensor_tensor(out=xo_.t[:, q, :], in0=xa.t[:, 128:256], in1=gm.t[:, 5 + q, :], op=ALU.mult), [xa, gm], [xo_])
                    self.op("dve", lambda e, ya=ya, pp_=pp_: e.tensor_tensor(out=pp_.t[:, 0:128], in0=ya.t[:, 0:128], in1=self.ident.t[:], op=ALU.add), [ya, self.ident], [pp_])
                    self.op("dve", lambda e, ya=ya, pp_=pp_: e.tensor_tensor(out=pp_.t[:, 128:256], in0=ya.t[:, 128:256], in1=self.ident.t[:], op=ALU.add), [ya, self.ident], [pp_])
                    cur, nxt = ya, yb_
                    for lev in range(3):
                        self.mm(p2.t[:, 0:128], cur.t[:, 128:256], cur.t[:, 0:128], True, True, [cur], [p2])
                        self.mm(p2.t[:, 128:256], cur.t[:, 0:128], cur.t[:, 128:256], True, True, [cur], [p2])
                        self.op("act", lambda e, nxt=nxt, p2=p2: e.copy(out=nxt.t[:], in_=p2.t[:, 0:256]), [p2], [nxt])
                        cur, nxt = nxt, cur
                        self.mm(p1.t[:, 0:128], cur.t[:, 128:256], pp_.t[:, 0:128], True, True, [cur, pp_], [p1])
                        self.mm(p1.t[:, 128:256], pp_.t[:, 0:128], cur.t[:, 128:256], True, True, [cur, pp_], [p1])
                        self.op("dve", lambda e, pp_=pp_, p1=p1: e.tensor_tensor(out=pp_.t[:], in0=pp_.t[:], in1=p1.t[:, 0:256], op=ALU.add), [pp_, p1], [pp_])
                    for q in range(3):
                        self.mm(p2.t[:, 0:128], xo_.t[:, q, :], pp_.t[:, 0:128], True, True, [xo_, pp_], [p2])
                        self.op("act", lambda e, t1_=t1_, p2=p2: e.copy(out=t1_.t[:], in_=p2.t[:, 0:128]), [p2], [t1_])
                        self.mm(p1.t[:, 0:128], pp_.t[:, 128:256], t1_.t[:], True, True, [pp_, t1_], [p1])
                        self.op("dve", lambda e, pp_=pp_, p1=p1: e.tensor_tensor(out=pp_.t[:, 0:128], in0=pp_.t[:, 0:128], in1=p1.t[:, 0:128], op=ALU.add), [pp_, p1], [pp_])
                        self.op("pe", lambda e, p2=p2, pp_=pp_: e.transpose(p2.t[:, 128:256], pp_.t[:, 0:128], self.ident.t[:]), [pp_, self.ident], [p2])
                        self.op("act", lambda e, pp_=pp_, p2=p2: e.copy(out=pp_.t[:, 128:256], in_=p2.t[:, 128:256]), [p2], [pp_])
                    self.mm(p1.t[:, 0:256], pp_.t[:, 128:256], so.t[:], True, True, [pp_, so], [p1])
                    self.op("act", lambda e, so2=so2, p1=p1: e.copy(out=so2.t[:], in_=p1.t[:, 0:256]), [p1], [so2])
                    scur = so2
                    self.op("pe", lambda e, p3=p3, scur=scur: e.transpose(p3.t[:, 0:128], scur.t[:, 128:256], self.ident.t[:]), [scur, self.ident], [p3])
                    self.op("act", lambda e, wt_=wt_, p3=p3: e.copy(out=wt_.t[:], in_=p3.t[:, 0:128]), [p3], [wt_])
                    Sh = S.t[:, d, h, :]
                    self.mm(p0.t[:, 0:128], wt_.t[:], Sh, True, True, [wt_, S], [p0])
                    self.op("dve", lambda e, vn_=vn_, scur=scur, p0=p0: e.tensor_tensor(out=vn_.t[:], in0=scur.t[:, 0:128], in1=p0.t[:, 0:128], op=ALU.subtract), [scur, p0], [vn_])
                    self.mm(p0.t[:, 128:256], QTc, Sh, True, True, [U, S], [p0])
                    self.mm(p0.t[:, 256:384], at_.t[:], vn_.t[:], True, True, [at_, vn_], [p0])
                    self.op("act", lambda e, o2_=o2_, p0=p0: e.copy(out=o2_.t[:], in_=p0.t[:, 256:384]), [p0], [o2_])
                    self.op("dve", lambda e, ot=ot, h=h, p0=p0, o2_=o2_, col=col: e.scalar_tensor_tensor(out=ot.t[:, h * 128:(h + 1) * 128], in0=p0.t[:, 128:256], scalar=col(3),
                                                                                          in1=o2_.t[:], op0=ALU.mult, op1=ALU.add), [p0, sc, o2_], [ot])
                    self.mm(p3.t[:, 128:256], ks.t[:], vn_.t[:], True, True, [ks, vn_], [p3])
                    self.op("dve", lambda e, Sh=Sh, p3=p3, col=col: e.scalar_tensor_tensor(out=Sh, in0=Sh, scalar=col(7), in1=p3.t[:, 128:256], op0=ALU.mult, op1=ALU.add),
                            [S, sc, p3], [S])
                rows = slice(t0, t0 + 128)
                if d == 0:
                    B = Buf("of")
                    ofB[tl] = B
                    self.P.dma("sp", self.s_of[rows, :], ot.t[:], [ot.b], [B])
                else:
                    f_, z_, y_, yT_ = ofl[ti % 2], gz[ti % 2], ybt[ti % 2], yT[ti % 2]
                    self.P.dma("sp", f_.t[:], self.s_of[rows, :], [ofB[tl]], [f_.b])
                    self.dma("sp", z_.t[:], self.s_gz[rows, :], writes=[z_])
                    self.op("dve", lambda e, f_=f_, ot=ot: e.tensor_tensor(out=f_.t[:], in0=f_.t[:], in1=ot.t[:], op=ALU.add), [f_, ot], [f_])
                    for h in range(4):
                        hs = slice(h * 128, (h + 1) * 128)
                        self.op("act", lambda e, ot=ot, f_=f_, hs=hs, h=h: e.activation(out=ot.t[:, hs], in_=f_.t[:, hs], func=AF.Square, accum_out=ms.t[:, h:h + 1]), [f_], [ot, ms])
                    self.op("act", lambda e: e.activation(out=ms.t[:, 4:8], in_=ms.t[:, 0:4], func=AF.Sqrt, bias=eps6.t[:], scale=1.0 / 128), [ms, eps6], [ms])
                    self.op("dve", lambda e: e.reciprocal(out=ms.t[:, 4:8], in_=ms.t[:, 4:8]), [ms], [ms])
                    for h in range(4):
                        hs = slice(h * 128, (h + 1) * 128)
                        self.op("dve", lambda e, f_=f_, hs=hs, h=h: e.scalar_tensor_tensor(out=f_.t[:, hs], in0=f_.t[:, hs], scalar=ms.t[:, 4 + h:5 + h], in1=pbc.t[:, 16:144],
                                                                                    op0=ALU.mult, op1=ALU.mult), [f_, ms, pbc], [f_])
                    self.op("dve", lambda e, f_=f_, z_=z_, y_=y_: e.tensor_tensor(out=y_.t[:], in0=f_.t[:], in1=z_.t[:], op=ALU.mult), [f_, z_], [y_])
                    ptr = pb[3]
                    ptb = ptr.t[:].bitcast(BF16)
                    for c in range(4):
                        self.op("pe", lambda e, c=c, y_=y_, ptb=ptb: e.transpose(ptb[:, c * 128:(c + 1) * 128], y_.t[:, c * 128:(c + 1) * 128], self.identb.t[:]), [y_, self.identb], [ptr])
                    self.op("act", lambda e, yT_=yT_, ptb=ptb: e.copy(out=yT_.t[:], in_=ptb[:, 0:512]), [ptr], [yT_])
                    self.dma("pool", self.s_y[2][:, t0:t0 + 128].rearrange("(c p) t -> p c t", p=128), yT_.t[:].rearrange("p (c t) -> p c t", c=4), reads=[yT_])
        self.release(mk)

    def finish(self):
        self.P.emit()
        self.st.close()
        return self.nc

    def xrows(self, src_kind, tok0, t):
        a = tok0 + t * 128
        if src_kind == "xs":
            return self.xs[a:a + 128, :]
        if tok0 < NCTX:
            return self.ctx_in[a:a + 128, :]
        if src_kind == "in":
            return self.x_in[a - NCTX:a - NCTX + 128, :]
        return self.out[a - NCTX:a - NCTX + 128, :]

    def phase_ffn(self, l, f, j, src_kind, dst_kind, skip_ctx=False):
        mk = self.mark()
        self.store_q = "act"
        self.alloc_dense()
        for (tok0, gw, is_ctx) in self.groups:
            if is_ctx and skip_ctx:
                continue
            nt = gw // 128
            for t in range(nt):
                self.dma("sp", self.xt[t].t[:], self.xrows(src_kind, tok0, t), writes=[self.xt[t]])
            self.ffn(l, f, j, 1 if is_ctx else 0, self.xt, gw)
            for t in range(nt):
                dk = "xs" if (is_ctx or dst_kind == "xs") else "out"
                self.dma("pool", self.xrows(dk, tok0, t), self.xt[t].t[:], reads=[self.xt[t]])
        self.release(mk)

    def phase_a2(self, l):
        mk = self.mark()
        self.store_q = "act"
        self.alloc_ln()
        self.alloc_a2()
        for (tok0, gw, is_ctx) in self.groups:
            nt = gw // 128
            r = 1 if is_ctx else 0
            for t in range(nt):
                self.dma("sp", self.xt[t].t[:], self.xrows("xs", tok0, t), writes=[self.xt[t]])
                self.ln_to_hT(self.xt[t], 1, r, t)
            self.mixer_inputs(l, r, tok0, gw)
        self.release(mk)

    def phase_c1(self, l, skip_ctx):
        mk = self.mark()
        self.store_q = "act"
        self.alloc_ln(nw=3)
        self.pk2 = 0
        accm = self.sb("c1_acc", (128, 8, 512), F32)
        mT = self.sb("c1_mT", (128, 8, 512), BF16)
        yin2 = [self.sb("c1_y%d" % i, (128, 4, 512), BF16) for i in range(2)]
        pj2 = [self.sb("c1_pj%d" % i, (128, 4, D), BF16) for i in range(2)]
        wo = self.sb("c1_wo", (128, 8, D), BF16)
        sgt = [self.sb("c1_sg%d" % i, (128, 512), F32) for i in range(2)]
        tm = [self.sb("c1_tm%d" % i, (128, 512), F32) for i in range(2)]
        wb = self.wb[l]
        self.dma("sp", wo.t[:], wb["wout"].rearrange("(k p) c -> p k c", p=128), writes=[wo])
        kbox = [0]

        def group(tok0, gw, is_ctx):
            kk = kbox[0]
            nt = gw // 128
            r = 1 if is_ctx else 0
            for t in range(nt):
                self.dma("sp", self.xt[t].t[:], self.xrows("xs", tok0, t), writes=[self.xt[t]])
                self.ln_to_hT(self.xt[t], 1, r, t)
            for i in range(4):
                yin, pj = yin2[i % 2], pj2[i % 2]
                self.dma("sp", yin.t[:, :, 0:gw], self.s_y[i][:, tok0:tok0 + gw].rearrange("(c p) t -> p c t", p=128), writes=[yin])
                self.dma("sp", pj.t[:], wb["proj"][i].rearrange("(k p) c -> p k c", p=128), writes=[pj])
                for half in range(2):
                    wt = self.load_wcols(l, O_GATES + i * D + half * 512, 512)
                    for q in range(4):
                        fc = half * 4 + q
                        pg = self.cm_mm(wt, q * 128, gw)
                        pp = self._acc()
                        for kc in range(4):
                            self.mm(pp.t[:, 0:gw], pj.t[:, kc, fc * 128:(fc + 1) * 128], yin.t[:, kc, 0:gw], kc == 0, kc == 3, [pj, yin], [pp])
                        sg_, tm_ = sgt[kk % 2], tm[kk % 2]
                        kk += 1
                        self.op("act", lambda e, pg=pg, sg_=sg_: e.activation(out=sg_.t[:, 0:gw], in_=pg.t[:, 0:gw], func=AF.Sigmoid), [pg], [sg_])
                        if i == 0:
                            self.op("dve", lambda e, sg_=sg_, pp=pp, fc=fc: e.tensor_tensor(out=accm.t[:, fc, 0:gw], in0=sg_.t[:, 0:gw], in1=pp.t[:, 0:gw], op=ALU.mult), [sg_, pp], [accm])
                        else:
                            self.op("dve", lambda e, sg_=sg_, pp=pp, tm_=tm_: e.tensor_tensor(out=tm_.t[:, 0:gw], in0=sg_.t[:, 0:gw], in1=pp.t[:, 0:gw], op=ALU.mult), [sg_, pp], [tm_])
                            self.op("pool", lambda e, tm_=tm_, fc=fc: e.tensor_tensor(out=accm.t[:, fc, 0:gw], in0=accm.t[:, fc, 0:gw], in1=tm_.t[:, 0:gw], op=ALU.add), [tm_, accm], [accm])
            for fc in range(8):
                self.op("act", lambda e, fc=fc: e.copy(out=mT.t[:, fc, 0:gw], in_=accm.t[:, fc, 0:gw]), [accm], [mT])
            for t in range(nt):
                yb = [self.pbank[6], self.pbank[7]]
                for h in range(2):
                    for fc in range(8):
                        self.mm(yb[h].t[:], mT.t[:, fc, t * 128:(t + 1) * 128], wo.t[:, fc, h * 512:(h + 1) * 512], fc == 0, fc == 7, [mT, wo], [yb[h]])
                self.post_norm(self.xt[t], yb, self.gate_bc[r][1], 1, self.xt[t])
                self.dma("pool", self.xrows("xs", tok0, t), self.xt[t].t[:], reads=[self.xt[t]])
            kbox[0] = kk

        for (tok0, gw, is_ctx) in self.groups:
            if not (is_ctx and skip_ctx):
                group(tok0, gw, is_ctx)
        self.release(mk)

    def build(self):
        self.declare_io()
        self.alloc_common()
        for l in range(self.n_layers):
            last = (l == self.n_layers - 1)
            if l == 0:
                self.cast_weights(0)
                self.P.barrier()
                for l2 in range(1, self.n_layers):
                    self.cast_weights(l2)
            self.modulation(l)
            steps = [lambda: self.phase_ffn(l, 0, 0, "in" if l == 0 else "xs", "xs"),
                     lambda: self.phase_a2(l),
                     lambda: self.mixers_cla(l, not last),
                     lambda: self.mixer_gdn(l),
                     lambda: self.phase_c1(l, skip_ctx=last),
                     lambda: self.phase_ffn(l, 1, 2, "xs", "out" if last else "xs", skip_ctx=last)]
            for si, st in enumerate(steps):
                if si in self.dbg.get("skip", ()):
                    continue
                st()
        return self.finish()


def host_constants(TL):
    t = np.arange(TL)
    row = (t // 64).astype(np.float32)
    colp = (t % 64).astype(np.float32)
    inv = (np.float32(10000.0) ** (-np.arange(16, dtype=np.float32) / np.float32(16))).astype(np.float32)
    rc = np.zeros((128, TL), np.float32)
    rs = np.zeros((128, TL), np.float32)
    for p in range(128):
        d = p % 64
        j = d % 32
        pos = row if j < 16 else colp
        ang = (pos * inv[j % 16]).astype(np.float32)
        rc[p] = np.cos(ang)
        rs[p] = np.sin(ang) * (-1.0 if d < 32 else 1.0)
    qq = np.arange(128)[:, None]
    kk = np.arange(128)[None, :]
    NEG = np.float32(-30000.0)
    am = np.zeros((128, 3, 128), np.float32)
    am[:, 0, :] = np.where(kk >= qq, 0.0, NEG)
    am[:, 2, :] = np.where(kk <= qq, 0.0, NEG)
    p = np.arange(128)[:, None]
    f = np.arange(128)[None, :]
    d16 = (p // 16 == f // 16)
    offs = [((p // (2 * b) == f // (2 * b)) & (p // b != f // b)) for b in (16, 32, 64)]
    gmk = np.stack([(f < p), (p <= f), (f > p), (p >= f), d16] + offs, axis=1).astype(np.float32)
    return dict(rope_c=rc, rope_s=rs, att_mask=am.reshape(128, 384), gdn_mask=gmk.reshape(128, 1024))


_NC_CACHE = {}


def kernel(**inputs):
    x = np.asarray(inputs["x"], np.float32)
    B_, TL, _ = x.shape
    key = (TL,)
    if key not in _NC_CACHE:
        _NC_CACHE[key] = MK(TL, n_layers=2).build()
    nc = _NC_CACHE[key]
    consts = host_constants(TL)
    in_maps = []
    for b in range(B_):
        m = {"x": np.ascontiguousarray(x[b]), "ctx": np.ascontiguousarray(np.asarray(inputs["ctx"], np.float32)[b]),
             "c": np.ascontiguousarray(np.asarray(inputs["c"], np.float32)[b:b + 1]),
             "c_ctx": np.ascontiguousarray(np.asarray(inputs["c_ctx"], np.float32)[None, :])}
        for n, _s in WEIGHT_SPECS:
            m[n] = np.ascontiguousarray(np.asarray(inputs[n], np.float32))
        m.update(consts)
        in_maps.append(m)
    res = run_bass_kernel_spmd(nc, in_maps, core_ids=list(range(B_)))
    return np.stack([np.asarray(r["out"], np.float32) for r in res.results], axis=0)
```

```python
import contextlib
import numpy as np
import concourse.bass as bass
import concourse.mybir as mybir
from concourse.ap import AP
from concourse.bass_utils import run_bass_kernel_spmd

F32 = mybir.dt.float32
BF16 = mybir.dt.bfloat16
AF = mybir.ActivationFunctionType
ALU = mybir.AluOpType

ENGS = ("pe", "act", "dve", "pool", "sp")
N_DMA_SEMS = 48

D = 1024
DFF = 2816
NCTX = 256
INW = 8976
INW_X = INW + 640
O_CA, O_CG, O_LX, O_LG, O_GQKV, O_GZ, O_GA, O_GB, O_AQ, O_AK, O_AV, O_GATES = (
    0, 512, 1024, 1536, 2048, 3584, 4096, 4104, 4112, 4624, 4752, 4880)
O_AQS, O_AKS = INW, INW + 512
ALPHA = 4.0 ** 0.25
LN_EPS = 1e-5
NORM_EPS = 1e-6
LRU_C = 8.0


class Buf:
    __slots__ = ("name", "lw", "rd")

    def __init__(self, name=""):
        self.name = name
        self.lw = None
        self.rd = []


class Op:
    __slots__ = ("eng", "fn", "waits", "inc", "idx", "dma", "sem", "val", "cnt")

    def __init__(self, eng, fn, dma):
        self.eng = eng
        self.fn = fn
        self.dma = dma
        self.waits = []
        self.inc = False
        self.idx = -1
        self.sem = -1
        self.val = 0
        self.cnt = 0


class Prog:
    def __init__(self, nc):
        self.nc = nc
        self.q = {e: [] for e in ENGS}
        self.waited = {e: {s: -1 for s in ENGS} for e in ENGS}
        self.waited_dma = {e: set() for e in ENGS}
        self.dma_slot_last = [None] * N_DMA_SEMS
        self.dma_slot_cnt = [0] * N_DMA_SEMS
        self.dma_k = 0
        self.pending = {e: [] for e in ENGS}

    def _need(self, op, d):
        if d is None or d is op:
            return
        e = op.eng
        if d.dma:
            if id(d) in self.waited_dma[e]:
                return
            self.waited_dma[e].add(id(d))
            op.waits.append(d)
            return
        if d.eng == e and not op.dma and e == "pe":
            return
        if self.waited[e][d.eng] >= d.idx:
            return
        self.waited[e][d.eng] = d.idx
        op.waits.append(d)
        d.inc = True

    def add(self, eng, fn, reads=(), writes=(), dma=False):
        op = Op(eng, fn, dma)
        op.idx = len(self.q[eng])
        if self.pending[eng]:
            for d in self.pending[eng]:
                self._need(op, d)
            self.pending[eng] = []
        for b in reads:
            self._need(op, b.lw)
        for b in writes:
            self._need(op, b.lw)
            for r in b.rd:
                self._need(op, r)
        if dma:
            s = self.dma_k % N_DMA_SEMS
            self.dma_k += 1
            self._need(op, self.dma_slot_last[s])
            self.dma_slot_last[s] = op
            self.dma_slot_cnt[s] += 1
            op.sem = s
            op.val = 16 * self.dma_slot_cnt[s]
        for b in reads:
            b.rd.append(op)
        for b in writes:
            b.lw = op
            b.rd = []
        self.q[eng].append(op)
        return op

    def dma(self, eng, out, in_, reads=(), writes=(), **kw):
        return self.add(eng, lambda e: e.dma_start(out=out, in_=in_, **kw), reads, writes, dma=True)

    def barrier(self):
        deps = []
        for e in ENGS:
            for op in reversed(self.q[e]):
                if not op.dma:
                    deps.append(op)
                    break
        deps += [o for o in self.dma_slot_last if o is not None]
        for e in ENGS:
            self.pending[e] = list(deps)

    def emit(self):
        nc = self.nc
        tot = {}
        for e in ENGS:
            c = 0
            for op in self.q[e]:
                if not op.dma and op.inc:
                    c += 1
                    op.cnt = c
            tot[e] = c
        with contextlib.ExitStack() as st:
            esem = {e: st.enter_context(nc.semaphore("s_" + e)) for e in ENGS}
            dsem = [st.enter_context(nc.semaphore("d_%d" % i)) for i in range(N_DMA_SEMS)]
            block = st.enter_context(nc.Block())

            def run(name, eng):
                for op in self.q[name]:
                    for d in op.waits:
                        if d.dma:
                            eng.wait_ge(dsem[d.sem], d.val)
                        else:
                            eng.wait_ge(esem[d.eng], d.cnt)
                    ins = op.fn(eng)
                    if op.dma:
                        ins.then_inc(dsem[op.sem], 16)
                    elif op.inc:
                        ins.then_inc(esem[op.eng], 1)
                if name == "sp":
                    for s in range(N_DMA_SEMS):
                        if self.dma_slot_cnt[s]:
                            eng.wait_ge(dsem[s], 16 * self.dma_slot_cnt[s])
                    for e in ENGS:
                        if tot[e]:
                            eng.wait_ge(esem[e], tot[e])

            @block.tensor
            def _(eng):
                run("pe", eng)

            @block.scalar
            def _(eng):
                run("act", eng)

            @block.vector
            def _(eng):
                run("dve", eng)

            @block.gpsimd
            def _(eng):
                run("pool", eng)

            @block.sync
            def _(eng):
                run("sp", eng)


class T:
    __slots__ = ("t", "b", "psum")

    def __init__(self, t, name, psum=False):
        self.t = t
        self.b = Buf(name)
        self.psum = psum


WEIGHT_SPECS = [
    ("ada_w", (2, D, 9 * D)), ("ada_b", (2, 9 * D)), ("norm_g", (2, 3, D)), ("norm_b", (2, 3, D)),
    ("ffn_w_in", (2, 2, D, 2 * DFF)), ("ffn_w_out", (2, 2, DFF, D)), ("w_in", (2, D, INW)),
    ("conv_w", (2, 31, 512)), ("conv_b", (2, 512)), ("conv_norm_g", (2, 512)), ("conv_norm_b", (2, 512)),
    ("lru_conv_w", (2, 2, 4, 512)), ("lru_conv_b", (2, 2, 512)), ("lru_w_r", (2, 2, 8, 64, 64)),
    ("lru_b_r", (2, 2, 512)), ("lru_w_i", (2, 2, 8, 64, 64)), ("lru_b_i", (2, 2, 512)), ("lru_lam", (2, 2, 512)),
    ("gdn_conv_w", (2, 2, 4, 1536)), ("gdn_a_log", (2, 2, 4)), ("gdn_dt_bias", (2, 2, 4)), ("gdn_norm_g", (2, 128)),
    ("att_sink", (2, 8)), ("proj_conv", (2, 512, D)), ("proj_lru", (2, 512, D)), ("proj_gdn", (2, 512, D)),
    ("proj_att", (2, 512, D)), ("w_out", (2, D, D)),
]


class MK:
    def __init__(self, TL, n_layers=2, dbg=None):
        self.TL = TL
        self.TT = TL + NCTX
        self.n_layers = n_layers
        self.dbg = dbg or {}
        self.nc = bass.Bass("TRN2", target_bir_lowering=False)
        self.P = Prog(self.nc)
        self.st = contextlib.ExitStack()
        self.arena = None
        self.store_q = "sp"
        self._tile_off = {}
        self.groups = [(0, NCTX, True)] + [(NCTX + 512 * i, 512, False) for i in range(TL // 512)]

    def dram_in(self, name, shape, dt=F32):
        return self.nc.dram_tensor(name, list(shape), dt, kind="ExternalInput").ap()

    def dram_out(self, name, shape, dt=F32):
        return self.nc.dram_tensor(name, list(shape), dt, kind="ExternalOutput").ap()

    def dram(self, name, shape, dt=F32):
        return self.nc.dram_tensor(name, list(shape), dt).ap()

    def sb(self, name, shape, dt=F32):
        if self.arena is None:
            self.arena_bytes = 206 * 1024
            self.arena = self.st.enter_context(self.nc.sbuf_tensor("arena", [128, self.arena_bytes // 4], F32))
            self.aoff = 0
        esz = 2 if dt == BF16 else 4
        n = 1
        for d in shape[1:]:
            n *= d
        nbytes = (n * esz + 3) // 4 * 4
        assert self.aoff + nbytes <= self.arena_bytes, "SBUF arena overflow at %s (%d + %d)" % (name, self.aoff, nbytes)
        v = self.arena[0:shape[0], self.aoff // 4:(self.aoff + nbytes) // 4]
        tile_off = self.aoff
        self.aoff += nbytes
        if dt == BF16:
            v = v.bitcast(BF16)
        if len(shape) == 3:
            v = v.rearrange("p (a b) -> p a b", a=shape[1])
        elif len(shape) == 4:
            v = v.rearrange("p (a b c) -> p a b c", a=shape[1], b=shape[2])
        t = T(v, name)
        self._tile_off[id(t)] = tile_off
        return t

    def mark(self):
        return self.aoff

    def release(self, m):
        self.aoff = m
        self.store_q = "sp"
        self.P.barrier()

    def ps(self, name, shape, dt=F32):
        return T(self.st.enter_context(self.nc.psum_tensor(name, list(shape), dt)), name, psum=True)

    def op(self, eng, fn, reads=(), writes=()):
        rd = [r.b if isinstance(r, T) else r for r in reads if not (isinstance(r, T) and r.psum)]
        wr = [w.b if isinstance(w, T) else w for w in writes] + [r.b for r in reads if isinstance(r, T) and r.psum]
        return self.P.add(eng, fn, rd, wr)

    def dma(self, eng, out, in_, reads=(), writes=(), cast=False, **kw):
        if eng == "pool" and not cast:
            eng = self.store_q
        return self.P.dma(eng, out, in_, [r.b if isinstance(r, T) else r for r in reads],
                          [w.b if isinstance(w, T) else w for w in writes], **kw)

    def mm(self, out, lhsT, rhs, start, stop, reads, writes):
        return self.op("pe", lambda e: e.matmul(out, lhsT=lhsT, rhs=rhs, start=start, stop=stop), reads, writes)

    def declare_io(self):
        self.x_in = self.dram_in("x", (self.TL, D))
        self.ctx_in = self.dram_in("ctx", (NCTX, D))
        self.c_in = self.dram_in("c", (1, D))
        self.cctx_in = self.dram_in("c_ctx", (1, D))
        self.w = {n: self.dram_in(n, (self.n_layers,) + tuple(s[1:])) for n, s in WEIGHT_SPECS}
        self.ropec_in = self.dram_in("rope_c", (128, self.TL))
        self.ropes_in = self.dram_in("rope_s", (128, self.TL))
        self.attmask_in = self.dram_in("att_mask", (128, 384))
        self.gdnmask_in = self.dram_in("gdn_mask", (128, 1024))
        self.out = self.dram_out("out", (self.TL, D))
        TT = self.TT
        self.xs = self.dram("xs", (TT, D))
        self.wb = {}
        for l in range(self.n_layers):
            self.wb[l] = dict(
                w1=self.dram("w1b%d" % l, (2, D, 2 * DFF), BF16),
                w2=self.dram("w2b%d" % l, (2, DFF, D), BF16),
                win=self.dram("winb%d" % l, (D, INW_X), BF16),
                proj=self.dram("projb%d" % l, (4, 512, D), BF16),
                wout=self.dram("woutb%d" % l, (D, D), BF16),
            )
        mk_d = self.dram_out if self.dbg.get("dump") else self.dram
        self.s_ucv = mk_d("s_ucv", (512, TT))
        self.s_lx = mk_d("s_lx", (512, TT))
        self.s_lg = mk_d("s_lg", (512, TT))
        self.s_gqkv = mk_d("s_gqkv", (1536, TT))
        self.s_gz = mk_d("s_gz", (TT, 512))
        self.s_ab = mk_d("s_ab", (TT, 16))
        self.s_q = self.dram("s_q", (512, TT), BF16)
        self.s_k = self.dram("s_k", (128, TT), BF16)
        self.s_v = self.dram("s_v", (TT, 128), BF16)
        mk_y = self.dram_out if self.dbg.get("dump") else self.dram
        self.s_y = [mk_y("s_y%d" % i, (512, TT), BF16) for i in range(4)]
        self.s_hf = mk_d("s_hf", (512, TT))
        self.s_of = mk_d("s_of", (TT, 512))
        self.s_ob = self.dram("s_ob", (TT, 512))

    def alloc_common(self):
        nc = self.nc
        self.ident = self.sb("ident", (128, 128), F32)
        self.identb = self.sb("identb", (128, 128), BF16)
        self.ones = self.sb("ones", (128, 128), F32)
        self.eps_t = self.sb("eps_t", (128, 1), F32)
        self.op("pool", lambda e: e.memset(self.ident.t[:], 1.0), [], [self.ident])
        self.op("pool", lambda e: e.affine_select(out=self.ident.t[:], in_=self.ident.t[:], pattern=[[-1, 128]],
                                                  compare_op=ALU.is_equal, fill=0.0, base=0, channel_multiplier=1),
                [self.ident], [self.ident])
        self.op("pool", lambda e: e.tensor_copy(out=self.identb.t[:], in_=self.ident.t[:]), [self.ident], [self.identb])
        self.op("pool", lambda e: e.memset(self.ones.t[:], 1.0), [], [self.ones])
        self.op("pool", lambda e: e.memset(self.eps_t.t[:], LN_EPS), [], [self.eps_t])
        psall = self.st.enter_context(self.nc.psum_tensor("psall", [128, 4096], F32))
        self.pbank = [T(psall[:, i * 512:(i + 1) * 512], "pb%d" % i, psum=True) for i in range(8)]
        self.pquad = [T(psall[:, i * 1024:(i + 1) * 1024], "pq%d" % i, psum=True) for i in range(4)]

    def cast_weights(self, l):
        wb = self.wb[l]
        w = self.w

        def cast(dst, src, rows, rstep=128):
            ncol = src.shape[-1]
            b = max(d for d in range(1, 1025) if ncol % d == 0)
            for r0 in range(0, rows, rstep):
                r1 = min(rows, r0 + rstep)
                self.dma("pool", dst[r0:r1].rearrange("r (a b) -> r a b", b=b), src[r0:r1].rearrange("r (a b) -> r a b", b=b), cast=True)

        for f in range(2):
            cast(wb["w1"][f], w["ffn_w_in"][l, f], D)
            cast(wb["w2"][f], w["ffn_w_out"][l, f], DFF)
        cast(wb["win"][:, 0:INW], w["w_in"][l], D)
        for h in range(10):
            src0 = O_AQ + 64 * h
            dst0 = O_AQS + 64 * h
            for r0 in range(0, D, 256):
                self.dma("pool", wb["win"][r0:r0 + 256, dst0:dst0 + 32], w["w_in"][l, r0:r0 + 256, src0 + 32:src0 + 64], cast=True)
                self.dma("pool", wb["win"][r0:r0 + 256, dst0 + 32:dst0 + 64], w["w_in"][l, r0:r0 + 256, src0:src0 + 32], cast=True)
        for i, n in enumerate(("proj_conv", "proj_lru", "proj_gdn", "proj_att")):
            cast(wb["proj"][i], w[n][l], 512)
        cast(wb["wout"], w["w_out"][l], D)

    def modulation(self, l):
        nc = self.nc
        w = self.w
        if l == 0:
            self.cT = self.sb("cT", (128, 8, 2), F32)
            self.colmod = self.sb("colmod", (128, 48, 2), F32)
            self.gate_bc = [[self.sb("gbc%d_%d" % (r, j), (128, D), F32) for j in range(3)] for r in range(2)]
            self.ng_bc = [self.sb("ngbc%d" % j, (128, D), F32) for j in range(3)]
            self.nb_bc = [self.sb("nbbc%d" % j, (128, D), F32) for j in range(3)]
            self.sel2 = self.sb("sel2", (2, 2, 128), F32)
            craw = self.sb("craw", (128, 8, 2), F32)
            self.dma("sp", craw.t[:, :, 0], self.c_in.rearrange("o (k p) -> p (o k)", p=128), writes=[craw], allow_slow_non_contiguous=True)
            self.dma("sp", craw.t[:, :, 1], self.cctx_in.rearrange("o (k p) -> p (o k)", p=128), writes=[craw], allow_slow_non_contiguous=True)
            self.op("act", lambda e: e.activation(out=self.cT.t[:], in_=craw.t[:], func=AF.Silu), [craw], [self.cT])
            self.op("pool", lambda e: e.memset(self.sel2.t[:], 0.0), [], [self.sel2])
            self.op("pool", lambda e: e.memset(self.sel2.t[0:1, 0, :], 1.0), [self.sel2], [self.sel2])
            self.dma("sp", self.sel2.t[1:2, 1, :], self.ones.t[0:1, :], reads=[self.ones, self.sel2], writes=[self.sel2])
        mk = self.mark()
        adaw = [self.sb("adaw%d" % i, (128, 8, 512), F32) for i in range(2)]
        adab = [self.sb("adab%d" % i, (2, 512), F32) for i in range(2)]
        modrow = [self.sb("modrow%d" % i, (2, 512), F32) for i in range(2)]
        nrow = [self.sb("nrow%d" % i, (1, 512), F32) for i in range(2)]
        cT = self.cT
        pb = self.pbank
        colbank = pb[2]
        for n in range(18):
            wt = adaw[n % 2]
            ab = adab[n % 2]
            mr = modrow[n % 2]
            self.dma("sp", wt.t[:], w["ada_w"][l, :, n * 512:(n + 1) * 512].rearrange("(k p) c -> p k c", p=128), writes=[wt])
            self.dma("sp", ab.t[0:1, :], w["ada_b"][l:l + 1, n * 512:(n + 1) * 512], writes=[ab])
            self.dma("sp", ab.t[1:2, :], w["ada_b"][l:l + 1, n * 512:(n + 1) * 512], writes=[ab])
            bank = pb[n % 2]
            for k in range(8):
                self.mm(bank.t[0:2, :], cT.t[:, k, :], wt.t[:, k, :], k == 0, k == 7, [cT, wt], [bank])
            self.op("dve", lambda e, mr=mr, bank=bank, ab=ab: e.tensor_tensor(out=mr.t[:], in0=bank.t[0:2, :], in1=ab.t[:], op=ALU.add),
                    [bank, ab], [mr])
            j, kind, h = n // 6, (n // 2) % 3, n % 2
            if kind < 2:
                for q in range(4):
                    kc = h * 4 + q
                    idx = (j * 2 + kind) * 8 + kc
                    self.mm(colbank.t[:, idx * 2:idx * 2 + 2], mr.t[0:2, q * 128:(q + 1) * 128], self.ident.t[0:2, 0:2], True, True,
                            [mr, self.ident], [colbank])
            else:
                for r in range(2):
                    gb = pb[3 + r]
                    self.mm(gb.t[:], self.sel2.t[0:2, r, :], mr.t[0:2, :], True, True, [self.sel2, mr], [gb])
                    g = self.gate_bc[r][j]
                    self.op("act", lambda e, g=g, gb=gb, h=h, j=j: e.mul(out=g.t[:, h * 512:(h + 1) * 512], in_=gb.t[:],
                                                                         mul=(1.0 if j == 1 else 0.5)), [gb], [g])
        self.op("dve", lambda e: e.tensor_copy(out=self.colmod.t[:].rearrange("p a b -> p (a b)"), in_=colbank.t[:, 0:96]),
                [colbank], [self.colmod])
        for j in range(3):
            i0 = (j * 2 + 1) * 8
            self.op("dve", lambda e, i0=i0: e.tensor_scalar_add(out=self.colmod.t[:, i0:i0 + 8, :], in0=self.colmod.t[:, i0:i0 + 8, :], scalar1=1.0),
                    [self.colmod], [self.colmod])
        k = 0
        for which, (name, dst) in enumerate((("norm_g", self.ng_bc), ("norm_b", self.nb_bc))):
            for j in range(3):
                for h in range(2):
                    nr = nrow[k % 2]
                    bank = pb[5 + k % 2]
                    k += 1
                    self.dma("sp", nr.t[0:1, :], w[name][l, j:j + 1, h * 512:(h + 1) * 512], writes=[nr])
                    self.mm(bank.t[:], self.ones.t[0:1, :], nr.t[0:1, :], True, True, [self.ones, nr], [bank])
                    self.op("act", lambda e, d=dst[j], bank=bank, h=h: e.copy(out=d.t[:, h * 512:(h + 1) * 512], in_=bank.t[:]), [bank], [dst[j]])
        self.release(mk)

    def colvec(self, j, kind, kc, r):
        i = (j * 2 + kind) * 8 + kc
        return self.colmod.t[:, i, r:r + 1]

    def alloc_ln(self, nw=4):
        self.xt = [self.sb("xt%d" % i, (128, D), F32) for i in range(4)]
        self.xn = [self.sb("xn%d" % i, (128, D), BF16) for i in range(2)]
        self.st6 = [self.sb("st6_%d" % i, (128, 2, 6), F32) for i in range(2)]
        self.mv = [self.sb("mv%d" % i, (128, 2), F32) for i in range(2)]
        self.rstd = [self.sb("rstd%d" % i, (128, 1), F32) for i in range(2)]
        self.hT = self.sb("hT", (128, 8, 512), BF16)
        self.wblk = [self.sb("wblk%d" % i, (128, 8, 512), BF16) for i in range(nw)]
        self.tmp = [self.sb("tmp%d" % i, (128, D), F32) for i in range(2)]
        self.wk = 0
        self.lnk = 0

    def alloc_ffn(self):
        self.actT = self.sb("actT", (128, 22, 512), BF16)
        self.w2 = self.sb("w2", (128, 22, D), BF16)
        self.sg = [self.sb("sg%d" % i, (128, 512), F32) for i in range(2)]

    def alloc_dense(self):
        self.alloc_ln()
        self.alloc_ffn()

    def next_wblk(self):
        t = self.wblk[self.wk % len(self.wblk)]
        self.wk += 1
        return t

    def ln_to_hT(self, xt, j, r, t):
        i = self.lnk % 2
        self.lnk += 1
        st6, mv, rstd, xn = self.st6[i], self.mv[i], self.rstd[i], self.xn[i]
        for h in range(2):
            self.op("dve", lambda e, h=h: e.bn_stats(out=st6.t[:, h, :], in_=xt.t[:, h * 512:(h + 1) * 512]), [xt], [st6])
        self.op("dve", lambda e: e.bn_aggr(out=mv.t[:], in_=st6.t[:].rearrange("p a b -> p (a b)")), [st6], [mv])
        self.op("act", lambda e: e.activation(out=rstd.t[:], in_=mv.t[:, 1:2], func=AF.Sqrt, bias=self.eps_t.t[:], scale=1.0), [mv, self.eps_t], [rstd])
        self.op("dve", lambda e: e.reciprocal(out=rstd.t[:], in_=rstd.t[:]), [rstd], [rstd])
        self.op("dve", lambda e: e.tensor_scalar(out=xn.t[:], in0=xt.t[:], scalar1=mv.t[:, 0:1], scalar2=rstd.t[:], op0=ALU.subtract, op1=ALU.mult),
                [xt, mv, rstd], [xn])
        bank = self.pbank[i]
        pbf = bank.t[:].bitcast(BF16)
        for kc in range(8):
            self.op("pe", lambda e, kc=kc: e.transpose(pbf[:, kc * 128:(kc + 1) * 128], xn.t[:, kc * 128:(kc + 1) * 128], self.identb.t[:]),
                    [xn, self.identb], [bank])
        for kc in range(8):
            eng = "act" if kc % 2 == 0 else "dve"
            dst = self.hT.t[:, kc, t * 128:(t + 1) * 128]
            src = pbf[:, kc * 128:(kc + 1) * 128]
            sc = self.colvec(j, 1, kc, r)
            sh = self.colvec(j, 0, kc, r)
            if eng == "act":
                self.op("act", lambda e, dst=dst, src=src, sc=sc, sh=sh: e.activation(out=dst, in_=src, func=AF.Identity, scale=sc, bias=sh),
                        [bank, self.colmod], [self.hT])
            else:
                self.op("dve", lambda e, dst=dst, src=src, sc=sc, sh=sh: e.tensor_scalar(out=dst, in0=src, scalar1=sc, scalar2=sh, op0=ALU.mult, op1=ALU.add),
                        [bank, self.colmod], [self.hT])

    def post_norm(self, xt, ybanks, gate, j, out_t):
        i = self.lnk % 2
        self.lnk += 1
        st6, mv, rstd, tmp = self.st6[i], self.mv[i], self.rstd[i], self.tmp[i]
        for h in range(2):
            sl = slice(h * 512, (h + 1) * 512)
            self.op("dve", lambda e, h=h, sl=sl: e.tensor_tensor(out=tmp.t[:, sl], in0=ybanks[h].t[:], in1=gate.t[:, sl], op=ALU.mult),
                    [ybanks[h], gate], [tmp])
            self.op("dve", lambda e, sl=sl: e.scalar_tensor_tensor(out=tmp.t[:, sl], in0=xt.t[:, sl], scalar=ALPHA, in1=tmp.t[:, sl], op0=ALU.mult, op1=ALU.add),
                    [xt, tmp], [tmp])
            self.op("dve", lambda e, h=h, sl=sl: e.bn_stats(out=st6.t[:, h, :], in_=tmp.t[:, sl]), [tmp], [st6])
        self.op("dve", lambda e: e.bn_aggr(out=mv.t[:], in_=st6.t[:].rearrange("p a b -> p (a b)")), [st6], [mv])
        self.op("act", lambda e: e.activation(out=rstd.t[:], in_=mv.t[:, 1:2], func=AF.Sqrt, bias=self.eps_t.t[:], scale=1.0), [mv, self.eps_t], [rstd])
        self.op("dve", lambda e: e.reciprocal(out=rstd.t[:], in_=rstd.t[:]), [rstd], [rstd])
        self.op("dve", lambda e: e.tensor_scalar(out=tmp.t[:], in0=tmp.t[:], scalar1=mv.t[:, 0:1], scalar2=rstd.t[:], op0=ALU.subtract, op1=ALU.mult),
                [tmp, mv, rstd], [tmp])
        self.op("pool", lambda e: e.tensor_tensor(out=tmp.t[:], in0=tmp.t[:], in1=self.ng_bc[j].t[:], op=ALU.mult), [tmp, self.ng_bc[j]], [tmp])
        self.op("pool", lambda e: e.tensor_tensor(out=out_t.t[:], in0=tmp.t[:], in1=self.nb_bc[j].t[:], op=ALU.add), [tmp, self.nb_bc[j]], [out_t])

    def ffn(self, l, f, j, r, xts, gw):
        nt = gw // 128
        wb = self.wb[l]
        for t in range(nt):
            self.ln_to_hT(xts[t], j, r, t)
        self.dma("sp", self.w2.t[:], wb["w2"][f].rearrange("(jc p) c -> p jc c", p=128), writes=[self.w2])
        w1 = wb["w1"][f]
        blocks = []

        def load(bi):
            wt = self.next_wblk()
            for half in range(2):
                c0 = half * DFF + bi * 256
                self.dma("sp", wt.t[:, :, half * 256:(half + 1) * 256], w1[:, c0:c0 + 256].rearrange("(k p) c -> p k c", p=128), writes=[wt])
            return wt

        blocks.append(load(0))
        blocks.append(load(1))
        pk = 0
        for bi in range(11):
            if bi + 2 < 11:
                blocks.append(load(bi + 2))
            wt = blocks[bi]
            for sub in range(2):
                jc = bi * 2 + sub
                pg = self.pbank[2 + (pk % 2) * 2]
                pu = self.pbank[3 + (pk % 2) * 2]
                sg = self.sg[pk % 2]
                pk += 1
                for k in range(8):
                    self.mm(pg.t[:, 0:gw], wt.t[:, k, sub * 128:(sub + 1) * 128], self.hT.t[:, k, 0:gw], k == 0, k == 7, [wt, self.hT], [pg])
                for k in range(8):
                    self.mm(pu.t[:, 0:gw], wt.t[:, k, 256 + sub * 128:256 + (sub + 1) * 128], self.hT.t[:, k, 0:gw], k == 0, k == 7, [wt, self.hT], [pu])
                self.op("act", lambda e, pg=pg, sg=sg: e.activation(out=sg.t[:, 0:gw], in_=pg.t[:, 0:gw], func=AF.Silu), [pg], [sg])
                self.op("dve", lambda e, pu=pu, sg=sg, jc=jc: e.tensor_tensor(out=self.actT.t[:, jc, 0:gw], in0=sg.t[:, 0:gw], in1=pu.t[:, 0:gw], op=ALU.mult),
                        [pu, sg], [self.actT])
        for t in range(nt):
            yb = [self.pbank[6], self.pbank[7]]
            for h in range(2):
                for jc in range(22):
                    self.mm(yb[h].t[:], self.actT.t[:, jc, t * 128:(t + 1) * 128], self.w2.t[:, jc, h * 512:(h + 1) * 512], jc == 0, jc == 21,
                            [self.actT, self.w2], [yb[h]])
            self.post_norm(xts[t], yb, self.gate_bc[r][j], j, xts[t])


    def alloc_a2(self):
        self.stg = [self.sb("stg%d" % i, (128, 512), F32) for i in range(4)]
        self.stgb = [self.sb("stgb%d" % i, (128, 512), BF16) for i in range(2)]
        self.rc = self.sb("rope_c_t", (128, 512), F32)
        self.rs = self.sb("rope_s_t", (128, 512), F32)
        self.sk = 0
        self.pk2 = 0

    def _stg(self):
        t = self.stg[self.sk % 4]
        self.sk += 1
        return t

    def _acc(self):
        b = self.pbank[2 + self.pk2 % 4]
        self.pk2 += 1
        return b

    def load_wcols(self, l, c0, n):
        wt = self.next_wblk()
        self.dma("sp", wt.t[:, :, 0:n], self.wb[l]["win"][:, c0:c0 + n].rearrange("(k p) c -> p k c", p=128), writes=[wt])
        return wt

    def cm_mm(self, wt, off, gw):
        acc = self._acc()
        for k in range(8):
            self.mm(acc.t[:, 0:gw], wt.t[:, k, off:off + 128], self.hT.t[:, k, 0:gw], k == 0, k == 7, [wt, self.hT], [acc])
        return acc

    def gelu_tanh(self, x, out, n):
        t = self._stg()
        self.op("dve", lambda e: e.tensor_tensor(out=t.t[:, 0:n], in0=x.t[:, 0:n], in1=x.t[:, 0:n], op=ALU.mult), [x], [t])
        self.op("dve", lambda e: e.tensor_scalar(out=t.t[:, 0:n], in0=t.t[:, 0:n], scalar1=0.044715, scalar2=1.0, op0=ALU.mult, op1=ALU.add), [t], [t])
        self.op("dve", lambda e: e.tensor_tensor(out=t.t[:, 0:n], in0=t.t[:, 0:n], in1=x.t[:, 0:n], op=ALU.mult), [t, x], [t])
        self.op("act", lambda e: e.activation(out=t.t[:, 0:n], in_=t.t[:, 0:n], func=AF.Sigmoid, scale=1.5957691216057308), [t], [t])
        self.op("dve", lambda e: e.tensor_tensor(out=out.t[:, 0:n], in0=t.t[:, 0:n], in1=x.t[:, 0:n], op=ALU.mult), [t, x], [out])

    def mixer_inputs(self, l, r, tok0, gw):
        nt = gw // 128
        ts = slice(tok0, tok0 + gw)
        wa = self.load_wcols(l, O_CA, 512)
        wg = self.load_wcols(l, O_CG, 512)
        for i in range(4):
            pa = self.cm_mm(wa, i * 128, gw)
            pg = self.cm_mm(wg, i * 128, gw)
            sg = self._stg()
            u = self._stg()
            self.op("act", lambda e, pg=pg, sg=sg: e.activation(out=sg.t[:, 0:gw], in_=pg.t[:, 0:gw], func=AF.Sigmoid), [pg], [sg])
            self.op("dve", lambda e, pa=pa, sg=sg, u=u: e.tensor_tensor(out=u.t[:, 0:gw], in0=pa.t[:, 0:gw], in1=sg.t[:, 0:gw], op=ALU.mult), [pa, sg], [u])
            self.dma("pool", self.s_ucv[i * 128:(i + 1) * 128, ts], u.t[:, 0:gw], reads=[u])
        wx = self.load_wcols(l, O_LX, 512)
        for i in range(4):
            pa = self.cm_mm(wx, i * 128, gw)
            u = self._stg()
            self.op("act", lambda e, pa=pa, u=u: e.copy(out=u.t[:, 0:gw], in_=pa.t[:, 0:gw]), [pa], [u])
            self.dma("pool", self.s_lx[i * 128:(i + 1) * 128, ts], u.t[:, 0:gw], reads=[u])
        wx = self.load_wcols(l, O_LG, 512)
        for i in range(4):
            pa = self.cm_mm(wx, i * 128, gw)
            x = self._stg()
            u = self._stg()
            self.op("act", lambda e, pa=pa, x=x: e.copy(out=x.t[:, 0:gw], in_=pa.t[:, 0:gw]), [pa], [x])
            self.gelu_tanh(x, u, gw)
            self.dma("pool", self.s_lg[i * 128:(i + 1) * 128, ts], u.t[:, 0:gw], reads=[u])
        for blk in range(3):
            wx = self.load_wcols(l, O_GQKV + blk * 512, 512)
            for i in range(4):
                pa = self.cm_mm(wx, i * 128, gw)
                u = self._stg()
                self.op("act", lambda e, pa=pa, u=u: e.copy(out=u.t[:, 0:gw], in_=pa.t[:, 0:gw]), [pa], [u])
                c = blk * 4 + i
                self.dma("pool", self.s_gqkv[c * 128:(c + 1) * 128, ts], u.t[:, 0:gw], reads=[u])
        wz = self.load_wcols(l, O_GZ, 512)
        wab = self.load_wcols(l, O_GA, 16)
        wv = self.load_wcols(l, O_AV, 128)
        for t in range(nt):
            rows = slice(tok0 + t * 128, tok0 + (t + 1) * 128)
            acc = self._acc()
            for k in range(8):
                self.mm(acc.t[:, 0:512], self.hT.t[:, k, t * 128:(t + 1) * 128], wz.t[:, k, 0:512], k == 0, k == 7, [wz, self.hT], [acc])
            u = self._stg()
            self.op("act", lambda e, acc=acc, u=u: e.activation(out=u.t[:], in_=acc.t[:], func=AF.Silu), [acc], [u])
            self.dma("pool", self.s_gz[rows, :], u.t[:], reads=[u])
            acc = self._acc()
            for k in range(8):
                self.mm(acc.t[:, 0:16], self.hT.t[:, k, t * 128:(t + 1) * 128], wab.t[:, k, 0:16], k == 0, k == 7, [wab, self.hT], [acc])
            for k in range(8):
                self.mm(acc.t[:, 128:256], self.hT.t[:, k, t * 128:(t + 1) * 128], wv.t[:, k, 0:128], k == 0, k == 7, [wv, self.hT], [acc])
            u = self._stg()
            ub = self.stgb[t % 2]
            self.op("act", lambda e, acc=acc, u=u: e.copy(out=u.t[:, 0:16], in_=acc.t[:, 0:16]), [acc], [u])
            self.op("dve", lambda e, acc=acc, ub=ub: e.tensor_copy(out=ub.t[:, 0:128], in_=acc.t[:, 128:256]), [acc], [ub])
            self.dma("pool", self.s_ab[rows, :], u.t[:, 0:16], reads=[u])
            self.dma("pool", self.s_v[rows, :], ub.t[:, 0:128], reads=[ub])
        if r == 0:
            self.dma("sp", self.rc.t[:, 0:gw], self.ropec_in[:, tok0 - NCTX:tok0 - NCTX + gw], writes=[self.rc])
            self.dma("sp", self.rs.t[:, 0:gw], self.ropes_in[:, tok0 - NCTX:tok0 - NCTX + gw], writes=[self.rs])
        wq = self.load_wcols(l, O_AQ, 512)
        wqs = self.load_wcols(l, O_AQS, 512) if r == 0 else None
        wk = self.load_wcols(l, O_AK, 128)
        wks = self.load_wcols(l, O_AKS, 128) if r == 0 else None
        for i in range(5):
            w1_, w2_, off = (wq, wqs, i * 128) if i < 4 else (wk, wks, 0)
            scale = 0.125 if i < 4 else 1.0
            pa = self.cm_mm(w1_, off, gw)
            ub = self.stgb[i % 2]
            if r == 0:
                pb_ = self.cm_mm(w2_, off, gw)
                a = self._stg()
                b = self._stg()
                self.op("dve", lambda e, pa=pa, a=a: e.tensor_tensor(out=a.t[:, 0:gw], in0=pa.t[:, 0:gw], in1=self.rc.t[:, 0:gw], op=ALU.mult), [pa, self.rc], [a])
                self.op("dve", lambda e, pb_=pb_, b=b: e.tensor_tensor(out=b.t[:, 0:gw], in0=pb_.t[:, 0:gw], in1=self.rs.t[:, 0:gw], op=ALU.mult), [pb_, self.rs], [b])
                self.op("dve", lambda e, a=a, b=b: e.tensor_tensor(out=a.t[:, 0:gw], in0=a.t[:, 0:gw], in1=b.t[:, 0:gw], op=ALU.add), [a, b], [a])
                self.op("act", lambda e, a=a, ub=ub, scale=scale: e.mul(out=ub.t[:, 0:gw], in_=a.t[:, 0:gw], mul=scale), [a], [ub])
            else:
                self.op("act", lambda e, pa=pa, ub=ub, scale=scale: e.mul(out=ub.t[:, 0:gw], in_=pa.t[:, 0:gw], mul=scale), [pa], [ub])
            if i < 4:
                self.dma("pool", self.s_q[i * 128:(i + 1) * 128, ts], ub.t[:, 0:gw], reads=[ub])
            else:
                self.dma("pool", self.s_k[:, ts], ub.t[:, 0:gw], reads=[ub])

    def col_load(self, dst, src_row_ap, n):
        self.dma("sp", dst, src_row_ap.rearrange("o (k p) -> p (o k)", p=128), writes=[], allow_slow_non_contiguous=True)

    def seq_segments(self):
        segs = [(0, NCTX, 0, NCTX)]
        for i in range(self.TL // 512):
            segs.append((NCTX + 512 * i, 512, NCTX, self.TT))
        return segs

    def gen_conv(self, l):
        w = self.w
        cw = self.sb("cv_w", (128, 4, 31), F32)
        cb = self.sb("cv_b", (128, 4), F32)
        cg = self.sb("cv_g", (128, 4), F32)
        cbt = self.sb("cv_bt", (128, 4), F32)
        wB = Buf("cvw")
        for c in range(4):
            self.P.dma("sp", cw.t[:, c, :], w["conv_w"][l, :, c * 128:(c + 1) * 128].rearrange("k p -> p k"), [], [cw.b], allow_slow_non_contiguous=True)
        for dst, name in ((cb, "conv_b"), (cg, "conv_norm_g"), (cbt, "conv_norm_b")):
            self.P.dma("sp", dst.t[:], w[name][l:l + 1, :].rearrange("o (k p) -> p (o k)", p=128), [], [dst.b], allow_slow_non_contiguous=True)
        uin = [self.sb("cv_u%d" % i, (128, 542), F32) for i in range(2)]
        Dg = self.sb("cv_Dg", (128, 4, 31, 128), F32)
        for c in range(4):
            idb = AP(self.ident.t[:].tensor, self.ident.t[:].offset, [list(self.ident.t[:].ap[0]), [0, 31], list(self.ident.t[:].ap[1])])
            wv = cw.t[:, c, :]
            wbc = AP(wv.tensor, wv.offset, [list(wv.ap[0]), list(wv.ap[1]), [0, 128]])
            self.op("pool", lambda e, c=c, idb=idb, wbc=wbc: e.tensor_tensor(out=Dg.t[:, c], in0=idb, in1=wbc, op=ALU.mult), [self.ident, cw], [Dg])
        acc = [self.sb("cv_acc%d" % i, (128, 512), F32) for i in range(4)]
        sq = self.sb("cv_sq", (128, 512), F32)
        mean = self.sb("cv_mean", (128, 512), F32)
        rstd = self.sb("cv_rstd", (128, 512), F32)
        yb = [self.sb("cv_y%d" % i, (128, 512), BF16) for i in range(2)]
        kbox = [0]

        def chunk(t0, n, s0, s1):
            p1, p2 = self.pbank[0], self.pbank[1]
            for c in range(4):
                u = uin[kbox[0] % 2]
                kbox[0] += 1
                lo, hi = max(s0, t0 - 15), min(s1, t0 + n + 15)
                if lo > t0 - 15 or hi < t0 + n + 15:
                    self.op("pool", lambda e, u=u: e.memset(u.t[:], 0.0), [], [u])
                self.dma("sp", u.t[:, lo - (t0 - 15):hi - (t0 - 15)], self.s_ucv[c * 128:(c + 1) * 128, lo:hi], writes=[u])
                a = acc[c]
                p3 = self.pbank[7]
                for kk in range(31):
                    self.mm(p3.t[:, 0:n], Dg.t[:, c, kk, :], u.t[:, kk:kk + n], kk == 0, kk == 30, [Dg, u], [p3])
                self.op("act", lambda e, a=a, p3=p3, c=c: e.activation(out=a.t[:, 0:n], in_=p3.t[:, 0:n], func=AF.Identity, bias=cb.t[:, c:c + 1], scale=1.0), [p3, cb], [a])
                self.op("act", lambda e, a=a: e.activation(out=sq.t[:, 0:n], in_=a.t[:, 0:n], func=AF.Square), [a], [sq])
                self.mm(p1.t[:, 0:n], self.ones.t[:], a.t[:, 0:n], c == 0, c == 3, [self.ones, a], [p1])
                self.mm(p2.t[:, 0:n], self.ones.t[:], sq.t[:, 0:n], c == 0, c == 3, [self.ones, sq], [p2])
            self.op("act", lambda e: e.mul(out=mean.t[:, 0:n], in_=p1.t[:, 0:n], mul=1.0 / 512), [p1], [mean])
            self.op("dve", lambda e: e.tensor_tensor(out=sq.t[:, 0:n], in0=mean.t[:, 0:n], in1=mean.t[:, 0:n], op=ALU.mult), [mean], [sq])
            self.op("dve", lambda e: e.scalar_tensor_tensor(out=rstd.t[:, 0:n], in0=p2.t[:, 0:n], scalar=1.0 / 512, in1=sq.t[:, 0:n], op0=ALU.mult, op1=ALU.subtract),
                    [p2, sq], [rstd])
            self.op("act", lambda e: e.activation(out=rstd.t[:, 0:n], in_=rstd.t[:, 0:n], func=AF.Sqrt, bias=self.eps_t.t[:], scale=1.0), [rstd, self.eps_t], [rstd])
            self.op("dve", lambda e: e.reciprocal(out=rstd.t[:, 0:n], in_=rstd.t[:, 0:n]), [rstd], [rstd])
            for c in range(4):
                a = acc[c]
                y = yb[c % 2]
                self.op("dve", lambda e, a=a: e.tensor_tensor(out=a.t[:, 0:n], in0=a.t[:, 0:n], in1=mean.t[:, 0:n], op=ALU.subtract), [a, mean], [a])
                self.op("dve", lambda e, a=a: e.tensor_tensor(out=a.t[:, 0:n], in0=a.t[:, 0:n], in1=rstd.t[:, 0:n], op=ALU.mult), [a, rstd], [a])
                self.op("act", lambda e, a=a, y=y, c=c: e.activation(out=y.t[:, 0:n], in_=a.t[:, 0:n], func=AF.Silu, scale=cg.t[:, c:c + 1], bias=cbt.t[:, c:c + 1]),
                        [a, cg, cbt], [y])
                self.dma("pool", self.s_y[0][c * 128:(c + 1) * 128, t0:t0 + n], y.t[:, 0:n], reads=[y])
        for seg in self.seq_segments():
            chunk(*seg)
            yield

    def gen_lru(self, l):
        w = self.w
        cw = self.sb("lr_cw", (128, 2, 4, 4), F32)
        prm = self.sb("lr_prm", (128, 5, 2, 4), F32)
        wr = self.sb("lr_wr", (128, 2, 4, 128), F32)
        wi = self.sb("lr_wi", (128, 2, 4, 128), F32)
        self.op("pool", lambda e: e.memset(wr.t[:], 0.0), [], [wr])
        self.op("pool", lambda e: e.memset(wi.t[:], 0.0), [], [wi])
        for d in range(2):
            for c in range(4):
                self.P.dma("sp", cw.t[:, d, c, :], w["lru_conv_w"][l, d, :, c * 128:(c + 1) * 128].rearrange("k p -> p k"), [], [cw.b], allow_slow_non_contiguous=True)
                for nb in range(2):
                    blk = slice(nb * 64, nb * 64 + 64)
                    self.dma("sp", wr.t[blk, d, c, nb * 64:nb * 64 + 64], w["lru_w_r"][l, d, 2 * c + nb], writes=[wr])
                    self.dma("sp", wi.t[blk, d, c, nb * 64:nb * 64 + 64], w["lru_w_i"][l, d, 2 * c + nb], writes=[wi])
            for i, name in enumerate(("lru_conv_b", "lru_b_r", "lru_b_i", "lru_lam")):
                self.P.dma("sp", prm.t[:, i, d, :], w[name][l, d:d + 1, :].rearrange("o (k p) -> p (o k)", p=128), [], [prm.b], allow_slow_non_contiguous=True)
        self.op("act", lambda e: e.activation(out=prm.t[:, 4], in_=prm.t[:, 3], func=AF.Exp, scale=-1.0), [prm], [prm])
        self.op("act", lambda e: e.activation(out=prm.t[:, 4], in_=prm.t[:, 4], func=AF.Ln, bias=1.0, scale=1.0), [prm], [prm])
        self.op("act", lambda e: e.mul(out=prm.t[:, 4], in_=prm.t[:, 4], mul=-LRU_C), [prm], [prm])
        xin = [self.sb("lr_x%d" % i, (128, 515), F32) for i in range(2)]
        xc = [self.sb("lr_xc%d" % i, (128, 512), F32) for i in range(2)]
        rg = [self.sb("lr_r%d" % i, (128, 512), F32) for i in range(2)]
        ig = [self.sb("lr_i%d" % i, (128, 512), F32) for i in range(2)]
        av = [self.sb("lr_a%d" % i, (128, 512), F32) for i in range(2)]
        hv = [self.sb("lr_h%d" % i, (128, 512), F32) for i in range(2)]
        hf = [self.sb("lr_hf%d" % i, (128, 512), F32) for i in range(2)]
        gl = [self.sb("lr_gl%d" % i, (128, 512), F32) for i in range(2)]
        yb = [self.sb("lr_y%d" % i, (128, 512), BF16) for i in range(2)]
        state = self.sb("lr_state", (128, 2, 4), F32)
        self.op("pool", lambda e: e.memset(state.t[:], 0.0), [], [state])
        hfB = {}
        segs = self.seq_segments()
        k = 0

        def rev(ap, n):
            full = ap[:, 0:n]
            return AP(full.tensor, full.offset + (n - 1), [[full.ap[0][0], 128], [-1, n]])

        def step(d, t0, n, s0, s1, c, i):
            x, xcv, r_, i_, a_, h_ = xin[i], xc[i], rg[i], ig[i], av[i], hv[i]
            lo, hi = (max(s0, t0 - 3), t0 + n) if d == 0 else (t0, min(s1, t0 + n + 3))
            base = t0 - 3 if d == 0 else t0
            if hi - lo < n + 3:
                self.op("pool", lambda e, x=x: e.memset(x.t[:], 0.0), [], [x])
            self.dma("sp", x.t[:, lo - base:hi - base], self.s_lx[c * 128:(c + 1) * 128, lo:hi], writes=[x])
            for kk in range(4):
                off = kk if d == 0 else 3 - kk
                if kk == 0:
                    self.op("dve", lambda e, x=x, xcv=xcv, off=off, c=c, d=d: e.tensor_scalar(out=xcv.t[:, 0:n], in0=x.t[:, off:off + n], scalar1=cw.t[:, d, c, 0:1],
                                                                                  scalar2=prm.t[:, 0, d, c:c + 1], op0=ALU.mult, op1=ALU.add), [x, cw, prm], [xcv])
                else:
                    self.op("dve", lambda e, x=x, xcv=xcv, off=off, c=c, d=d, kk=kk: e.scalar_tensor_tensor(out=xcv.t[:, 0:n], in0=x.t[:, off:off + n],
                            scalar=cw.t[:, d, c, kk:kk + 1], in1=xcv.t[:, 0:n], op0=ALU.mult, op1=ALU.add), [x, cw, xcv], [xcv])
            pr, pi = self.pbank[2], self.pbank[3]
            self.mm(pr.t[:, 0:n], wr.t[:, d, c, :], xcv.t[:, 0:n], True, True, [wr, xcv], [pr])
            self.mm(pi.t[:, 0:n], wi.t[:, d, c, :], xcv.t[:, 0:n], True, True, [wi, xcv], [pi])
            self.op("act", lambda e, pr=pr, r_=r_, c=c, d=d: e.activation(out=r_.t[:, 0:n], in_=pr.t[:, 0:n], func=AF.Sigmoid, bias=prm.t[:, 1, d, c:c + 1], scale=1.0), [pr, prm], [r_])
            self.op("act", lambda e, pi=pi, i_=i_, c=c, d=d: e.activation(out=i_.t[:, 0:n], in_=pi.t[:, 0:n], func=AF.Sigmoid, bias=prm.t[:, 2, d, c:c + 1], scale=1.0), [pi, prm], [i_])
            self.op("act", lambda e, r_=r_, a_=a_, c=c, d=d: e.activation(out=a_.t[:, 0:n], in_=r_.t[:, 0:n], func=AF.Exp, scale=prm.t[:, 4, d, c:c + 1]), [r_, prm], [a_])
            self.op("dve", lambda e, a_=a_, r_=r_: e.tensor_tensor(out=r_.t[:, 0:n], in0=a_.t[:, 0:n], in1=a_.t[:, 0:n], op=ALU.mult), [a_], [r_])
            self.op("act", lambda e, r_=r_: e.activation(out=r_.t[:, 0:n], in_=r_.t[:, 0:n], func=AF.Sqrt, bias=1.0, scale=-1.0), [r_], [r_])
            self.op("dve", lambda e, i_=i_, xcv=xcv: e.tensor_tensor(out=i_.t[:, 0:n], in0=i_.t[:, 0:n], in1=xcv.t[:, 0:n], op=ALU.mult), [i_, xcv], [i_])
            self.op("dve", lambda e, i_=i_, r_=r_: e.tensor_tensor(out=i_.t[:, 0:n], in0=i_.t[:, 0:n], in1=r_.t[:, 0:n], op=ALU.mult), [i_, r_], [i_])
            st = state.t[:, d, c:c + 1]
            if d == 0:
                self.op("dve", lambda e, h_=h_, a_=a_, i_=i_, st=st: e.tensor_tensor_scan(out=h_.t[:, 0:n], data0=a_.t[:, 0:n], data1=i_.t[:, 0:n], initial=st,
                                                                                  op0=ALU.mult, op1=ALU.add), [a_, i_, state], [h_])
                self.op("act", lambda e, h_=h_, st=st: e.copy(out=st, in_=h_.t[:, n - 1:n]), [h_], [state])
                B = Buf("hf")
                hfB[(c, t0)] = B
                self.P.dma(self.store_q, self.s_hf[c * 128:(c + 1) * 128, t0:t0 + n], h_.t[:, 0:n], [h_.b], [B])
            else:
                self.op("dve", lambda e, h_=h_, a_=a_, i_=i_, st=st: e.tensor_tensor_scan(out=rev(h_.t, n), data0=rev(a_.t, n), data1=rev(i_.t, n), initial=st,
                                                                                  op0=ALU.mult, op1=ALU.add), [a_, i_, state], [h_])
                self.op("act", lambda e, h_=h_, st=st: e.copy(out=st, in_=h_.t[:, 0:1]), [h_], [state])
                f_, g_, y = hf[i], gl[i], yb[i]
                self.P.dma("sp", f_.t[:, 0:n], self.s_hf[c * 128:(c + 1) * 128, t0:t0 + n], [hfB[(c, t0)]], [f_.b])
                self.dma("sp", g_.t[:, 0:n], self.s_lg[c * 128:(c + 1) * 128, t0:t0 + n], writes=[g_])
                self.op("dve", lambda e, f_=f_, h_=h_: e.tensor_tensor(out=f_.t[:, 0:n], in0=f_.t[:, 0:n], in1=h_.t[:, 0:n], op=ALU.add), [f_, h_], [f_])
                self.op("dve", lambda e, f_=f_, g_=g_, y=y: e.tensor_tensor(out=y.t[:, 0:n], in0=f_.t[:, 0:n], in1=g_.t[:, 0:n], op=ALU.mult), [f_, g_], [y])
                self.dma("pool", self.s_y[1][c * 128:(c + 1) * 128, t0:t0 + n], y.t[:, 0:n], reads=[y])
        for d in range(2):
            order = segs if d == 0 else [segs[0]] + segs[:0:-1]
            for (t0, n, s0, s1) in order:
                for c in range(4):
                    step(d, t0, n, s0, s1, c, k % 2)
                    k += 1
                    yield


    def gen_att(self, l, ctx_out):
        w = self.w
        bm = self.sb("at_bm", (128, 384), F32)
        self.dma("sp", bm.t[:], self.attmask_in, writes=[bm])
        srow = self.sb("at_srow", (1, 8), F32)
        sinkB = self.sb("at_sink", (128, 8), F32)
        self.dma("sp", srow.t[:], w["att_sink"][l:l + 1, :], writes=[srow])
        pb = self.pbank
        self.mm(pb[4].t[:, 0:8], self.ones.t[0:1, :], srow.t[0:1, :], True, True, [self.ones, srow], [pb[4]])
        self.op("act", lambda e: e.copy(out=sinkB.t[:], in_=pb[4].t[:, 0:8]), [pb[4]], [sinkB])
        qT = [self.sb("at_q%d" % i, (64, 128), BF16) for i in range(4)]
        kT = [self.sb("at_k%d" % i, (64, 640), BF16) for i in range(2)]
        vv = [self.sb("at_v%d" % i, (128, 5, 64), BF16) for i in range(2)]
        sc = [self.sb("at_sc%d" % i, (128, 640), F32) for i in range(2)]
        pp = [self.sb("at_p%d" % i, (128, 640), BF16) for i in range(2)]
        pT = [self.sb("at_pT%d" % i, (128, 5, 128), BF16) for i in range(2)]
        st = [self.sb("at_st%d" % i, (128, 8), F32) for i in range(2)]
        otok = [self.sb("at_o%d" % i, (128, 512), BF16) for i in range(2)]
        oT = [self.sb("at_oT%d" % i, (128, 512), BF16) for i in range(2)]
        nblk = self.TL // 128
        blocks = ([("c", 0), ("c", 1)] if ctx_out else []) + [("l", i) for i in range(nblk)]
        kq = 0
        kk = 0
        for bi, (kind, i) in enumerate(blocks):
            q0 = i * 128 if kind == "c" else NCTX + i * 128
            if kind == "c":
                lat = []
            else:
                lat = [b for b in (i - 1, i, i + 1) if 0 <= b < nblk]
            nl = len(lat) * 128
            nk = 256 + nl
            nch = nk // 128
            ot = otok[bi % 2]
            for g in range(2):
                kt = kT[kk % 2]
                vt = vv[kk % 2]
                kk += 1
                self.dma("sp", kt.t[:, 0:256], self.s_k[g * 64:(g + 1) * 64, 0:256], writes=[kt])
                self.dma("sp", vt.t[:, 0:2, :], self.s_v[0:256, g * 64:(g + 1) * 64].rearrange("(c p) d -> p c d", p=128), writes=[vt])
                if lat:
                    l0 = NCTX + lat[0] * 128
                    self.dma("sp", kt.t[:, 256:256 + nl], self.s_k[g * 64:(g + 1) * 64, l0:l0 + nl], writes=[kt])
                    self.dma("sp", vt.t[:, 2:2 + len(lat), :], self.s_v[l0:l0 + nl, g * 64:(g + 1) * 64].rearrange("(c p) d -> p c d", p=128), writes=[vt])
                for h in range(4):
                    hq = g * 4 + h
                    j = kq % 2
                    q = qT[kq % 4]
                    kq += 1
                    s_, p_, pt_, st_ = sc[j], pp[j], pT[j], st[j]
                    self.dma("sp", q.t[:], self.s_q[hq * 64:(hq + 1) * 64, q0:q0 + 128], writes=[q])
                    pc, pl, ptr, po = pb[4], pb[5], pb[6], pb[4]
                    self.mm(pc.t[:, 0:256], q.t[:], kt.t[:, 0:256], True, True, [q, kt], [pc])
                    self.op("act", lambda e, s_=s_, pc=pc: e.copy(out=s_.t[:, 0:256], in_=pc.t[:, 0:256]), [pc], [s_])
                    if lat:
                        self.mm(pl.t[:, 0:nl], q.t[:], kt.t[:, 256:256 + nl], True, True, [q, kt], [pl])
                        m0 = (lat[0] - (i - 1)) * 128
                        self.op("dve", lambda e, s_=s_, pl=pl, m0=m0, nl=nl: e.tensor_tensor(out=s_.t[:, 256:256 + nl], in0=pl.t[:, 0:nl], in1=bm.t[:, m0:m0 + nl], op=ALU.add),
                                [pl, bm], [s_])
                    self.op("dve", lambda e, s_=s_, st_=st_, nk=nk: e.reduce_max(out=st_.t[:, 0:1], in_=s_.t[:, 0:nk], axis=mybir.AxisListType.X), [s_], [st_])
                    self.op("dve", lambda e, st_=st_, hq=hq: e.tensor_tensor(out=st_.t[:, 0:1], in0=st_.t[:, 0:1], in1=sinkB.t[:, hq:hq + 1], op=ALU.max), [st_, sinkB], [st_])
                    self.op("dve", lambda e, st_=st_: e.tensor_scalar_mul(out=st_.t[:, 1:2], in0=st_.t[:, 0:1], scalar1=-1.0), [st_], [st_])
                    self.op("act", lambda e, s_=s_, p_=p_, st_=st_, nk=nk: e.activation(out=p_.t[:, 0:nk], in_=s_.t[:, 0:nk], func=AF.Exp, bias=st_.t[:, 1:2], scale=1.0,
                                                                                accum_out=st_.t[:, 2:3]), [s_, st_], [p_, st_])
                    self.op("act", lambda e, st_=st_, hq=hq: e.activation(out=st_.t[:, 3:4], in_=sinkB.t[:, hq:hq + 1], func=AF.Exp, bias=st_.t[:, 1:2], scale=1.0), [st_, sinkB], [st_])
                    self.op("dve", lambda e, st_=st_: e.tensor_tensor(out=st_.t[:, 4:5], in0=st_.t[:, 2:3], in1=st_.t[:, 3:4], op=ALU.add), [st_], [st_])
                    self.op("dve", lambda e, st_=st_: e.reciprocal(out=st_.t[:, 5:6], in_=st_.t[:, 4:5]), [st_], [st_])
                    ptb = ptr.t[:].bitcast(BF16)
                    for c in range(nch):
                        self.op("pe", lambda e, c=c, p_=p_, ptb=ptb: e.transpose(ptb[:, c * 128:(c + 1) * 128], p_.t[:, c * 128:(c + 1) * 128], self.identb.t[:]),
                                [p_, self.identb], [ptr])
                    self.op("act", lambda e, pt_=pt_, ptb=ptb, nk=nk: e.copy(out=pt_.t[:].rearrange("p a b -> p (a b)")[:, 0:nk], in_=ptb[:, 0:nk]), [ptr], [pt_])
                    for c in range(nch):
                        self.mm(po.t[:, 256:320], pt_.t[:, c, :], vt.t[:, c, :], c == 0, c == nch - 1, [pt_, vt], [po])
                    self.op("dve", lambda e, ot=ot, po=po, st_=st_, hq=hq: e.tensor_scalar_mul(out=ot.t[:, hq * 64:(hq + 1) * 64], in0=po.t[:, 256:320], scalar1=st_.t[:, 5:6]),
                            [po, st_], [ot])
                    yield
            ptr = pb[6]
            ptb = ptr.t[:].bitcast(BF16)
            ot_T = oT[bi % 2]
            for c in range(4):
                self.op("pe", lambda e, c=c, ot=ot, ptb=ptb: e.transpose(ptb[:, c * 128:(c + 1) * 128], ot.t[:, c * 128:(c + 1) * 128], self.identb.t[:]), [ot, self.identb], [ptr])
            self.op("act", lambda e, ot_T=ot_T, ptb=ptb: e.copy(out=ot_T.t[:], in_=ptb[:, 0:512]), [ptr], [ot_T])
            self.dma("pool", self.s_y[3][:, q0:q0 + 128].rearrange("(c p) t -> p c t", p=128), ot_T.t[:].rearrange("p (c t) -> p c t", c=4), reads=[ot_T])

    def mixers_cla(self, l, ctx_out):
        mk = self.mark()
        self.store_q = "pool"
        gens = [self.gen_conv(l), self.gen_lru(l), self.gen_att(l, ctx_out)]
        weights = [1, 8, 31]
        alive = [True, True, True]
        while any(alive):
            for gi in range(3):
                for _ in range(weights[gi]):
                    if not alive[gi]:
                        break
                    try:
                        next(gens[gi])
                    except StopIteration:
                        alive[gi] = False
        self.release(mk)

    def alias(self, parent, name, shape, dt=F32, off_bytes=0):
        v = parent.t
        flat = v
        base = self._tile_off[id(parent)] + off_bytes
        n = 1
        for d in shape[1:]:
            n *= d
        nbytes = n * (2 if dt == BF16 else 4)
        a = self.arena[0:shape[0], base // 4:(base + nbytes) // 4]
        if dt == BF16:
            a = a.bitcast(BF16)
        if len(shape) == 3:
            a = a.rearrange("p (a b) -> p a b", a=shape[1])
        t = T(a, name)
        t.b = parent.b
        self._tile_off[id(t)] = base
        return t

    @staticmethod
    def bc_inner(ap2, n):
        return AP(ap2.tensor, ap2.offset, [list(ap2.ap[0]), list(ap2.ap[1]), [0, n]])

    @staticmethod
    def bc_inner3(ap3, n):
        return AP(ap3.tensor, ap3.offset, [list(ap3.ap[0]), list(ap3.ap[1]), list(ap3.ap[2]), [0, n]])

    @staticmethod
    def bc_mid(ap2, h):
        return AP(ap2.tensor, ap2.offset, [list(ap2.ap[0]), [0, h], list(ap2.ap[1])])

    def mixer_gdn(self, l):
        w = self.w
        mk = self.mark()
        PQ = self.pquad
        gm = self.sb("gd_mask", (128, 8, 128), F32)
        self.dma("sp", gm.t[:], self.gdnmask_in.rearrange("p (a b) -> p a b", a=8), writes=[gm])
        cw = self.sb("gd_cw", (128, 2, 4, 12), F32)
        for d in range(2):
            for c in range(12):
                self.P.dma("sp", cw.t[:, d, :, c], w["gdn_conv_w"][l, d, :, c * 128:(c + 1) * 128].rearrange("k p -> p k"), [], [cw.b], allow_slow_non_contiguous=True)
        prow = self.sb("gd_prow", (1, 16 + 128), F32)
        self.dma("sp", prow.t[:, 0:8], w["gdn_a_log"][l:l + 1].rearrange("o d h -> o (d h)"), writes=[prow])
        self.dma("sp", prow.t[:, 8:16], w["gdn_dt_bias"][l:l + 1].rearrange("o d h -> o (d h)"), writes=[prow])
        self.dma("sp", prow.t[:, 16:144], w["gdn_norm_g"][l:l + 1, :], writes=[prow])
        pbc = self.sb("gd_pbc", (128, 144), F32)
        self.mm(PQ[0].t[:, 0:144], self.ones.t[0:1, :], prow.t[0:1, :], True, True, [self.ones, prow], [PQ[0]])
        self.op("act", lambda e: e.copy(out=pbc.t[:], in_=PQ[0].t[:, 0:144]), [PQ[0]], [pbc])
        self.op("act", lambda e: e.activation(out=pbc.t[:, 0:8], in_=pbc.t[:, 0:8], func=AF.Exp), [pbc], [pbc])
        self.op("act", lambda e: e.mul(out=pbc.t[:, 0:8], in_=pbc.t[:, 0:8], mul=-1.0), [pbc], [pbc])
        eps6 = self.sb("gd_eps", (128, 1), F32)
        self.op("pool", lambda e: e.memset(eps6.t[:], NORM_EPS), [], [eps6])
        S = self.sb("gd_S", (128, 2, 4, 128), F32)
        self.op("pool", lambda e: e.memset(S.t[:], 0.0), [], [S])
        ntile = self.TT // 128
        tiles = list(range(ntile))
        ofB = {}
        bc_inner, bc_mid = self.bc_inner, self.bc_mid

        def chain(d, hp):
            n_ = "gd%d%d_" % (d, hp)
            Xin = self.sb(n_ + "Xin", (128, 6, 131), F32)
            Ya = self.alias(Xin, n_ + "Ya", (128, 2, 256))
            U = self.sb(n_ + "U", (128, 6, 128), F32)
            Yb = self.alias(U, n_ + "Yb", (128, 2, 256))
            T1 = self.sb(n_ + "T1", (128, 2, 128), F32)
            R1 = self.sb(n_ + "R1", (128, 1024), F32)
            tmpU = self.alias(R1, n_ + "tmpU", (128, 6, 128))
            EE = self.alias(R1, n_ + "EE", (128, 2, 256))
            GE = self.alias(R1, n_ + "GE", (128, 2, 256), off_bytes=2048)
            abs_ = [self.sb(n_ + "ab%d" % i, (128, 16), F32) for i in range(2)]
            scs = [self.sb(n_ + "sc%d" % i, (128, 8, 2), F32) for i in range(2)]
            KQ = self.sb(n_ + "KQ", (128, 2, 256), F32)
            so = self.sb(n_ + "so", (128, 2, 256), F32)
            sol = self.sb(n_ + "sol", (128, 2, 256), F32)
            Ks = self.sb(n_ + "Ks", (128, 2, 128), F32)
            gL = self.sb(n_ + "gL", (128, 2, 128), F32)
            XX = self.sb(n_ + "XX", (128, 2, 256), F32)
            aT = self.sb(n_ + "aT", (128, 2, 128), F32)
            PP = self.sb(n_ + "PP", (128, 2, 256), F32)
            Xo = self.sb(n_ + "Xo", (128, 3, 2, 128), F32)
            wT = self.sb(n_ + "wT", (128, 2, 128), F32)
            vn = self.sb(n_ + "vn", (128, 2, 128), F32)
            o2 = self.sb(n_ + "o2", (128, 2, 128), F32)
            otok = self.sb(n_ + "ot", (128, 2, 128), F32)
            QA, QB = self.pbank[4 * d + 2 * hp], self.pbank[4 * d + 2 * hp + 1]
            qa4 = QA.t[:].rearrange("p (h c) -> p h c", h=2)
            qb4 = QB.t[:].rearrange("p (h c) -> p h c", h=2)
            qa8 = QA.t[:].rearrange("p (h c) -> p h c", h=4)
            SL, IU = gm.t[:, 2 * d, :], gm.t[:, 2 * d + 1, :]
            SL_b, IU_b = bc_mid(SL, 2), bc_mid(IU, 2)
            D16_b = bc_mid(gm.t[:, 4, :], 4)
            I_b = bc_mid(self.ident.t[:], 4)
            cw5 = cw.t[:].rearrange("p d k (s h) -> p d k s h", s=3)
            cwv = lambda kk: cw5[:, d, kk, :, 2 * hp:2 * hp + 2]
            order = tiles if d == 0 else [1, 0] + tiles[:1:-1]
            def s0(tl, sc, ab):
                t0 = tl * 128
                s0_, s1_ = (0, NCTX) if t0 < NCTX else (NCTX, self.TT)
                lo, hi = (max(s0_, t0 - 3), t0 + 128) if d == 0 else (t0, min(s1_, t0 + 131))
                base = t0 - 3 if d == 0 else t0

                def p0():
                    if hi - lo < 131:
                        self.op("pool", lambda e: e.memset(Xin.t[:], 0.0), [], [Xin])
                    for s3 in range(3):
                        r0 = (s3 * 4 + 2 * hp) * 128
                        self.dma("sp", Xin.t[:, 2 * s3:2 * s3 + 2, lo - base:hi - base], self.s_gqkv[r0:r0 + 256, lo:hi].rearrange("(c p) t -> p c t", p=128), writes=[Xin])
                    self.dma("sp", ab.t[:], self.s_ab[t0:t0 + 128, :], writes=[ab])
                    for kk in range(4):
                        off = kk if d == 0 else 3 - kk
                        wb_ = self.bc_inner3(cwv(kk), 128)
                        src = Xin.t[:, :, off:off + 128].rearrange("p (s h) t -> p s h t", s=3)
                        if kk == 0:
                            self.op("dve", lambda e, src=src, wb_=wb_: e.tensor_tensor(out=U.t[:].rearrange("p (s h) t -> p s h t", s=3), in0=src, in1=wb_, op=ALU.mult), [Xin, cw], [U])
                        else:
                            self.op("pool", lambda e, src=src, wb_=wb_: e.tensor_tensor(out=tmpU.t[:].rearrange("p (s h) t -> p s h t", s=3), in0=src, in1=wb_, op=ALU.mult), [Xin, cw], [tmpU])
                            self.op("dve", lambda e: e.tensor_tensor(out=U.t[:], in0=U.t[:], in1=tmpU.t[:], op=ALU.add), [U, tmpU], [U])

                def p1():
                    self.op("act", lambda e: e.activation(out=U.t[:], in_=U.t[:], func=AF.Silu), [U], [U])
                    self.op("act", lambda e: e.activation(out=tmpU.t[:, 0:4, :], in_=U.t[:, 0:4, :], func=AF.Square), [U], [tmpU])
                    for c in range(4):
                        self.mm(qa8[:, c, :], self.ones.t[:], tmpU.t[:, c, :], True, True, [self.ones, tmpU], [QA])
                    self.op("act", lambda e: e.activation(out=tmpU.t[:, 0:4, :], in_=qa8, func=AF.Ln, bias=eps6.t[:], scale=1.0), [QA, eps6], [tmpU])
                    self.op("act", lambda e: e.activation(out=tmpU.t[:, 0:4, :], in_=tmpU.t[:, 0:4, :], func=AF.Exp, scale=-0.5), [tmpU], [tmpU])

                def p2():
                    self.op("dve", lambda e: e.tensor_tensor(out=U.t[:, 0:4, :], in0=U.t[:, 0:4, :], in1=tmpU.t[:, 0:4, :], op=ALU.mult), [U, tmpU], [U])
                    self.op("act", lambda e: e.mul(out=U.t[:, 0:2, :], in_=U.t[:, 0:2, :], mul=128.0 ** -0.5), [U], [U])

                def p3():
                    a4 = ab.t[:, 4 * d + 2 * hp:4 * d + 2 * hp + 2]
                    b4 = ab.t[:, 8 + 4 * d + 2 * hp:10 + 4 * d + 2 * hp]
                    self.op("act", lambda e, b4=b4: e.activation(out=sc.t[:, 0, :], in_=b4, func=AF.Exp, scale=-1.0), [ab], [sc])
                    self.op("act", lambda e: e.activation(out=sc.t[:, 0, :], in_=sc.t[:, 0, :], func=AF.Ln, bias=1.0, scale=1.0), [sc], [sc])
                    self.op("act", lambda e: e.activation(out=sc.t[:, 0, :], in_=sc.t[:, 0, :], func=AF.Exp, scale=-1.0), [sc], [sc])
                    self.op("dve", lambda e, a4=a4: e.tensor_tensor(out=sc.t[:, 1, :], in0=a4, in1=pbc.t[:, 8 + 4 * d + 2 * hp:10 + 4 * d + 2 * hp], op=ALU.add), [ab, pbc], [sc])
                    self.op("act", lambda e: e.activation(out=sc.t[:, 1, :], in_=sc.t[:, 1, :], func=AF.Exp), [sc], [sc])
                    self.op("act", lambda e: e.activation(out=sc.t[:, 1, :], in_=sc.t[:, 1, :], func=AF.Ln, bias=1.0, scale=1.0), [sc], [sc])
                    self.op("dve", lambda e: e.tensor_tensor(out=sc.t[:, 1, :], in0=sc.t[:, 1, :], in1=pbc.t[:, 4 * d + 2 * hp:4 * d + 2 * hp + 2], op=ALU.mult), [sc, pbc], [sc])
                    self.mm(QB.t[:, 0:2], IU, sc.t[:, 1, :], True, True, [gm, sc], [QB])
                    self.mm(QB.t[:, 4:6], self.ones.t[:], sc.t[:, 1, :], True, True, [self.ones, sc], [QB])
                    self.op("act", lambda e: e.copy(out=sc.t[:, 2, :], in_=QB.t[:, 0:2]), [QB], [sc])
                    self.op("act", lambda e: e.activation(out=sc.t[:, 3, :], in_=QB.t[:, 0:2], func=AF.Exp), [QB], [sc])
                    self.op("act", lambda e: e.activation(out=sc.t[:, 7, :], in_=QB.t[:, 4:6], func=AF.Exp), [QB], [sc])
                    self.op("dve", lambda e: e.tensor_tensor(out=sc.t[:, 5, :], in0=QB.t[:, 4:6], in1=sc.t[:, 2, :], op=ALU.subtract), [QB, sc], [sc])
                    self.op("dve", lambda e: e.tensor_tensor(out=sc.t[:, 4, :], in0=sc.t[:, 0, :], in1=sc.t[:, 3, :], op=ALU.mult), [sc], [sc])
                    self.op("act", lambda e: e.activation(out=sc.t[:, 5, :], in_=sc.t[:, 5, :], func=AF.Exp), [sc], [sc])
                    self.op("dve", lambda e: e.tensor_scalar_mul(out=sc.t[:, 6, :], in0=sc.t[:, 0, :], scalar1=-1.0), [sc], [sc])
                return [p0, p1, p2, p3]

            nxt_pieces = s0(order[0], scs[0], abs_[0])
            for p_ in nxt_pieces:
                p_()
                yield
            for ti, tl in enumerate(order):
                t0 = tl * 128
                sc = scs[ti % 2]
                colb = lambda q, sc=sc: bc_inner(sc.t[:, q, :], 128)
                nxt_pieces = s0(order[ti + 1], scs[(ti + 1) % 2], abs_[(ti + 1) % 2]) if ti + 1 < len(order) else []
                self.op("act", lambda e: e.copy(out=KQ.t[:, :, 0:128], in_=U.t[:, 2:4, :]), [U], [KQ])
                self.op("pool", lambda e: e.tensor_copy(out=KQ.t[:, :, 128:256], in_=U.t[:, 0:2, :]), [U], [KQ])
                for h in range(2):
                    self.op("pe", lambda e, h=h: e.transpose(qa4[:, h, 0:128], U.t[:, 2 + h, :], self.ident.t[:]), [U, self.ident], [QA])
                    self.op("pe", lambda e, h=h: e.transpose(qa4[:, h, 128:256], U.t[:, 4 + h, :], self.ident.t[:]), [U, self.ident], [QA])
                self.op("dve", lambda e, colb=colb: e.tensor_tensor(out=so.t[:, :, 0:128], in0=qa4[:, :, 128:256], in1=colb(0), op=ALU.mult), [QA, sc], [so])
                self.op("dve", lambda e, colb=colb: e.tensor_tensor(out=so.t[:, :, 128:256], in0=qa4[:, :, 0:128], in1=colb(4), op=ALU.mult), [QA, sc], [so])
                self.op("dve", lambda e, colb=colb: e.tensor_tensor(out=Ks.t[:], in0=qa4[:, :, 0:128], in1=colb(5), op=ALU.mult), [QA, sc], [Ks])
                yield
                for h in range(2):
                    self.mm(qb4[:, h, :], KQ.t[:, h, 0:128], KQ.t[:, h, :], True, True, [KQ], [QB])
                self.op("dve", lambda e, colb=colb: e.tensor_tensor(out=gL.t[:], in0=SL_b, in1=colb(1), op=ALU.mult), [gm, sc], [gL])
                for h in range(2):
                    self.mm(qa4[:, h, 0:128], IU, gL.t[:, h, :], True, True, [gm, gL], [QA])
                    self.mm(qa4[:, h, 128:256], gL.t[:, h, :], IU, True, True, [gm, gL], [QA])
                self.op("act", lambda e: e.activation(out=EE.t[:], in_=qa4, func=AF.Exp), [QA], [EE])
                self.op("dve", lambda e: e.tensor_tensor(out=GE.t[:], in0=qb4, in1=EE.t[:], op=ALU.mult), [QB, EE], [GE])
                yield
                self.op("dve", lambda e, colb=colb: e.tensor_tensor(out=XX.t[:, :, 0:128], in0=GE.t[:, :, 0:128], in1=colb(6), op=ALU.mult), [GE, sc], [XX])
                self.op("dve", lambda e: e.tensor_tensor(out=XX.t[:, :, 0:128], in0=XX.t[:, :, 0:128], in1=SL_b, op=ALU.mult), [XX, gm], [XX])
                self.op("pool", lambda e: e.tensor_tensor(out=aT.t[:], in0=GE.t[:, :, 128:256], in1=IU_b, op=ALU.mult), [GE, gm], [aT])
                for h in range(2):
                    self.op("pe", lambda e, h=h: e.transpose(qb4[:, h, 0:128], XX.t[:, h, 0:128], self.ident.t[:]), [XX, self.ident], [QB])
                self.op("act", lambda e: e.copy(out=XX.t[:, :, 128:256], in_=qb4[:, :, 0:128]), [QB], [XX])
                xx8 = XX.t[:].rearrange("p h (a b) -> p (h a) b", a=2)
                ya8 = Ya.t[:].rearrange("p h (a b) -> p (h a) b", a=2)
                pp8 = PP.t[:].rearrange("p h (a b) -> p (h a) b", a=2)
                self.op("dve", lambda e, xx8=xx8, ya8=ya8: e.tensor_tensor(out=ya8, in0=xx8, in1=D16_b, op=ALU.mult), [XX, gm], [Ya])
                for q in range(3):
                    self.op("pool", lambda e, q=q: e.tensor_tensor(out=Xo.t[:, q], in0=XX.t[:, :, 128:256], in1=bc_mid(gm.t[:, 5 + q, :], 2), op=ALU.mult), [XX, gm], [Xo])
                self.op("dve", lambda e, ya8=ya8, pp8=pp8: e.tensor_tensor(out=pp8, in0=ya8, in1=I_b, op=ALU.add), [Ya, self.ident], [PP])
                yield
                cur, nxt = Ya, Yb
                for lev in range(3):
                    for h in range(2):
                        self.mm(qa4[:, h, 0:128], cur.t[:, h, 128:256], cur.t[:, h, 0:128], True, True, [cur], [QA])
                        self.mm(qa4[:, h, 128:256], cur.t[:, h, 0:128], cur.t[:, h, 128:256], True, True, [cur], [QA])
                    self.op("act", lambda e, nxt=nxt: e.copy(out=nxt.t[:], in_=qa4), [QA], [nxt])
                    cur, nxt = nxt, cur
                    for h in range(2):
                        self.mm(qb4[:, h, 0:128], cur.t[:, h, 128:256], PP.t[:, h, 0:128], True, True, [cur, PP], [QB])
                        self.mm(qb4[:, h, 128:256], PP.t[:, h, 0:128], cur.t[:, h, 128:256], True, True, [cur, PP], [QB])
                    self.op("dve", lambda e: e.tensor_tensor(out=PP.t[:], in0=PP.t[:], in1=qb4, op=ALU.add), [PP, QB], [PP])
                    yield
                if nxt_pieces:
                    nxt_pieces[0]()
                for q in range(3):
                    for h in range(2):
                        self.mm(qa4[:, h, 0:128], Xo.t[:, q, h, :], PP.t[:, h, 0:128], True, True, [Xo, PP], [QA])
                    self.op("act", lambda e: e.copy(out=T1.t[:], in_=qa4[:, :, 0:128]), [QA], [T1])
                    for h in range(2):
                        self.mm(qb4[:, h, 0:128], PP.t[:, h, 128:256], T1.t[:, h, :], True, True, [PP, T1], [QB])
                    self.op("dve", lambda e: e.tensor_tensor(out=PP.t[:, :, 0:128], in0=PP.t[:, :, 0:128], in1=qb4[:, :, 0:128], op=ALU.add), [PP, QB], [PP])
                    for h in range(2):
                        self.op("pe", lambda e, h=h: e.transpose(qa4[:, h, 128:256], PP.t[:, h, 0:128], self.ident.t[:]), [PP, self.ident], [QA])
                    self.op("act", lambda e: e.copy(out=PP.t[:, :, 128:256], in_=qa4[:, :, 128:256]), [QA], [PP])
                    if nxt_pieces:
                        nxt_pieces[1 + q]()
                    yield
                for h in range(2):
                    self.mm(qb4[:, h, :], PP.t[:, h, 128:256], so.t[:, h, :], True, True, [PP, so], [QB])
                self.op("act", lambda e: e.copy(out=sol.t[:], in_=qb4), [QB], [sol])
                for h in range(2):
                    self.op("pe", lambda e, h=h: e.transpose(qa4[:, h, 0:128], sol.t[:, h, 128:256], self.ident.t[:]), [sol, self.ident], [QA])
                self.op("act", lambda e: e.copy(out=wT.t[:], in_=qa4[:, :, 0:128]), [QA], [wT])
                yield
                Sd = S.t[:, d, 2 * hp:2 * hp + 2]
                for h in range(2):
                    self.mm(qb4[:, h, 0:128], wT.t[:, h, :], Sd[:, h, :], True, True, [wT, S], [QB])
                self.op("dve", lambda e: e.tensor_tensor(out=vn.t[:], in0=sol.t[:, :, 0:128], in1=qb4[:, :, 0:128], op=ALU.subtract), [sol, QB], [vn])
                for h in range(2):
                    self.mm(qa4[:, h, 0:128], KQ.t[:, h, 128:256], Sd[:, h, :], True, True, [KQ, S], [QA])
                    self.mm(qa4[:, h, 128:256], aT.t[:, h, :], vn.t[:, h, :], True, True, [aT, vn], [QA])
                self.op("act", lambda e: e.copy(out=o2.t[:], in_=qa4[:, :, 128:256]), [QA], [o2])
                self.op("dve", lambda e, colb=colb: e.tensor_tensor(out=otok.t[:], in0=qa4[:, :, 0:128], in1=colb(3), op=ALU.mult), [QA, sc], [otok])
                self.op("dve", lambda e: e.tensor_tensor(out=otok.t[:], in0=otok.t[:], in1=o2.t[:], op=ALU.add), [otok, o2], [otok])
                for h in range(2):
                    self.mm(qb4[:, h, 128:256], Ks.t[:, h, :], vn.t[:, h, :], True, True, [Ks, vn], [QB])
                self.op("pool", lambda e, colb=colb, Sd=Sd: e.tensor_tensor(out=Sd, in0=Sd, in1=colb(7), op=ALU.mult), [S, sc], [S])
                self.op("dve", lambda e, Sd=Sd: e.tensor_tensor(out=Sd, in0=Sd, in1=qb4[:, :, 128:256], op=ALU.add), [S, QB], [S])
                rows = slice(t0, t0 + 128)
                cs = slice(hp * 256, (hp + 1) * 256)
                otf = otok.t[:].rearrange("p h c -> p (h c)")
                B = Buf("o")
                ofB[(d, tl, hp)] = B
                self.P.dma("sp", (self.s_of if d == 0 else self.s_ob)[rows, cs], otf, [otok.b], [B])
                yield "T"

        e_of = self.sb("gde_of", (128, 512), F32)
        e_ob = self.sb("gde_ob", (128, 512), F32)
        e_gz = self.sb("gde_gz", (128, 512), F32)
        e_sq = self.sb("gde_sq", (128, 512), F32)
        e_ms = self.sb("gde_ms", (128, 8), F32)
        e_y = self.sb("gde_y", (128, 512), BF16)
        e_yT = self.sb("gde_yT", (128, 512), BF16)
        ebank = self.pbank[7]

        def epilogue(tl):
            t0 = tl * 128
            rows = slice(t0, t0 + 128)
            self.P.dma("sp", e_of.t[:], self.s_of[rows, :], [ofB[(0, tl, 0)], ofB[(0, tl, 1)]], [e_of.b])
            self.P.dma("sp", e_ob.t[:], self.s_ob[rows, :], [ofB[(1, tl, 0)], ofB[(1, tl, 1)]], [e_ob.b])
            self.dma("sp", e_gz.t[:], self.s_gz[rows, :], writes=[e_gz])
            self.op("dve", lambda e: e.tensor_tensor(out=e_of.t[:], in0=e_of.t[:], in1=e_ob.t[:], op=ALU.add), [e_of, e_ob], [e_of])
            for h in range(4):
                hs = slice(h * 128, (h + 1) * 128)
                self.op("act", lambda e, hs=hs, h=h: e.activation(out=e_sq.t[:, hs], in_=e_of.t[:, hs], func=AF.Square, accum_out=e_ms.t[:, h:h + 1]), [e_of], [e_sq, e_ms])
            self.op("act", lambda e: e.activation(out=e_ms.t[:, 4:8], in_=e_ms.t[:, 0:4], func=AF.Sqrt, bias=eps6.t[:], scale=1.0 / 128), [e_ms, eps6], [e_ms])
            self.op("dve", lambda e: e.reciprocal(out=e_ms.t[:, 4:8], in_=e_ms.t[:, 4:8]), [e_ms], [e_ms])
            of4 = e_of.t[:].rearrange("p (h c) -> p h c", h=4)
            self.op("dve", lambda e: e.tensor_tensor(out=of4, in0=of4, in1=bc_inner(e_ms.t[:, 4:8], 128), op=ALU.mult), [e_of, e_ms], [e_of])
            self.op("pool", lambda e: e.tensor_tensor(out=of4, in0=of4, in1=bc_mid(pbc.t[:, 16:144], 4), op=ALU.mult), [e_of, pbc], [e_of])
            self.op("dve", lambda e: e.tensor_tensor(out=e_y.t[:], in0=e_of.t[:], in1=e_gz.t[:], op=ALU.mult), [e_of, e_gz], [e_y])
            ptb = ebank.t[:, 0:256].bitcast(BF16)
            for c in range(4):
                self.op("pe", lambda e, c=c: e.transpose(ptb[:, c * 128:(c + 1) * 128], e_y.t[:, c * 128:(c + 1) * 128], self.identb.t[:]), [e_y, self.identb], [ebank])
            self.op("act", lambda e: e.copy(out=e_yT.t[:], in_=ptb[:, 0:512]), [ebank], [e_yT])
            self.dma("sp", self.s_y[2][:, t0:t0 + 128].rearrange("(c p) t -> p c t", p=128), e_yT.t[:].rearrange("p (c t) -> p c t", c=4), reads=[e_yT])

        gens = [chain(0, 0), chain(0, 1), chain(1, 0), chain(1, 1)]
        offs = [0, 6, 3, 9]
        alive = [True] * 4
        done_tiles = [0] * 4
        order_b = [1, 0] + tiles[:1:-1]
        pos_f = {t: i for i, t in enumerate(tiles)}
        pos_b = {t: i for i, t in enumerate(order_b)}
        ready_at = sorted(tiles, key=lambda t: max(pos_f[t], pos_b[t]))
        ep_i = 0
        rnd = 0
        while any(alive) or ep_i < ntile:
            for gi in range(4):
                if not alive[gi] or rnd < offs[gi]:
                    continue
                try:
                    if next(gens[gi]) == "T":
                        done_tiles[gi] += 1
                except StopIteration:
                    alive[gi] = False
            if ep_i < ntile:
                t = ready_at[ep_i]
                if min(done_tiles[0], done_tiles[1]) > pos_f[t] and min(done_tiles[2], done_tiles[3]) > pos_b[t]:
                    epilogue(t)
                    ep_i += 1
                elif not any(alive):
                    raise RuntimeError("gdn epilogue gating never satisfied")
            rnd += 1
        self.release(mk)

    @staticmethod
    def _gdn_fwd_needed(bsteps, nst, ntile):
        order = [1, 0] + list(range(ntile - 1, 1, -1))
        tl = order[min(bsteps // nst, ntile - 1)]
        return (tl + 1) * nst


    def mixer_gdn_old(self, l):
        w = self.w
        mk = self.mark()
        pb = self.pbank
        gm = self.sb("gd_mask", (128, 8, 128), F32)
        self.dma("sp", gm.t[:], self.gdnmask_in.rearrange("p (a b) -> p a b", a=8), writes=[gm])
        cw = self.sb("gd_cw", (128, 2, 12, 4), F32)
        for d in range(2):
            for c in range(12):
                self.P.dma("sp", cw.t[:, d, c, :], w["gdn_conv_w"][l, d, :, c * 128:(c + 1) * 128].rearrange("k p -> p k"), [], [cw.b], allow_slow_non_contiguous=True)
        prow = self.sb("gd_prow", (1, 16 + 128), F32)
        self.dma("sp", prow.t[:, 0:8], w["gdn_a_log"][l:l + 1].rearrange("o d h -> o (d h)"), writes=[prow])
        self.dma("sp", prow.t[:, 8:16], w["gdn_dt_bias"][l:l + 1].rearrange("o d h -> o (d h)"), writes=[prow])
        self.dma("sp", prow.t[:, 16:144], w["gdn_norm_g"][l:l + 1, :], writes=[prow])
        pbc = self.sb("gd_pbc", (128, 144), F32)
        self.mm(pb[0].t[:, 0:144], self.ones.t[0:1, :], prow.t[0:1, :], True, True, [self.ones, prow], [pb[0]])
        self.op("act", lambda e: e.copy(out=pbc.t[:], in_=pb[0].t[:, 0:144]), [pb[0]], [pbc])
        self.op("act", lambda e: e.activation(out=pbc.t[:, 0:8], in_=pbc.t[:, 0:8], func=AF.Exp), [pbc], [pbc])
        self.op("act", lambda e: e.mul(out=pbc.t[:, 0:8], in_=pbc.t[:, 0:8], mul=-1.0), [pbc], [pbc])
        eps6 = self.sb("gd_eps", (128, 1), F32)
        self.op("pool", lambda e: e.memset(eps6.t[:], NORM_EPS), [], [eps6])
        S = self.sb("gd_S", (128, 2, 4, 128), F32)
        self.op("pool", lambda e: e.memset(S.t[:], 0.0), [], [S])
        xin = [self.sb("gd_x%d" % i, (128, 131), F32) for i in range(3)]
        U = self.sb("gd_U", (128, 12, 128), F32)
        sq = [self.sb("gd_sq%d" % i, (128, 128), F32) for i in range(2)]
        ab = self.sb("gd_ab", (128, 16), F32)
        sc = self.sb("gd_sc", (128, 8, 4), F32)
        KQ = [self.sb("gd_KQ%d" % i, (128, 256), F32) for i in range(2)]
        Ktok = [self.sb("gd_Ks%d" % i, (128, 128), F32) for i in range(2)]
        sol = [self.sb("gd_sol%d" % i, (128, 256), F32) for i in range(4)]
        XX = [self.sb("gd_XX%d" % i, (128, 256), F32) for i in range(4)]
        Yb = [self.sb("gd_Yb%d" % i, (128, 256), F32) for i in range(2)]
        PPt = [self.sb("gd_PP%d" % i, (128, 256), F32) for i in range(2)]
        Xo = [self.sb("gd_Xo%d" % i, (128, 3, 128), F32) for i in range(2)]
        T1t = [self.sb("gd_T1%d" % i, (128, 128), F32) for i in range(2)]
        EN = [self.sb("gd_EN%d" % i, (128, 128), F32) for i in range(2)]
        ET = [self.sb("gd_ET%d" % i, (128, 128), F32) for i in range(2)]
        gL = [self.sb("gd_gL%d" % i, (128, 128), F32) for i in range(2)]
        aT = [self.sb("gd_aT%d" % i, (128, 128), F32) for i in range(2)]
        wT = [self.sb("gd_wT%d" % i, (128, 128), F32) for i in range(2)]
        vn = [self.sb("gd_vn%d" % i, (128, 128), F32) for i in range(2)]
        o2 = [self.sb("gd_o2%d" % i, (128, 128), F32) for i in range(2)]
        otok = [self.sb("gd_o%d" % i, (128, 512), F32) for i in range(2)]
        ofl = [self.sb("gd_of%d" % i, (128, 512), F32) for i in range(2)]
        gz = [self.sb("gd_gz%d" % i, (128, 512), F32) for i in range(2)]
        ms = self.sb("gd_ms", (128, 8), F32)
        ybt = [self.sb("gd_y%d" % i, (128, 512), BF16) for i in range(2)]
        yT = [self.sb("gd_yT%d" % i, (128, 512), BF16) for i in range(2)]
        ntile = self.TT // 128
        tiles = list(range(ntile))
        ofB = {}
        kx = 0
        hk = 0
        for d in range(2):
            SL, IU = gm.t[:, 2 * d, :], gm.t[:, 2 * d + 1, :]
            order = tiles if d == 0 else [1, 0] + tiles[:1:-1]
            for ti, tl in enumerate(order):
                t0 = tl * 128
                s0, s1 = (0, NCTX) if t0 < NCTX else (NCTX, self.TT)
                lo, hi = (max(s0, t0 - 3), t0 + 128) if d == 0 else (t0, min(s1, t0 + 131))
                base = t0 - 3 if d == 0 else t0
                for c in range(12):
                    x = xin[kx % 3]
                    kx += 1
                    if hi - lo < 131:
                        self.op("pool", lambda e, x=x: e.memset(x.t[:], 0.0), [], [x])
                    self.dma("sp", x.t[:, lo - base:hi - base], self.s_gqkv[c * 128:(c + 1) * 128, lo:hi], writes=[x])
                    for kk in range(4):
                        off = kk if d == 0 else 3 - kk
                        if kk == 0:
                            self.op("dve", lambda e, x=x, off=off, c=c, d=d: e.tensor_scalar_mul(out=U.t[:, c, :], in0=x.t[:, off:off + 128], scalar1=cw.t[:, d, c, 0:1]), [x, cw], [U])
                        else:
                            self.op("dve", lambda e, x=x, off=off, c=c, d=d, kk=kk: e.scalar_tensor_tensor(out=U.t[:, c, :], in0=x.t[:, off:off + 128], scalar=cw.t[:, d, c, kk:kk + 1],
                                                                                            in1=U.t[:, c, :], op0=ALU.mult, op1=ALU.add), [x, cw, U], [U])
                    self.op("act", lambda e, c=c: e.activation(out=U.t[:, c, :], in_=U.t[:, c, :], func=AF.Silu), [U], [U])
                for c in range(8):
                    q_ = sq[c % 2]
                    pn = pb[c % 2]
                    self.op("act", lambda e, c=c, q_=q_: e.activation(out=q_.t[:], in_=U.t[:, c, :], func=AF.Square), [U], [q_])
                    self.mm(pn.t[:, 0:128], self.ones.t[:], q_.t[:], True, True, [self.ones, q_], [pn])
                    self.op("act", lambda e, q_=q_, pn=pn: e.activation(out=q_.t[:], in_=pn.t[:, 0:128], func=AF.Sqrt, bias=eps6.t[:], scale=1.0), [pn, eps6], [q_])
                    self.op("dve", lambda e, q_=q_: e.reciprocal(out=q_.t[:], in_=q_.t[:]), [q_], [q_])
                    cst = 128.0 ** -0.5 if c < 4 else 1.0
                    self.op("dve", lambda e, c=c, q_=q_, cst=cst: e.scalar_tensor_tensor(out=U.t[:, c, :], in0=U.t[:, c, :], scalar=cst, in1=q_.t[:], op0=ALU.mult, op1=ALU.mult), [U, q_], [U])
                self.dma("sp", ab.t[:], self.s_ab[t0:t0 + 128, :], writes=[ab])
                a4 = ab.t[:, 4 * d:4 * d + 4]
                b4 = ab.t[:, 8 + 4 * d:12 + 4 * d]
                self.op("act", lambda e, b4=b4: e.activation(out=sc.t[:, 0, :], in_=b4, func=AF.Sigmoid), [ab], [sc])
                self.op("dve", lambda e, a4=a4, d=d: e.tensor_tensor(out=sc.t[:, 1, :], in0=a4, in1=pbc.t[:, 8 + 4 * d:12 + 4 * d], op=ALU.add), [ab, pbc], [sc])
                self.op("act", lambda e: e.activation(out=sc.t[:, 1, :], in_=sc.t[:, 1, :], func=AF.Exp), [sc], [sc])
                self.op("act", lambda e: e.activation(out=sc.t[:, 1, :], in_=sc.t[:, 1, :], func=AF.Ln, bias=1.0, scale=1.0), [sc], [sc])
                self.op("dve", lambda e, d=d: e.tensor_tensor(out=sc.t[:, 1, :], in0=sc.t[:, 1, :], in1=pbc.t[:, 4 * d:4 * d + 4], op=ALU.mult), [sc, pbc], [sc])
                pg = pb[2]
                self.mm(pg.t[:, 0:4], IU, sc.t[:, 1, :], True, True, [gm, sc], [pg])
                self.mm(pg.t[:, 4:8], self.ones.t[:], sc.t[:, 1, :], True, True, [self.ones, sc], [pg])
                self.op("act", lambda e, pg=pg: e.copy(out=sc.t[:, 2, :], in_=pg.t[:, 0:4]), [pg], [sc])
                self.op("act", lambda e, pg=pg: e.activation(out=sc.t[:, 3, :], in_=pg.t[:, 0:4], func=AF.Exp), [pg], [sc])
                self.op("act", lambda e, pg=pg: e.activation(out=sc.t[:, 7, :], in_=pg.t[:, 4:8], func=AF.Exp), [pg], [sc])
                self.op("dve", lambda e: e.tensor_tensor(out=sc.t[:, 4, :], in0=sc.t[:, 0, :], in1=sc.t[:, 3, :], op=ALU.mult), [sc], [sc])
                self.op("dve", lambda e, pg=pg: e.tensor_tensor(out=sc.t[:, 5, :], in0=pg.t[:, 4:8], in1=sc.t[:, 2, :], op=ALU.subtract), [pg, sc], [sc])
                self.op("act", lambda e: e.activation(out=sc.t[:, 5, :], in_=sc.t[:, 5, :], func=AF.Exp), [sc], [sc])
                self.op("dve", lambda e: e.tensor_scalar_mul(out=sc.t[:, 6, :], in0=sc.t[:, 0, :], scalar1=-1.0), [sc], [sc])
                ot = otok[ti % 2]
                for h in range(4):
                    j = hk % 2
                    hk += 1
                    QTc, KTc, VTc = U.t[:, h, :], U.t[:, 4 + h, :], U.t[:, 8 + h, :]
                    kq, ks, so, en, et, gl_, at_, wt_, vn_, o2_ = KQ[j], Ktok[j], sol[2 * j], EN[j], ET[j], gL[j], aT[j], wT[j], vn[j], o2[j]
                    so2 = sol[2 * j + 1]
                    xa, xb = XX[2 * j], XX[2 * j + 1]
                    col = lambda q, h=h: sc.t[:, q, h:h + 1]
                    self.op("pool", lambda e, kq=kq, KTc=KTc: e.tensor_copy(out=kq.t[:, 0:128], in_=KTc), [U], [kq])
                    self.op("pool", lambda e, kq=kq, QTc=QTc: e.tensor_copy(out=kq.t[:, 128:256], in_=QTc), [U], [kq])
                    p0, p1, p2, p3 = pb[4 * j], pb[4 * j + 1], pb[4 * j + 2], pb[4 * j + 3]
                    self.op("pe", lambda e, p0=p0, KTc=KTc: e.transpose(p0.t[:, 0:128], KTc, self.ident.t[:]), [U, self.ident], [p0])
                    self.op("pe", lambda e, p0=p0, VTc=VTc: e.transpose(p0.t[:, 128:256], VTc, self.ident.t[:]), [U, self.ident], [p0])
                    self.op("dve", lambda e, so=so, p0=p0, col=col: e.tensor_scalar_mul(out=so.t[:, 0:128], in0=p0.t[:, 128:256], scalar1=col(0)), [p0, sc], [so])
                    self.op("dve", lambda e, so=so, p0=p0, col=col: e.tensor_scalar_mul(out=so.t[:, 128:256], in0=p0.t[:, 0:128], scalar1=col(4)), [p0, sc], [so])
                    self.op("act", lambda e, ks=ks, p0=p0, col=col: e.mul(out=ks.t[:], in_=p0.t[:, 0:128], mul=col(5)), [p0, sc], [ks])
                    self.mm(p1.t[:, 0:256], KTc, kq.t[:], True, True, [U, kq], [p1])
                    self.op("dve", lambda e, gl_=gl_, col=col, SL=SL: e.tensor_scalar_mul(out=gl_.t[:], in0=SL, scalar1=col(1)), [gm, sc], [gl_])
                    self.mm(p2.t[:, 0:128], IU, gl_.t[:], True, True, [gm, gl_], [p2])
                    self.mm(p2.t[:, 128:256], gl_.t[:], IU, True, True, [gm, gl_], [p2])
                    self.op("act", lambda e, en=en, p2=p2: e.activation(out=en.t[:], in_=p2.t[:, 0:128], func=AF.Exp), [p2], [en])
                    self.op("act", lambda e, et=et, p2=p2: e.activation(out=et.t[:], in_=p2.t[:, 128:256], func=AF.Exp), [p2], [et])
                    self.op("dve", lambda e, en=en, p1=p1: e.tensor_tensor(out=en.t[:], in0=p1.t[:, 0:128], in1=en.t[:], op=ALU.mult), [p1, en], [en])
                    self.op("dve", lambda e, en=en, xa=xa, col=col, SL=SL: e.scalar_tensor_tensor(out=xa.t[:, 0:128], in0=en.t[:], scalar=col(6), in1=SL, op0=ALU.mult, op1=ALU.mult),
                            [en, sc, gm], [xa])
                    self.op("dve", lambda e, et=et, p1=p1: e.tensor_tensor(out=et.t[:], in0=p1.t[:, 128:256], in1=et.t[:], op=ALU.mult), [p1, et], [et])
                    self.op("pool", lambda e, et=et, at_=at_, IU=IU: e.tensor_tensor(out=at_.t[:], in0=et.t[:], in1=IU, op=ALU.mult), [et, gm], [at_])
                    self.op("pe", lambda e, p3=p3, xa=xa: e.transpose(p3.t[:, 0:128], xa.t[:, 0:128], self.ident.t[:]), [xa, self.ident], [p3])
                    self.op("act", lambda e, xa=xa, p3=p3: e.copy(out=xa.t[:, 128:256], in_=p3.t[:, 0:128]), [p3], [xa])
                    D16 = gm.t[:, 4, :]
                    ya, yb_, pp_, xo_, t1_ = xb, Yb[j], PPt[j], Xo[j], T1t[j]
                    self.op("pool", lambda e, xa=xa, ya=ya, D16=D16: e.tensor_tensor(out=ya.t[:, 0:128], in0=xa.t[:, 0:128], in1=D16, op=ALU.mult), [xa, gm], [ya])
                    self.op("pool", lambda e, xa=xa, ya=ya, D16=D16: e.tensor_tensor(out=ya.t[:, 128:256], in0=xa.t[:, 128:256], in1=D16, op=ALU.mult), [xa, gm], [ya])
                    for q in range(3):
                        self.op("pool", lambda e, xa=xa, xo_=xo_, q=q: e.tensor_tensor(out=xo_.t[:, q, :], in0=xa.t[:, 128:256], in1=gm.t[:, 5 + q, :], op=ALU.mult), [xa, gm], [xo_])
                    self.op("dve", lambda e, ya=ya, pp_=pp_: e.tensor_tensor(out=pp_.t[:, 0:128], in0=ya.t[:, 0:128], in1=self.ident.t[:], op=ALU.add), [ya, self.ident], [pp_])
                    self.op("dve", lambda e, ya=ya, pp_=pp_: e.tensor_tensor(out=pp_.t[:, 128:256], in0=ya.t[:, 128:256], in1=self.ident.t[:], op=ALU.add), [ya, self.ident], [pp_])
                    cur, nxt = ya, yb_
                    for lev in range(3):
                        self.mm(p2.t[:, 0:128], cur.t[:, 128:256], cur.t[:, 0:128], True, True, [cur], [p2])
                        self.mm(p2.t[:, 128:256], cur.t[:, 0:128], cur.t[:, 128:256], True, True, [cur], [p2])
                        self.op("act", lambda e, nxt=nxt, p2=p2: e.copy(out=nxt.t[:], in_=p2.t[:, 0:256]), [p2], [nxt])
                        cur, nxt = nxt, cur
                        self.mm(p1.t[:, 0:128], cur.t[:, 128:256], pp_.t[:, 0:128], True, True, [cur, pp_], [p1])
                        self.mm(p1.t[:, 128:256], pp_.t[:, 0:128], cur.t[:, 128:256], True, True, [cur, pp_], [p1])
                        self.op("dve", lambda e, pp_=pp_, p1=p1: e.tensor_tensor(out=pp_.t[:], in0=pp_.t[:], in1=p1.t[:, 0:256], op=ALU.add), [pp_, p1], [pp_])
                    for q in range(3):
                        self.mm(p2.t[:, 0:128], xo_.t[:, q, :], pp_.t[:, 0:128], True, True, [xo_, pp_], [p2])
                        self.op("act", lambda e, t1_=t1_, p2=p2: e.copy(out=t1_.t[:], in_=p2.t[:, 0:128]), [p2], [t1_])
                        self.mm(p1.t[:, 0:128], pp_.t[:, 128:256], t1_.t[:], True, True, [pp_, t1_], [p1])
                        self.op("dve", lambda e, pp_=pp_, p1=p1: e.tensor_tensor(out=pp_.t[:, 0:128], in0=pp_.t[:, 0:128], in1=p1.t[:, 0:128], op=ALU.add), [pp_, p1], [pp_])
                        self.op("pe", lambda e, p2=p2, pp_=pp_: e.transpose(p2.t[:, 128:256], pp_.t[:, 0:128], self.ident.t[:]), [pp_, self.ident], [p2])
                        self.op("act", lambda e, pp_=pp_, p2=p2: e.copy(out=pp_.t[:, 128:256], in_=p2.t[:, 128:256]), [p2], [pp_])
                    self.mm(p1.t[:, 0:256], pp_.t[:, 128:256], so.t[:], True, True, [pp_, so], [p1])
                    self.op("act", lambda e, so2=so2, p1=p1: e.copy(out=so2.t[:], in_=p1.t[:, 0:256]), [p1], [so2])
                    scur = so2
                    self.op("pe", lambda e, p3=p3, scur=scur: e.transpose(p3.t[:, 0:128], scur.t[:, 128:256], self.ident.t[:]), [scur, self.ident], [p3])
                    self.op("act", lambda e, wt_=wt_, p3=p3: e.copy(out=wt_.t[:], in_=p3.t[:, 0:128]), [p3], [wt_])
                    Sh = S.t[:, d, h, :]
                    self.mm(p0.t[:, 0:128], wt_.t[:], Sh, True, True, [wt_, S], [p0])
                    self.op("dve", lambda e, vn_=vn_, scur=scur, p0=p0: e.tensor_tensor(out=vn_.t[:], in0=scur.t[:, 0:128], in1=p0.t[:, 0:128], op=ALU.subtract), [scur, p0], [vn_])
                    self.mm(p0.t[:, 128:256], QTc, Sh, True, True, [U, S], [p0])
                    self.mm(p0.t[:, 256:384], at_.t[:], vn_.t[:], True, True, [at_, vn_], [p0])
                    self.op("act", lambda e, o2_=o2_, p0=p0: e.copy(out=o2_.t[:], in_=p0.t[:, 256:384]), [p0], [o2_])
                    self.op("dve", lambda e, ot=ot, h=h, p0=p0, o2_=o2_, col=col: e.scalar_tensor_tensor(out=ot.t[:, h * 128:(h + 1) * 128], in0=p0.t[:, 128:256], scalar=col(3),
                                                                                          in1=o2_.t[:], op0=ALU.mult, op1=ALU.add), [p0, sc, o2_], [ot])
                    self.mm(p3.t[:, 128:256], ks.t[:], vn_.t[:], True, True, [ks, vn_], [p3])
                    self.op("dve", lambda e, Sh=Sh, p3=p3, col=col: e.scalar_tensor_tensor(out=Sh, in0=Sh, scalar=col(7), in1=p3.t[:, 128:256], op0=ALU.mult, op1=ALU.add),
                            [S, sc, p3], [S])
                rows = slice(t0, t0 + 128)
                if d == 0:
                    B = Buf("of")
                    ofB[tl] = B
                    self.P.dma("sp", self.s_of[rows, :], ot.t[:], [ot.b], [B])
                else:
                    f_, z_, y_, yT_ = ofl[ti % 2], gz[ti % 2], ybt[ti % 2], yT[ti % 2]
                    self.P.dma("sp", f_.t[:], self.s_of[rows, :], [ofB[tl]], [f_.b])
                    self.dma("sp", z_.t[:], self.s_gz[rows, :], writes=[z_])
                    self.op("dve", lambda e, f_=f_, ot=ot: e.tensor_tensor(out=f_.t[:], in0=f_.t[:], in1=ot.t[:], op=ALU.add), [f_, ot], [f_])
                    for h in range(4):
                        hs = slice(h * 128, (h + 1) * 128)
                        self.op("act", lambda e, ot=ot, f_=f_, hs=hs, h=h: e.activation(out=ot.t[:, hs], in_=f_.t[:, hs], func=AF.Square, accum_out=ms.t[:, h:h + 1]), [f_], [ot, ms])
                    self.op("act", lambda e: e.activation(out=ms.t[:, 4:8], in_=ms.t[:, 0:4], func=AF.Sqrt, bias=eps6.t[:], scale=1.0 / 128), [ms, eps6], [ms])
                    self.op("dve", lambda e: e.reciprocal(out=ms.t[:, 4:8], in_=ms.t[:, 4:8]), [ms], [ms])
                    for h in range(4):
                        hs = slice(h * 128, (h + 1) * 128)
                        self.op("dve", lambda e, f_=f_, hs=hs, h=h: e.scalar_tensor_tensor(out=f_.t[:, hs], in0=f_.t[:, hs], scalar=ms.t[:, 4 + h:5 + h], in1=pbc.t[:, 16:144],
                                                                                    op0=ALU.mult, op1=ALU.mult), [f_, ms, pbc], [f_])
                    self.op("dve", lambda e, f_=f_, z_=z_, y_=y_: e.tensor_tensor(out=y_.t[:], in0=f_.t[:], in1=z_.t[:], op=ALU.mult), [f_, z_], [y_])
                    ptr = pb[3]
                    ptb = ptr.t[:].bitcast(BF16)
                    for c in range(4):
                        self.op("pe", lambda e, c=c, y_=y_, ptb=ptb: e.transpose(ptb[:, c * 128:(c + 1) * 128], y_.t[:, c * 128:(c + 1) * 128], self.identb.t[:]), [y_, self.identb], [ptr])
                    self.op("act", lambda e, yT_=yT_, ptb=ptb: e.copy(out=yT_.t[:], in_=ptb[:, 0:512]), [ptr], [yT_])
                    self.dma("pool", self.s_y[2][:, t0:t0 + 128].rearrange("(c p) t -> p c t", p=128), yT_.t[:].rearrange("p (c t) -> p c t", c=4), reads=[yT_])
        self.release(mk)

    def finish(self):
        self.P.emit()
        self.st.close()
        return self.nc

    def xrows(self, src_kind, tok0, t):
        a = tok0 + t * 128
        if src_kind == "xs":
            return self.xs[a:a + 128, :]
        if tok0 < NCTX:
            return self.ctx_in[a:a + 128, :]
        if src_kind == "in":
            return self.x_in[a - NCTX:a - NCTX + 128, :]
        return self.out[a - NCTX:a - NCTX + 128, :]

    def phase_ffn(self, l, f, j, src_kind, dst_kind, skip_ctx=False):
        mk = self.mark()
        self.store_q = "act"
        self.alloc_dense()
        for (tok0, gw, is_ctx) in self.groups:
            if is_ctx and skip_ctx:
                continue
            nt = gw // 128
            for t in range(nt):
                self.dma("sp", self.xt[t].t[:], self.xrows(src_kind, tok0, t), writes=[self.xt[t]])
            self.ffn(l, f, j, 1 if is_ctx else 0, self.xt, gw)
            for t in range(nt):
                dk = "xs" if (is_ctx or dst_kind == "xs") else "out"
                self.dma("pool", self.xrows(dk, tok0, t), self.xt[t].t[:], reads=[self.xt[t]])
        self.release(mk)

    def phase_a2(self, l):
        mk = self.mark()
        self.store_q = "act"
        self.alloc_ln()
        self.alloc_a2()
        for (tok0, gw, is_ctx) in self.groups:
            nt = gw // 128
            r = 1 if is_ctx else 0
            for t in range(nt):
                self.dma("sp", self.xt[t].t[:], self.xrows("xs", tok0, t), writes=[self.xt[t]])
                self.ln_to_hT(self.xt[t], 1, r, t)
            self.mixer_inputs(l, r, tok0, gw)
        self.release(mk)

    def phase_c1(self, l, skip_ctx):
        mk = self.mark()
        self.store_q = "act"
        self.alloc_ln(nw=3)
        self.pk2 = 0
        accm = self.sb("c1_acc", (128, 8, 512), F32)
        mT = self.sb("c1_mT", (128, 8, 512), BF16)
        yin2 = [self.sb("c1_y%d" % i, (128, 4, 512), BF16) for i in range(2)]
        pj2 = [self.sb("c1_pj%d" % i, (128, 4, D), BF16) for i in range(2)]
        wo = self.sb("c1_wo", (128, 8, D), BF16)
        sgt = [self.sb("c1_sg%d" % i, (128, 512), F32) for i in range(2)]
        tm = [self.sb("c1_tm%d" % i, (128, 512), F32) for i in range(2)]
        wb = self.wb[l]
        self.dma("sp", wo.t[:], wb["wout"].rearrange("(k p) c -> p k c", p=128), writes=[wo])
        kbox = [0]

        def group(tok0, gw, is_ctx):
            kk = kbox[0]
            nt = gw // 128
            r = 1 if is_ctx else 0
            for t in range(nt):
                self.dma("sp", self.xt[t].t[:], self.xrows("xs", tok0, t), writes=[self.xt[t]])
                self.ln_to_hT(self.xt[t], 1, r, t)
            for i in range(4):
                yin, pj = yin2[i % 2], pj2[i % 2]
                self.dma("sp", yin.t[:, :, 0:gw], self.s_y[i][:, tok0:tok0 + gw].rearrange("(c p) t -> p c t", p=128), writes=[yin])
                self.dma("sp", pj.t[:], wb["proj"][i].rearrange("(k p) c -> p k c", p=128), writes=[pj])
                for half in range(2):
                    wt = self.load_wcols(l, O_GATES + i * D + half * 512, 512)
                    for q in range(4):
                        fc = half * 4 + q
                        pg = self.cm_mm(wt, q * 128, gw)
                        pp = self._acc()
                        for kc in range(4):
                            self.mm(pp.t[:, 0:gw], pj.t[:, kc, fc * 128:(fc + 1) * 128], yin.t[:, kc, 0:gw], kc == 0, kc == 3, [pj, yin], [pp])
                        sg_, tm_ = sgt[kk % 2], tm[kk % 2]
                        kk += 1
                        self.op("act", lambda e, pg=pg, sg_=sg_: e.activation(out=sg_.t[:, 0:gw], in_=pg.t[:, 0:gw], func=AF.Sigmoid), [pg], [sg_])
                        if i == 0:
                            self.op("dve", lambda e, sg_=sg_, pp=pp, fc=fc: e.tensor_tensor(out=accm.t[:, fc, 0:gw], in0=sg_.t[:, 0:gw], in1=pp.t[:, 0:gw], op=ALU.mult), [sg_, pp], [accm])
                        else:
                            self.op("dve", lambda e, sg_=sg_, pp=pp, tm_=tm_: e.tensor_tensor(out=tm_.t[:, 0:gw], in0=sg_.t[:, 0:gw], in1=pp.t[:, 0:gw], op=ALU.mult), [sg_, pp], [tm_])
                            self.op("pool", lambda e, tm_=tm_, fc=fc: e.tensor_tensor(out=accm.t[:, fc, 0:gw], in0=accm.t[:, fc, 0:gw], in1=tm_.t[:, 0:gw], op=ALU.add), [tm_, accm], [accm])
            for fc in range(8):
                self.op("act", lambda e, fc=fc: e.copy(out=mT.t[:, fc, 0:gw], in_=accm.t[:, fc, 0:gw]), [accm], [mT])
            for t in range(nt):
                yb = [self.pbank[6], self.pbank[7]]
                for h in range(2):
                    for fc in range(8):
                        self.mm(yb[h].t[:], mT.t[:, fc, t * 128:(t + 1) * 128], wo.t[:, fc, h * 512:(h + 1) * 512], fc == 0, fc == 7, [mT, wo], [yb[h]])
                self.post_norm(self.xt[t], yb, self.gate_bc[r][1], 1, self.xt[t])
                self.dma("pool", self.xrows("xs", tok0, t), self.xt[t].t[:], reads=[self.xt[t]])
            kbox[0] = kk

        for (tok0, gw, is_ctx) in self.groups:
            if not (is_ctx and skip_ctx):
                group(tok0, gw, is_ctx)
        self.release(mk)

    def build(self):
        self.declare_io()
        self.alloc_common()
        for l in range(self.n_layers):
            last = (l == self.n_layers - 1)
            if l == 0:
                self.cast_weights(0)
                self.P.barrier()
                for l2 in range(1, self.n_layers):
                    self.cast_weights(l2)
            self.modulation(l)
            steps = [lambda: self.phase_ffn(l, 0, 0, "in" if l == 0 else "xs", "xs"),
                     lambda: self.phase_a2(l),
                     lambda: self.mixers_cla(l, not last),
                     lambda: self.mixer_gdn(l),
                     lambda: self.phase_c1(l, skip_ctx=last),
                     lambda: self.phase_ffn(l, 1, 2, "xs", "out" if last else "xs", skip_ctx=last)]
            for si, st in enumerate(steps):
                if si in self.dbg.get("skip", ()):
                    continue
                st()
        return self.finish()


def host_constants(TL):
    t = np.arange(TL)
    row = (t // 64).astype(np.float32)
    colp = (t % 64).astype(np.float32)
    inv = (np.float32(10000.0) ** (-np.arange(16, dtype=np.float32) / np.float32(16))).astype(np.float32)
    rc = np.zeros((128, TL), np.float32)
    rs = np.zeros((128, TL), np.float32)
    for p in range(128):
        d = p % 64
        j = d % 32
        pos = row if j < 16 else colp
        ang = (pos * inv[j % 16]).astype(np.float32)
        rc[p] = np.cos(ang)
        rs[p] = np.sin(ang) * (-1.0 if d < 32 else 1.0)
    qq = np.arange(128)[:, None]
    kk = np.arange(128)[None, :]
    NEG = np.float32(-30000.0)
    am = np.zeros((128, 3, 128), np.float32)
    am[:, 0, :] = np.where(kk >= qq, 0.0, NEG)
    am[:, 2, :] = np.where(kk <= qq, 0.0, NEG)
    p = np.arange(128)[:, None]
    f = np.arange(128)[None, :]
    d16 = (p // 16 == f // 16)
    offs = [((p // (2 * b) == f // (2 * b)) & (p // b != f // b)) for b in (16, 32, 64)]
    gmk = np.stack([(f < p), (p <= f), (f > p), (p >= f), d16] + offs, axis=1).astype(np.float32)
    return dict(rope_c=rc, rope_s=rs, att_mask=am.reshape(128, 384), gdn_mask=gmk.reshape(128, 1024))


_NC_CACHE = {}


def kernel(**inputs):
    x = np.asarray(inputs["x"], np.float32)
    B_, TL, _ = x.shape
    key = (TL,)
    if key not in _NC_CACHE:
        _NC_CACHE[key] = MK(TL, n_layers=2).build()
    nc = _NC_CACHE[key]
    consts = host_constants(TL)
    in_maps = []
    for b in range(B_):
        m = {"x": np.ascontiguousarray(x[b]), "ctx": np.ascontiguousarray(np.asarray(inputs["ctx"], np.float32)[b]),
             "c": np.ascontiguousarray(np.asarray(inputs["c"], np.float32)[b:b + 1]),
             "c_ctx": np.ascontiguousarray(np.asarray(inputs["c_ctx"], np.float32)[None, :])}
        for n, _s in WEIGHT_SPECS:
            m[n] = np.ascontiguousarray(np.asarray(inputs[n], np.float32))
        m.update(consts)
        in_maps.append(m)
    res = run_bass_kernel_spmd(nc, in_maps, core_ids=list(range(B_)))
    return np.stack([np.asarray(r["out"], np.float32) for r in res.results], axis=0)
```
